# Optimizing a Trainium2 kernel written in Bass

```python
import math
import jax, jax.numpy as jnp
from jax import lax
import numpy as np

D_MODEL = 1024
BATCH = 16
SEQ = 2048
DEPTH = 4
DEC_BATCH = 4
DEC_SEQ = 4096
PAST_LEN = 128

N_META = 16
D_MIX = D_MODEL
HEAD_DIM = 64
N_Q_HEADS = 8
N_KV_HEADS = 2
Q_PER_KV = N_Q_HEADS // N_KV_HEADS
D_ATTN = N_Q_HEADS * HEAD_DIM
D_KV = N_KV_HEADS * HEAD_DIM
WINDOW = 128
BLOCK = 128
D_FOURIER = D_MIX // 4
FOURIER_GROUP = 64
N_FOURIER_GROUPS = D_FOURIER // FOURIER_GROUP
D_CONV = D_MIX - D_ATTN - D_FOURIER
CONV_WIDTH = 31
D_IN = D_ATTN + 2 * D_KV + D_FOURIER + 2 * D_CONV
D_FF = ((8 * D_MODEL // 3 + 127) // 128) * 128
EPS = 1e-6
NEG = -1e30

kernel_name = "hymba_style_fnet_conformer_encoder"


def rms_norm(x, g):
    xf = x.astype(jnp.float32)
    y = xf * lax.rsqrt(jnp.mean(xf * xf, axis=-1, keepdims=True) + EPS)
    return (y * g.astype(jnp.float32)).astype(x.dtype)


def swiglu(x, w_gate, w_up, w_down):
    return (jax.nn.silu(x @ w_gate) * (x @ w_up)) @ w_down


def alibi_slopes():
    i = jnp.arange(1, N_Q_HEADS + 1, dtype=jnp.float32)
    return jnp.exp2(-8.0 * i / N_Q_HEADS)


def windowed_gqa(q, k, v, sink):
    B, L = q.shape[0], q.shape[1]
    lead = BLOCK - N_META
    nb = (L - N_META) // BLOCK + 1
    scale = 1.0 / math.sqrt(HEAD_DIM)
    qb = jnp.pad(q, ((0, 0), (lead, 0), (0, 0), (0, 0))).reshape(B, nb, BLOCK, N_KV_HEADS, Q_PER_KV, HEAD_DIM)

    def band(t):
        tp = jnp.pad(t, ((0, 0), (BLOCK + lead, BLOCK), (0, 0), (0, 0))).reshape(B, nb + 2, BLOCK, N_KV_HEADS, HEAD_DIM)
        return jnp.concatenate([tp[:, :-2], tp[:, 1:-1], tp[:, 2:]], axis=2)

    kb, vb = band(k), band(v)
    km, vm = k[:, :N_META], v[:, :N_META]

    qpos = (jnp.arange(nb * BLOCK) - lead).reshape(nb, BLOCK)
    kpos = jnp.arange(nb)[:, None] * BLOCK + jnp.arange(3 * BLOCK)[None, :] - BLOCK - lead
    dist = jnp.abs(qpos[:, :, None] - kpos[:, None, :])
    kvalid = (kpos >= N_META) & (kpos < L)
    valid = (dist <= WINDOW) & kvalid[:, None, :]
    slopes = alibi_slopes()
    bias = jnp.where(valid[:, None], -slopes[None, :, None, None] * dist[:, None].astype(jnp.float32), NEG)
    bias = bias.reshape(nb, N_KV_HEADS, Q_PER_KV, BLOCK, 3 * BLOCK)

    s_band = jnp.einsum('bnqhgd,bnkhd->bnhgqk', qb, kb).astype(jnp.float32) * scale + bias
    s_meta = jnp.einsum('bnqhgd,bmhd->bnhgqm', qb, km).astype(jnp.float32) * scale
    s_sink = jnp.broadcast_to(sink.astype(jnp.float32).reshape(1, 1, N_KV_HEADS, Q_PER_KV, 1, 1),
                              s_meta.shape[:-1] + (1,))
    p = jax.nn.softmax(jnp.concatenate([s_meta, s_band, s_sink], axis=-1), axis=-1)
    p_meta = p[..., :N_META].astype(v.dtype)
    p_band = p[..., N_META:N_META + 3 * BLOCK].astype(v.dtype)
    o = (jnp.einsum('bnhgqm,bmhd->bnqhgd', p_meta, vm)
         + jnp.einsum('bnhgqk,bnkhd->bnqhgd', p_band, vb))
    return o.reshape(B, nb * BLOCK, D_ATTN)[:, lead:]


def fourier_mix(u):
    B, L, _ = u.shape
    ug = u.reshape(B, L, N_FOURIER_GROUPS, FOURIER_GROUP).astype(jnp.float32)
    f = jnp.fft.fft2(ug, axes=(1, 3), norm="ortho").real
    return f.reshape(B, L, D_FOURIER).astype(u.dtype)


def conv_module(a, gate, w_dw, b_dw, ln_g, ln_b):
    u = a * jax.nn.sigmoid(gate)
    y = lax.conv_general_dilated(u, w_dw[:, None, :], window_strides=(1,),
                                 padding=[(CONV_WIDTH // 2, CONV_WIDTH // 2)],
                                 dimension_numbers=('NWC', 'WIO', 'NWC'),
                                 feature_group_count=D_CONV) + b_dw
    yf = y.astype(jnp.float32)
    mu = jnp.mean(yf, axis=-1, keepdims=True)
    var = jnp.mean(jnp.square(yf - mu), axis=-1, keepdims=True)
    yn = (yf - mu) * lax.rsqrt(var + EPS) * ln_g.astype(jnp.float32) + ln_b.astype(jnp.float32)
    return jax.nn.silu(yn).astype(a.dtype)


def hybrid_mixer(h, w_in, w_dw, b_dw, ln_g, ln_b, sink, g_branch, w_out):
    B, L, _ = h.shape
    z = h @ w_in
    c1 = D_ATTN
    c2 = c1 + D_KV
    c3 = c2 + D_KV
    c4 = c3 + D_FOURIER
    c5 = c4 + D_CONV
    q, k, v, uf, ua, ug = jnp.split(z, [c1, c2, c3, c4, c5], axis=-1)
    o_attn = windowed_gqa(q.reshape(B, L, N_Q_HEADS, HEAD_DIM),
                          k.reshape(B, L, N_KV_HEADS, HEAD_DIM),
                          v.reshape(B, L, N_KV_HEADS, HEAD_DIM), sink)
    o_four = fourier_mix(uf)
    o_conv = conv_module(ua, ug, w_dw, b_dw, ln_g, ln_b)
    o = jnp.concatenate([
        rms_norm(o_attn, g_branch[:D_ATTN]),
        rms_norm(o_four, g_branch[D_ATTN:D_ATTN + D_FOURIER]),
        rms_norm(o_conv, g_branch[D_ATTN + D_FOURIER:]),
    ], axis=-1)
    return o @ w_out


def encoder_trunk(x, meta_tokens, ffn1_norm, ffn1_w_gate, ffn1_w_up, ffn1_w_down,
                  mix_norm, w_in, conv_w_dw, conv_b_dw, conv_ln_g, conv_ln_b, attn_sink,
                  branch_norm, w_out, ffn2_norm, ffn2_w_gate, ffn2_w_up, ffn2_w_down, final_norm):
    B = x.shape[0]
    meta = jnp.broadcast_to(meta_tokens.astype(x.dtype)[None], (B, N_META, D_MODEL))
    h = jnp.concatenate([meta, x], axis=1)
    for l in range(DEPTH):
        h = h + 0.5 * swiglu(rms_norm(h, ffn1_norm[l]), ffn1_w_gate[l], ffn1_w_up[l], ffn1_w_down[l])
        h = h + hybrid_mixer(rms_norm(h, mix_norm[l]), w_in[l], conv_w_dw[l], conv_b_dw[l],
                             conv_ln_g[l], conv_ln_b[l], attn_sink[l], branch_norm[l], w_out[l])
        h = h + 0.5 * swiglu(rms_norm(h, ffn2_norm[l]), ffn2_w_gate[l], ffn2_w_up[l], ffn2_w_down[l])
    return rms_norm(h, final_norm)[:, N_META:]


def setup_inputs(seed: int = 0) -> dict:
    key = jax.random.key(seed)
    ks = jax.random.split(key, 24)
    f32 = jnp.float32

    def nrm(k, shape, scale):
        return jax.random.normal(k, shape, f32) * scale

    def gain(k, shape):
        return 1.0 + 0.02 * jax.random.normal(k, shape, f32)

    return {
        "x_prompt": nrm(ks[0], (BATCH, SEQ, D_MODEL), 1.0),
        "x_sample": nrm(ks[1], (DEC_BATCH, DEC_SEQ, D_MODEL), 1.0),
        "meta_tokens": nrm(ks[2], (N_META, D_MODEL), 1.0),
        "ffn1_norm": gain(ks[3], (DEPTH, D_MODEL)),
        "ffn1_w_gate": nrm(ks[4], (DEPTH, D_MODEL, D_FF), D_MODEL ** -0.5),
        "ffn1_w_up": nrm(ks[5], (DEPTH, D_MODEL, D_FF), D_MODEL ** -0.5),
        "ffn1_w_down": nrm(ks[6], (DEPTH, D_FF, D_MODEL), D_FF ** -0.5),
        "mix_norm": gain(ks[7], (DEPTH, D_MODEL)),
        "w_in": nrm(ks[8], (DEPTH, D_MODEL, D_IN), D_MODEL ** -0.5),
        "conv_w_dw": nrm(ks[9], (DEPTH, CONV_WIDTH, D_CONV), CONV_WIDTH ** -0.5),
        "conv_b_dw": nrm(ks[10], (DEPTH, D_CONV), 0.01),
        "conv_ln_g": gain(ks[11], (DEPTH, D_CONV)),
        "conv_ln_b": nrm(ks[12], (DEPTH, D_CONV), 0.01),
        "attn_sink": nrm(ks[13], (DEPTH, N_Q_HEADS), 0.5),
        "branch_norm": gain(ks[14], (DEPTH, D_MIX)),
        "w_out": nrm(ks[15], (DEPTH, D_MIX, D_MODEL), D_MIX ** -0.5),
        "ffn2_norm": gain(ks[16], (DEPTH, D_MODEL)),
        "ffn2_w_gate": nrm(ks[17], (DEPTH, D_MODEL, D_FF), D_MODEL ** -0.5),
        "ffn2_w_up": nrm(ks[18], (DEPTH, D_MODEL, D_FF), D_MODEL ** -0.5),
        "ffn2_w_down": nrm(ks[19], (DEPTH, D_FF, D_MODEL), D_FF ** -0.5),
        "final_norm": gain(ks[20], (D_MODEL,)),
    }


def reference(x_prompt, x_sample, meta_tokens, ffn1_norm, ffn1_w_gate, ffn1_w_up, ffn1_w_down,
              mix_norm, w_in, conv_w_dw, conv_b_dw, conv_ln_g, conv_ln_b, attn_sink,
              branch_norm, w_out, ffn2_norm, ffn2_w_gate, ffn2_w_up, ffn2_w_down, final_norm):
    y_prompt = encoder_trunk(x_prompt, meta_tokens, ffn1_norm, ffn1_w_gate, ffn1_w_up, ffn1_w_down,
                             mix_norm, w_in, conv_w_dw, conv_b_dw, conv_ln_g, conv_ln_b, attn_sink,
                             branch_norm, w_out, ffn2_norm, ffn2_w_gate, ffn2_w_up, ffn2_w_down, final_norm)
    y_sample = encoder_trunk(x_sample, meta_tokens, ffn1_norm, ffn1_w_gate, ffn1_w_up, ffn1_w_down,
                             mix_norm, w_in, conv_w_dw, conv_b_dw, conv_ln_g, conv_ln_b, attn_sink,
                             branch_norm, w_out, ffn2_norm, ffn2_w_gate, ffn2_w_up, ffn2_w_down, final_norm)
    return (y_prompt, y_sample)
```

```python
import math
from contextlib import ExitStack

import numpy as np
import ml_dtypes

import concourse.bass as bass
import concourse.mybir as mybir
from concourse.bass_utils import run_bass_kernel_spmd

F32 = mybir.dt.float32
BF16 = mybir.dt.bfloat16
AF = mybir.ActivationFunctionType
ALU = mybir.AluOpType
NPBF = ml_dtypes.bfloat16

D = 1024
NM = 16
EPS = 1e-6
NEG = -1e30
CW = 31
NCORES = 8


class Cfg:
    def __init__(s, L=4, HS=2048, DFF=2816):
        s.L, s.HS, s.DFF = L, HS, DFF
        s.NF = DFF // 128
        assert s.NF % 2 == 0 and HS % 512 == 0
        s.TR = 3 * HS
        s.T = s.TR + 3 * NM
        s.NTL = 6 * s.NF + 20
        s.NKC0, s.NNT0 = 2 * HS // 512 + 1, 2 * HS // 128 + 1
        s.NKC1, s.NNT1 = HS // 512 + 1, HS // 128 + 1
        L8 = L * 8
        s.G1, s.GM, s.G2, s.GF, s.GB = 0, L8, 2 * L8, 3 * L8, 3 * L8 + 8
        s.CBD = s.GB + L8
        s.CLG = s.CBD + 2 * L
        s.CLB = s.CLG + 2 * L
        s.CWD = s.CLB + 2 * L
        s.NGC = s.CWD + 2 * L * CW
        s.CB = 2 * (HS + 46)
        s.NSLOT = max(s.NF, 22)


def _cols(v):
    return np.ascontiguousarray(np.asarray(v, np.float32).reshape(-1, 128).T)


def _dft_table(cfg, pos, seq, Ls, nkc, nnt):
    n = len(pos)
    pp = np.zeros(nnt * 128, np.int64); ss = np.full(nnt * 128, -1, np.int64)
    pp[:n] = pos; ss[:n] = seq
    pk = np.zeros(nkc * 512, np.int64); sk = np.full(nkc * 512, -2, np.int64)
    pk[:n] = pos; sk[:n] = seq
    out = np.empty((nkc, nnt, 128, 2, 512), NPBF)
    for kc in range(nkc):
        k_p = pk[kc * 512:(kc + 1) * 512]; k_s = sk[kc * 512:(kc + 1) * 512]
        r = (pp[:, None] * k_p[None, :]) % Ls
        ang = (2.0 * np.pi / Ls) * r.astype(np.float64)
        same = ((ss[:, None] == k_s[None, :]) & (ss[:, None] >= 0)).astype(np.float64) / math.sqrt(Ls)
        c = (np.cos(ang) * same).reshape(nnt, 128, 512)
        s_ = (np.sin(ang) * same).reshape(nnt, 128, 512)
        out[kc, :, :, 0, :] = c.astype(NPBF)
        out[kc, :, :, 1, :] = s_.astype(NPBF)
    return out


def host_prep(cfg, inp):
    L, HS, NF = cfg.L, cfg.HS, cfg.NF
    f32 = np.float32
    xp = np.asarray(inp["x_prompt"], f32); xs = np.asarray(inp["x_sample"], f32)
    meta = np.asarray(inp["meta_tokens"], f32)
    wt = np.empty((L * cfg.NTL, 128, 1024), f32)
    qperm = np.concatenate([np.r_[64 * j:64 * j + 64, 256 + 64 * j:256 + 64 * j + 64] for j in range(4)])
    for l in range(L):
        base = l * cfg.NTL
        for wi, (g, u, d) in enumerate((("ffn1_w_gate", "ffn1_w_up", "ffn1_w_down"),
                                        ("ffn2_w_gate", "ffn2_w_up", "ffn2_w_down"))):
            o = base + wi * 3 * NF
            wg = np.asarray(inp[g][l], f32).reshape(8, 128, NF, 128).transpose(2, 1, 0, 3)
            wu = np.asarray(inp[u][l], f32).reshape(8, 128, NF, 128).transpose(2, 1, 0, 3)
            wt[o:o + 2 * NF:2] = wg.reshape(NF, 128, 1024)
            wt[o + 1:o + 2 * NF:2] = wu.reshape(NF, 128, 1024)
            wd = np.asarray(inp[d][l], f32).reshape(NF // 2, 2, 128, 2, 4, 128).transpose(3, 0, 2, 1, 4, 5)
            wt[o + 2 * NF:o + 3 * NF] = wd.reshape(NF, 128, 1024)
        win = np.asarray(inp["w_in"][l], f32)
        cols = np.concatenate([qperm, np.arange(512, 1024),
                               np.r_[1024:1152], np.r_[1280:1408],
                               np.r_[1152:1280], np.r_[1408:1536]])
        wp = win[:, cols].reshape(8, 128, 12, 128).transpose(2, 1, 0, 3)
        wt[base + 6 * NF:base + 6 * NF + 12] = wp.reshape(12, 128, 1024)
        wo = np.asarray(inp["w_out"][l], f32)
        rows = np.concatenate([qperm, np.arange(512, 1024)])
        wop = wo[rows].reshape(8, 128, 8, 128).transpose(2, 1, 0, 3)
        wt[base + 6 * NF + 12:base + 6 * NF + 20] = wop.reshape(8, 128, 1024)
    gn = np.zeros((128, cfg.NGC), f32)
    for l in range(L):
        gn[:, cfg.G1 + 8 * l:cfg.G1 + 8 * l + 8] = _cols(inp["ffn1_norm"][l])
        gn[:, cfg.GM + 8 * l:cfg.GM + 8 * l + 8] = _cols(inp["mix_norm"][l])
        gn[:, cfg.G2 + 8 * l:cfg.G2 + 8 * l + 8] = _cols(inp["ffn2_norm"][l])
        br = np.asarray(inp["branch_norm"][l], f32)
        gn[:, cfg.GB + 8 * l:cfg.GB + 8 * l + 8] = _cols(br[rows])
        gn[:, cfg.CBD + 2 * l:cfg.CBD + 2 * l + 2] = _cols(inp["conv_b_dw"][l])
        gn[:, cfg.CLG + 2 * l:cfg.CLG + 2 * l + 2] = _cols(inp["conv_ln_g"][l])
        gn[:, cfg.CLB + 2 * l:cfg.CLB + 2 * l + 2] = _cols(inp["conv_ln_b"][l])
        w = np.asarray(inp["conv_w_dw"][l], f32)
        for cc in range(2):
            o = cfg.CWD + (2 * l + cc) * CW
            gn[:, o:o + CW] = w[:, cc * 128:(cc + 1) * 128].T
    gn[:, cfg.GF:cfg.GF + 8] = _cols(inp["final_norm"])
    sink = np.asarray(inp["attn_sink"], f32)
    sinkx = np.ascontiguousarray(np.repeat(sink.reshape(L, 2, 4, 1), 128, axis=3)).reshape(1, L * 1024)
    slopes = 2.0 ** (-(np.arange(8) + 1.0))
    ii = np.arange(128)[None, :]; jj = np.arange(128)[:, None]
    btab = np.zeros((128, 6, 4, 128), f32)
    for gi in range(2):
        for g in range(4):
            s = slopes[4 * gi + g]
            dp = 128 + ii - jj
            btab[:, gi * 3 + 0, g] = np.where(jj >= ii, -s * dp, NEG)
            btab[:, gi * 3 + 1, g] = -s * np.abs(ii - jj)
            dn = 128 + jj - ii
            btab[:, gi * 3 + 2, g] = np.where(jj <= ii, -s * dn, NEG)
    btab = btab.reshape(128, 6 * 512).astype(NPBF)
    mqb = np.zeros((128, 2, 4, 16), f32)
    tt = np.arange(128)[:, None]; mm = np.arange(16)[None, :]
    for gi in range(2):
        for g in range(4):
            dist = 16 + tt - mm
            mqb[:, gi, g] = np.where(dist <= 128, -slopes[4 * gi + g] * dist, NEG)
    mqb = mqb.reshape(128, 128)
    mmb = np.full((32, 2, 64), NEG, f32)
    mmb[0:16, 0] = 0.0; mmb[16:32, 1] = 0.0
    mmb = mmb.reshape(32, 128)
    c64 = np.zeros((128, 2, 2, 128), np.float64)
    a = np.arange(128)
    ang = 2 * np.pi * ((a[:, None] % 64) * (a[None, :] % 64) % 64) / 64.0
    blk = ((a[:, None] // 64) == (a[None, :] // 64)) / 8.0
    for cc in range(2):
        c64[:, cc, 0] = np.cos(ang) * blk
        c64[:, cc, 1] = -np.sin(ang) * blk
    c64 = c64.reshape(128, 512).astype(NPBF)
    identf = np.eye(128, dtype=f32)

    def dft_for(kind):
        if kind == "P":
            pos = np.concatenate([16 + np.arange(HS), 16 + np.arange(HS), np.arange(16), np.arange(16)])
            seq = np.concatenate([np.zeros(HS), np.ones(HS), np.zeros(16), np.ones(16)]).astype(np.int64)
            return _dft_table(cfg, pos, seq, HS + 16, cfg.NKC0, cfg.NNT0)
        pos = np.concatenate([16 + np.arange(2 * HS), np.arange(16), np.zeros(16, np.int64)])
        seq = np.concatenate([np.zeros(2 * HS + 16), -np.ones(16)]).astype(np.int64)
        return _dft_table(cfg, pos, seq, 2 * HS + 16, cfg.NKC0, cfg.NNT0)

    dftP, dftS = dft_for("P"), dft_for("S")
    pos1 = np.concatenate([16 + np.arange(HS), np.arange(16)])
    dft1 = _dft_table(cfg, pos1, np.zeros(HS + 16, np.int64), HS + 16, cfg.NKC1, cfg.NNT1)

    def mtab_for(kind):
        m = np.full((32, 2, 512), NEG, f32)
        m[0:16, 0] = 0.0
        if kind == "P":
            m[16:32, 1] = 0.0
        else:
            m[0:16, 1] = 0.0
        return m.reshape(32, 1024)

    in_maps = []
    for c in range(NCORES):
        kind = "P" if c < 4 else "S"
        xin = np.empty((cfg.T, D), f32)
        if kind == "P":
            for i in range(3):
                xin[i * HS:(i + 1) * HS] = xp[3 * c + i]
            mrows = [meta, meta, meta]
        else:
            s_ = c - 4
            xin[0:2 * HS] = xs[s_]
            xin[2 * HS:3 * HS] = xp[12 + s_]
            mrows = [meta, np.zeros_like(meta), meta]
        for i in range(3):
            xin[cfg.TR + 16 * i:cfg.TR + 16 * i + 16] = mrows[i]
        flags = np.zeros((128, 4), f32)
        flags[:, 0] = NEG if kind == "P" else 0.0
        flags[:, 1] = 0.0 if kind == "P" else 1.0
        flags[:, 2] = 1.0 if kind == "P" else 0.0
        in_maps.append(dict(xin=xin, wt=wt, gn=gn, sinkx=sinkx, btab=btab, mtab=mtab_for(kind), mqb=mqb,
                            mmb=mmb, c64=c64, identf=identf, flags=flags,
                            dft0=dftP if kind == "P" else dftS, dft1=dft1))
    return in_maps


class Buf:
    __slots__ = ("name", "lw", "rd")

    def __init__(s, name):
        s.name, s.lw, s.rd = name, None, {}


class Op:
    __slots__ = ("eng", "fn", "deps", "dmaq", "dmaval", "need_inc", "incval")


class Prog:
    ENGS = ("pe", "act", "dve", "pool", "sp")

    def __init__(s):
        s.ops = {e: [] for e in s.ENGS}
        s.ndma = {}
        s.dmaeng = {}

    def add(s, eng, fn, R=(), W=(), dma=None):
        op = Op()
        op.eng, op.fn, op.need_inc, op.incval = eng, fn, False, 0
        op.dmaq = dma
        if dma:
            s.ndma[dma] = s.ndma.get(dma, 0) + 1
            assert s.dmaeng.setdefault(dma, eng) == eng
            op.dmaval = 16 * s.ndma[dma]
        deps = {}

        def need(p):
            if p is None:
                return
            if p.dmaq is None and p.eng == "pe" and eng == "pe" and not dma:
                return
            key = ("d", p.dmaq) if p.dmaq else ("e", p.eng)
            cur = deps.get(key)
            if cur is None or s._later(p, cur):
                deps[key] = p

        for b in R:
            need(b.lw)
        for b in W:
            need(b.lw)
            for r in b.rd.values():
                need(r)
        op.deps = list(deps.values())
        for p in op.deps:
            if p.dmaq is None:
                p.need_inc = True
        op_key = ("d", dma) if dma else ("e", eng)
        for b in R:
            b.rd[op_key] = op
        for b in W:
            b.lw = op; b.rd = {}
        op.incval = len(s.ops[eng])
        s.ops[eng].append(op)
        return op

    @staticmethod
    def _later(a, b):
        if a.dmaq:
            return a.dmaval > b.dmaval
        return a.incval > b.incval

    def emit(s, nc, es):
        esem = {e: es.enter_context(nc.semaphore("sem_" + e)) for e in s.ENGS}
        dsem = {q: es.enter_context(nc.semaphore("dsem_" + q)) for q in s.ndma}
        for e in s.ENGS:
            c = 0
            for op in s.ops[e]:
                if op.dmaq is None and op.need_inc:
                    c += 1; op.incval = c
                else:
                    op.incval = -1
            assert c < 60000, (e, c)
        for q in s.ndma:
            assert 16 * s.ndma[q] < 60000, (q, s.ndma[q])
        block = es.enter_context(nc.Block())

        def body(ename):
            def run(eng):
                waited = {}
                for op in s.ops[ename]:
                    for p in op.deps:
                        if p.dmaq:
                            key = ("d", p.dmaq); sem = dsem[p.dmaq]; val = p.dmaval
                        else:
                            key = ("e", p.eng); sem = esem[p.eng]; val = p.incval
                        if waited.get(key, 0) < val:
                            eng.wait_ge(sem, val); waited[key] = val
                    ins = op.fn(eng)
                    if op.dmaq:
                        ins.then_inc(dsem[op.dmaq], 16)
                    elif op.need_inc:
                        ins.then_inc(esem[ename], 1)
                for q in s.ndma:
                    if s.dmaeng[q] == ename:
                        eng.wait_ge(dsem[q], 16 * s.ndma[q])
            return run

        block.sync(body("sp")); block.gpsimd(body("pool")); block.vector(body("dve"))
        block.tensor(body("pe")); block.scalar(body("act"))


def build_program(cfg):
    L, HS, NF = cfg.L, cfg.HS, cfg.NF
    nc = bass.Bass("TRN2", target_bir_lowering=False)
    P = Prog()

    def DT(name, shape, dt, kind):
        return nc.dram_tensor(name, list(shape), dt, kind=kind).ap()

    xin = DT("xin", [cfg.T, D], F32, "ExternalInput")
    wt = DT("wt", [L * cfg.NTL, 128, 1024], F32, "ExternalInput")
    gn_d = DT("gn", [128, cfg.NGC], F32, "ExternalInput")
    sinkx_d = DT("sinkx", [1, L * 1024], F32, "ExternalInput")
    btab_d = DT("btab", [128, 6 * 512], BF16, "ExternalInput")
    mtab_d = DT("mtab", [32, 1024], F32, "ExternalInput")
    mqb_d = DT("mqb", [128, 128], F32, "ExternalInput")
    mmb_d = DT("mmb", [32, 128], F32, "ExternalInput")
    c64_d = DT("c64", [128, 512], BF16, "ExternalInput")
    identf_d = DT("identf", [128, 128], F32, "ExternalInput")
    flags_d = DT("flags", [128, 4], F32, "ExternalInput")
    dft_d = [DT("dft0", [cfg.NKC0, cfg.NNT0, 128, 2, 512], BF16, "ExternalInput"),
             DT("dft1", [cfg.NKC1, cfg.NNT1, 128, 2, 512], BF16, "ExternalInput")]
    yout = DT("yout", [cfg.TR, D], F32, "ExternalOutput")
    wtb = DT("wtb", [L * cfg.NTL, 128, 1024], BF16, "Internal")
    hT = DT("hT", [128, 8, cfg.T], F32, "Internal")

    es = ExitStack()

    def SB(name, cols, dt, parts=128):
        return es.enter_context(nc.sbuf_tensor("sb_" + name, [parts, cols], dt))

    identf = SB("identf", 128, F32); identb = SB("identb", 128, BF16); onesb = SB("onesb", 128, BF16)
    gn = SB("gn", cfg.NGC, F32); flags = SB("flags", 4, F32); epsb = SB("epsb", 1, F32)
    btab = SB("btab", 6 * 512, BF16); mtab = SB("mtab", 1024, F32); mqb = SB("mqb", 128, F32); mmb = SB("mmb", 128, F32)
    c64 = SB("c64", 512, BF16)
    sinkst = SB("sinkst", 1024, F32); esrow = SB("esrow", 1024, BF16)
    dg = SB("dg", 2 * CW * 128, BF16)
    NT0 = cfg.NNT0
    TG0 = 2 * HS + 32
    kT = SB("kT", TG0, BF16); vt = SB("vt", NT0 * 128, BF16); uft = SB("uft", NT0 * 256, BF16)
    cbuf = SB("cbuf", 2 * cfg.CB, BF16)
    hbt = [SB("hb%d" % i, 8 * 512, F32) for i in range(2)]
    hnt = SB("hn", 8 * 512, BF16)
    actt = SB("act", cfg.NSLOT * 512, BF16)
    ont = SB("on", 8 * 512, BF16)
    sgt = SB("sg", 2 * 512, BF16)
    lnout = SB("lnout", 512, F32)
    rstt = [SB("rst%d" % i, 512, F32) for i in range(3)]
    sbt = [SB("sbt%d" % i, 512, F32) for i in range(2)]
    otmps = [SB("otmp0", 512, F32), SB("otmp1", 512, F32)]; rden = SB("rden", 512, F32)
    yct = SB("yct", 2 * 512, F32); ost = SB("ost", 2 * 1024, F32)
    ringt = SB("ring", 8 * 1024, BF16)
    pst = [es.enter_context(nc.psum_tensor("ps%d" % i, [128, 512], F32)) for i in range(8)]

    B = Buf
    b_const = B("const"); b_dg = [B("dg0"), B("dg1")]; b_esrow = B("esrow"); b_sinkst = B("sinkst")
    b_kT, b_vt, b_uft, b_cbuf = B("kT"), B("vt"), B("uft"), B("cbuf")
    b_hb = [[B("hb%d_%d" % (i, k)) for k in range(8)] for i in range(2)]
    b_hn = [B("hn%d" % k) for k in range(8)]
    b_act = [B("act%d" % k) for k in range(cfg.NSLOT)]
    b_on = [B("on%d" % k) for k in range(8)]
    b_sg = [B("sg0"), B("sg1")]; b_lnout = B("lnout"); b_rst = [B("rst%d" % i) for i in range(3)]
    b_sbt = [B("sbt0"), B("sbt1")]; b_otmps = [B("otmp0"), B("otmp1")]; b_rden = B("rden"); b_yc = B("yc")
    b_ost = [B("ost0"), B("ost1")]
    b_ring = [B("ring%d" % i) for i in range(8)]
    b_ps = [B("ps%d" % i) for i in range(8)]
    b_hTt = [B("hT%d" % i) for i in range((cfg.T + 127) // 128)]; b_wtb = [B("wtb%d" % l) for l in range(L)]

    hbv = [t[:, :].rearrange("p (c t) -> p c t", c=8) for t in hbt]
    hnv = hnt[:, :].rearrange("p (c t) -> p c t", c=8)
    actv = actt[:, :].rearrange("p (c t) -> p c t", t=512)
    onv = ont[:, :].rearrange("p (c t) -> p c t", c=8)
    vtv = vt[:, :].rearrange("p (n c) -> p n c", c=128)
    uftv = uft[:, :].rearrange("p (n c) -> p n c", c=256)
    cbv = cbuf[:, :].rearrange("p (c t) -> p c t", c=2)
    dgv = dg[:, :].rearrange("p (c j m) -> p c j m", c=2, j=CW)
    btv = btab[:, :].rearrange("p (k t) -> p k t", k=6)
    c64v = c64[:, :].rearrange("p (c s m) -> p c s m", c=2, s=2)
    ringv = [ringt[:, i * 1024:(i + 1) * 1024] for i in range(8)]

    def mm(out, lhsT, rhs, start, stop, R, W):
        P.add("pe", lambda e: e.matmul(out, lhsT=lhsT, rhs=rhs, start=start, stop=stop), R, W)

    def act(out, in_, func, R, W, scale=1.0, bias=None):
        if bias is None:
            P.add("act", lambda e: e.activation(out=out, in_=in_, func=func, scale=scale), R, W)
        else:
            P.add("act", lambda e: e.activation(out=out, in_=in_, func=func, scale=scale, bias=bias), R, W)

    def tt(eng, out, in0, in1, op, R, W):
        P.add(eng, lambda e: e.tensor_tensor(out=out, in0=in0, in1=in1, op=op), R, W)

    def ts(eng, out, in0, s1, s2, op0, op1, R, W):
        if s2 is None:
            P.add(eng, lambda e: e.tensor_scalar(out=out, in0=in0, scalar1=s1, scalar2=None, op0=op0), R, W)
        else:
            P.add(eng, lambda e: e.tensor_scalar(out=out, in0=in0, scalar1=s1, scalar2=s2, op0=op0, op1=op1), R, W)

    def stt(eng, out, in0, scalar, in1, op0, op1, R, W):
        P.add(eng, lambda e: e.scalar_tensor_tensor(out=out, in0=in0, scalar=scalar, in1=in1, op0=op0, op1=op1), R, W)

    def cp(eng, out, in_, R, W):
        if eng == "act":
            P.add("act", lambda e: e.activation(out=out, in_=in_, func=AF.Copy), R, W)
        else:
            P.add(eng, lambda e: e.tensor_copy(out=out, in_=in_), R, W)

    def dma(q, key, out, in_, R, W):
        P.add(q, lambda e: e.dma_start(out=out, in_=in_), R, W, dma=key)

    st = {"ps": 0, "ring": 0, "sg": 0, "sb": 0, "pt": 0, "hb": 0}

    def PS():
        i = st["ps"]; st["ps"] = (i + 1) % 8
        return pst[i], b_ps[i]

    def ring_load(src, view, l):
        i = st["ring"]; st["ring"] = (i + 1) % 8
        dma("sp", "ring%d" % i, view(ringv[i]), src, [b_wtb[l]] if l is not None else [], [b_ring[i]])
        return ringv[i], b_ring[i]

    def wtile(l, idx):
        flat, b = ring_load(wtb[l * cfg.NTL + idx], lambda s: s, l)
        return flat.rearrange("p (c m) -> p c m", c=8), b

    for (dst, src) in ((identf, identf_d), (gn, gn_d), (flags, flags_d), (btab, btab_d), (mqb, mqb_d), (c64, c64_d)):
        dma("sp", "const", dst[:, :], src[:, :], [], [b_const])
    dma("sp", "const", mtab[0:32, :], mtab_d[:, :], [], [b_const])
    dma("sp", "const", mmb[0:32, :], mmb_d[:, :], [], [b_const])
    cp("dve", identb[:, :], identf[:, :], [b_const], [b_const])
    P.add("dve", lambda e: e.memset(onesb[:, :], 1.0), [], [b_const])
    P.add("dve", lambda e: e.memset(epsb[:, :], EPS), [], [b_const])
    P.add("dve", lambda e: e.memset(cbuf[:, :], 0.0), [], [b_cbuf])
    P.add("dve", lambda e: e.memset(actt[:, :], 0.0), [], b_act)
    P.add("pool", lambda e: e.memset(kT[:, :], 0.0), [], [b_kT])
    P.add("pool", lambda e: e.memset(vt[:, :], 0.0), [], [b_vt])
    P.add("pool", lambda e: e.memset(uft[:, :], 0.0), [], [b_uft])

    CH = 8
    cast_todo = {l: [(l * cfg.NTL + t0, l * cfg.NTL + min(cfg.NTL, t0 + CH)) for t0 in range(0, cfg.NTL, CH)] for l in range(L)}

    def cast_some(l, k):
        for _ in range(k):
            if l < L and cast_todo[l]:
                a, b = cast_todo[l].pop(0)
                dma("pool", "cast%d" % l, wtb[a:b].rearrange("a p c -> (a p) c"), wt[a:b].rearrange("a p c -> (a p) c"),
                    [], [b_wtb[l]])

    ntile_in = (cfg.T + 127) // 128
    for ti in range(ntile_in):
        r0 = ti * 128; nr = min(128, cfg.T - r0)
        xi = ti % 2
        xst = hbt[0][:, xi * 1024:(xi + 1) * 1024]; bx = b_hb[0][xi]
        hst = hbt[1][:, xi * 1024:(xi + 1) * 1024].rearrange("p (c t) -> p c t", c=8); bh = b_hb[1][xi]
        dma("sp", "xl%d" % xi, xst[0:nr, :], xin[r0:r0 + nr, :], [], [bx])
        for hf in range(2):
            pt_, bp = PS()
            for j in range(4):
                kc = hf * 4 + j
                P.add("pe", (lambda o, i_, idn: (lambda e: e.transpose(o, i_, idn)))(
                    pt_[:, j * 128:j * 128 + nr], xst[0:nr, kc * 128:(kc + 1) * 128], identf[0:nr, 0:nr]),
                    [bx, b_const], [bp])
            cp("act" if hf == 0 else "dve", hst[:, hf * 4:hf * 4 + 4, 0:nr],
               pt_[:, :].rearrange("p (c t) -> p c t", c=4)[:, :, 0:nr], [bp], [bh])
        dma("pool", "xs%d" % xi, hT[:, :, r0:r0 + nr], hst[:, :, 0:nr], [bh], [b_hTt[ti]])
        if ti % 2 == 1:
            cast_some(0, 1)
    cast_some(0, 1000)

    def stats(srcs, n, count, rst_i, s0=0):
        stats_pre(srcs, n, s0)
        stats_post(len(srcs), n, count, rst_i, s0)

    def stats_post(nsrc, n, count, rst_i, s0=0):
        pt_, bp = PS()
        for k in range(nsrc):
            mm(pt_[:, :n], onesb[:, :], actv[:, s0 + k, :n], k == 0, k == nsrc - 1, [b_const, b_act[s0 + k]], [bp])
        act(lnout[:, :n], pt_[:, :n], AF.Ln, [bp, b_const], [b_lnout], scale=1.0 / count, bias=epsb[:, 0:1])
        act(rstt[rst_i][:, :n], lnout[:, :n], AF.Exp, [b_lnout], [b_rst[rst_i]], scale=-0.5)

    def stats_pre(srcs, n, s0=0):
        for k, (ap, b, inps) in enumerate(srcs):
            if inps or k % 3 == 1:
                act(actv[:, s0 + k, :n], ap, AF.Square, [b], [b_act[s0 + k]])
            else:
                tt("pool" if k % 3 == 0 else "dve", actv[:, s0 + k, :n], ap, ap, ALU.mult, [b], [b_act[s0 + k]])

    def norm_h(hi, n, gcol, rst_i=0):
        stats([(hbv[hi][:, kc, :n], b_hb[hi][kc], False) for kc in range(8)], n, float(D), rst_i)
        for kc in range(8):
            stt("dve", hnv[:, kc, :n], hbv[hi][:, kc, :n], gn[:, gcol + kc:gcol + kc + 1], rstt[rst_i][:, :n],
                ALU.mult, ALU.mult, [b_hb[hi][kc], b_rst[rst_i], b_const], [b_hn[kc]])

    def ffn(l, which, hi, n):
        base = which * 3 * NF
        norm_h(hi, n, (cfg.G1 if which == 0 else cfg.G2) + 8 * l)
        for f in range(NF):
            tg, bg = wtile(l, base + 2 * f)
            tu, bu = wtile(l, base + 2 * f + 1)
            pg, bpg = PS()
            for kc in range(8):
                mm(pg[:, :n], tg[:, kc, :], hnv[:, kc, :n], kc == 0, kc == 7, [bg, b_hn[kc]], [bpg])
            pu, bpu = PS()
            for kc in range(8):
                mm(pu[:, :n], tu[:, kc, :], hnv[:, kc, :n], kc == 0, kc == 7, [bu, b_hn[kc]], [bpu])
            si = st["sg"]; st["sg"] = 1 - si
            sgv = sgt[:, si * 512:si * 512 + n]
            act(sgv, pg[:, :n], AF.Silu, [bpg], [b_sg[si]])
            tt("dve", actv[:, f, :n], pu[:, :n], sgv, ALU.mult, [bpu, b_sg[si]], [b_act[f]])
        for half in range(2):
            acc = [PS() for _ in range(4)]
            for pair in range(NF // 2):
                td, bd = wtile(l, base + 2 * NF + half * (NF // 2) + pair)
                for m in range(2):
                    f = 2 * pair + m
                    for dcl in range(4):
                        mm(acc[dcl][0][:, :n], td[:, m * 4 + dcl, :], actv[:, f, :n], f == 0, f == NF - 1,
                           [bd, b_act[f]], [acc[dcl][1]])
            for dcl in range(4):
                dc = half * 4 + dcl
                stt("dve", hbv[hi][:, dc, :n], acc[dcl][0][:, :n], 0.5, hbv[hi][:, dc, :n], ALU.mult, ALU.add,
                    [acc[dcl][1], b_hb[hi][dc]], [b_hb[hi][dc]])

    groups = [dict(gid=0, halves=[0, 1], nreal=2 * HS, mslot=cfg.TR, nmeta=32),
              dict(gid=1, halves=[2], nreal=HS, mslot=cfg.TR + 32, nmeta=16)]
    for g in groups:
        ch = []
        for hi_, hh in enumerate(g["halves"]):
            for c in range(HS // 512):
                ch.append(dict(kind="real", hloc=hi_, g0=hi_ * HS + c * 512, n=512, s0=hh * HS + c * 512, coff=c * 512))
        ch.append(dict(kind="meta", g0=g["nreal"], n=g["nmeta"], s0=g["mslot"]))
        g["chunks"] = ch
        g["nh"] = len(g["halves"])

    def cbase(hloc):
        return hloc * (HS + 46)

    def hT_bufs(c):
        return b_hTt[c["s0"] // 128:(c["s0"] + c["n"] - 1) // 128 + 1]

    def load_h(c):
        hi = st["hb"]; st["hb"] = 1 - hi
        n = c["n"]
        dma("sp", "hb%d" % hi, hbv[hi][:, :, :n], hT[:, :, c["s0"]:c["s0"] + n], hT_bufs(c), b_hb[hi])
        return hi

    def store_h(c, hi):
        n = c["n"]
        dma("pool", "st%d" % hi, hT[:, :, c["s0"]:c["s0"] + n], hbv[hi][:, :, :n], b_hb[hi], hT_bufs(c))

    def pass_a(l, g):
        if g["nh"] == 1:
            P.add("pool", lambda e: e.memset(cbv[:, :, 31 + HS:46 + HS], 0.0), [], [b_cbuf])
        for c in g["chunks"]:
            n = c["n"]; g0 = c["g0"]
            cast_some(l + 1, 1)
            hi = load_h(c)
            ffn(l, 0, hi, n)
            if g["gid"] == 0 and c is g["chunks"][0]:
                layer_consts_dg(l)
            store_h(c, hi)
            norm_h(hi, n, cfg.GM + 8 * l)
            wb = 6 * NF
            tk, bk = wtile(l, wb + 4)
            pk, bpk = PS()
            for kc in range(8):
                mm(pk[:, :n], tk[:, kc, :], hnv[:, kc, :n], kc == 0, kc == 7, [bk, b_hn[kc]], [bpk])
            cp("act", kT[:, g0:g0 + n], pk[:, :n], [bpk], [b_kT])
            tv, bv = wtile(l, wb + 5)
            pv, bpv = PS()
            ntt = (n + 127) // 128
            for t_ in range(ntt):
                nr = min(128, n - t_ * 128)
                for kc in range(8):
                    mm(pv[0:nr, t_ * 128:(t_ + 1) * 128], hnv[:, kc, t_ * 128:t_ * 128 + nr], tv[:, kc, :],
                       kc == 0, kc == 7, [bv, b_hn[kc]], [bpv])
            nr = min(128, n)
            cp("dve", vtv[0:nr, g0 // 128:g0 // 128 + ntt, :],
               pv[0:nr, 0:ntt * 128].rearrange("p (a c) -> p a c", c=128), [bpv], [b_vt])
            tf0, bf0 = wtile(l, wb + 6)
            tf1, bf1 = wtile(l, wb + 7)
            pf = [PS() for _ in range((ntt + 1) // 2)]
            for c2, (tf_, bf_) in enumerate(((tf0, bf0), (tf1, bf1))):
                for t_ in range(ntt):
                    nr = min(128, n - t_ * 128)
                    pp, bpp = pf[t_ // 2]
                    o = (t_ % 2) * 256 + c2 * 128
                    for kc in range(8):
                        mm(pp[0:nr, o:o + 128], hnv[:, kc, t_ * 128:t_ * 128 + nr], tf_[:, kc, :],
                           kc == 0, kc == 7, [bf_, b_hn[kc]], [bpp])
            for q_, (pp, bpp) in enumerate(pf):
                na = min(2, ntt - 2 * q_)
                nr = min(128, n)
                cp("act" if q_ == 0 else "dve", uftv[0:nr, g0 // 128 + 2 * q_:g0 // 128 + 2 * q_ + na, :],
                   pp[0:nr, 0:na * 256].rearrange("p (a c) -> p a c", c=256), [bpp], [b_uft])
            for cc in range(2):
                ta, ba = wtile(l, wb + 8 + 2 * cc)
                tg_, bg_ = wtile(l, wb + 9 + 2 * cc)
                pa, bpa = PS()
                for kc in range(8):
                    mm(pa[:, :n], ta[:, kc, :], hnv[:, kc, :n], kc == 0, kc == 7, [ba, b_hn[kc]], [bpa])
                pg, bpg = PS()
                for kc in range(8):
                    mm(pg[:, :n], tg_[:, kc, :], hnv[:, kc, :n], kc == 0, kc == 7, [bg_, b_hn[kc]], [bpg])
                si = st["sb"]; st["sb"] = 1 - si
                sv = sbt[si][:, :n]
                act(sv, pg[:, :n], AF.Tanh, [bpg], [b_sbt[si]], scale=0.5)
                ts("dve", sv, sv, 0.5, 0.5, ALU.mult, ALU.add, [b_sbt[si]], [b_sbt[si]])
                if c["kind"] == "real":
                    p0 = cbase(c["hloc"]) + 31 + c["coff"]
                    tt("dve", cbv[:, cc, p0:p0 + n], pa[:, :n], sv, ALU.mult, [bpa, b_sbt[si]], [b_cbuf])
                else:
                    for hl in range(g["nh"]):
                        p0 = cbase(hl) + 15
                        tt("dve", cbv[:, cc, p0:p0 + 16], pa[:, hl * 16:hl * 16 + 16], sbt[si][:, hl * 16:hl * 16 + 16],
                           ALU.mult, [bpa, b_sbt[si]], [b_cbuf])
        if g["nh"] == 2:
            for cc in range(2):
                b0, b1 = cbase(0), cbase(1)
                ts("pool", cbv[:, cc, b0 + 31 + HS:b0 + 31 + HS + 15], cbv[:, cc, b1 + 31:b1 + 46], flags[:, 1:2], None,
                   ALU.mult, None, [b_cbuf, b_const], [b_cbuf])
                ts("pool", cbv[:, cc, b1:b1 + 31], cbv[:, cc, b1:b1 + 31], flags[:, 2:3], None,
                   ALU.mult, None, [b_cbuf, b_const], [b_cbuf])
                stt("dve", cbv[:, cc, b1:b1 + 31], cbv[:, cc, b0 + HS:b0 + HS + 31], flags[:, 1:2], cbv[:, cc, b1:b1 + 31],
                    ALU.mult, ALU.add, [b_cbuf, b_const], [b_cbuf])

    def layer_consts_dg(l):
        for cc in range(2):
            for j in range(CW):
                col = cfg.CWD + (2 * l + cc) * CW + j
                ts("pool", dgv[:, cc, j, :], identb[:, :], gn[:, col:col + 1], None, ALU.mult, None,
                   [b_const], [b_dg[cc]])

    def layer_consts(l):
        dma("sp", "sink", sinkst[0:1, :], sinkx_d[0:1, l * 1024:(l + 1) * 1024], [], [b_sinkst])
        act(esrow[0:1, :], sinkst[0:1, :], AF.Exp, [b_sinkst], [b_esrow])

    def attn_scores(u, lo, hi_):
        gi, qcols, nq = u["gi"], u["qcols"], u["nq"]
        N = 4 * nq
        for (k0, nk, vap, bias, negf) in u["kts"][lo:hi_]:
            ps_, bps = PS()
            mm(ps_[0:nk, 0:N], kT[:, k0:k0 + nk], actv[:, 8 + 4 * gi:12 + 4 * gi, qcols], True, True,
               [b_kT] + b_act[8 + 4 * gi:12 + 4 * gi], [bps])
            pi = 16 + st["pt"]; st["pt"] = (st["pt"] + 1) % 6
            ptv = actv[0:nk, pi, 0:N]
            if bias is None:
                act(ptv, ps_[0:nk, 0:N], AF.Exp, [bps], [b_act[pi]], scale=0.125)
            else:
                si = st["sb"]; st["sb"] = 1 - si
                sv = sbt[si][0:nk, 0:N]
                stt("dve", sv, ps_[0:nk, 0:N], 0.125, bias, ALU.mult, ALU.add, [bps, b_const], [b_sbt[si]])
                if negf:
                    ts("dve", sv, sv, flags[0:nk, 0:1], None, ALU.add, None, [b_sbt[si], b_const], [b_sbt[si]])
                act(ptv, sv, AF.Exp, [b_sbt[si]], [b_act[pi]])
            u["pts"].append((ptv, b_act[pi], nk, vap))

    def attn_pv(u):
        gi, nq, oi = u["gi"], u["nq"], u["oi"]
        R0 = 64 * gi
        N = 4 * nq
        pts = u["pts"]
        po, bpo = PS()
        for i, (ptv, bpt, nk, vap) in enumerate(pts):
            mm(po[:, 0:N], vap, ptv, i == 0, i == len(pts) - 1, [b_vt, bpt], [bpo])
        pd, bpd = PS()
        for i, (ptv, bpt, nk, vap) in enumerate(pts):
            mm(pd[:, 0:N], onesb[0:nk, :], ptv, i == 0, False, [b_const, bpt], [bpd])
        mm(pd[:, 0:N], onesb[0:1, :], u["sap"], False, True, [b_const, b_esrow], [bpd])
        act(rden[R0:R0 + 64, 0:N], pd[R0:R0 + 64, 0:N], AF.Ln, [bpd], [b_rden])
        act(rden[R0:R0 + 64, 0:N], rden[R0:R0 + 64, 0:N], AF.Exp, [b_rden], [b_rden], scale=-1.0)
        tt("dve", otmps[oi][R0:R0 + 64, 0:N], po[R0:R0 + 64, 0:N], rden[R0:R0 + 64, 0:N], ALU.mult, [bpo, b_rden], [b_otmps[oi]])

    def attention_run(l, units, hooks):
        def split(u):
            return (len(u["kts"]) + 1) // 2
        pend = []
        if units:
            attn_scores(units[0], 0, len(units[0]["kts"]))
        for i, u in enumerate(units):
            nxt = units[i + 1] if i + 1 < len(units) else None
            if nxt is not None:
                attn_scores(nxt, 0, split(nxt))
            attn_pv(u)
            if nxt is not None:
                attn_scores(nxt, split(nxt), len(nxt["kts"]))
            if u["fin"] is not None:
                pend.append(u["fin"])
                if len(pend) > 1:
                    attn_finish(l, *pend.pop(0))
            if i in hooks:
                hooks.pop(i)()
        while pend:
            attn_finish(l, *pend.pop(0))
        for k in sorted(hooks):
            hooks[k]()

    def attn_finish(l, nq, out_cols, oi):
        otmp = otmps[oi]; b_otmp = b_otmps[oi]
        ov = otmp[:, 0:4 * nq].rearrange("p (g q) -> p g q", g=4)
        for g_ in range(4):
            tt("pool", actv[:, g_, 0:nq], ov[:, g_, :], ov[:, g_, :], ALU.mult, [b_otmp], [b_act[g_]])
        pt_, bp = PS()
        for g_ in range(4):
            mm(pt_[:, 0:nq], onesb[:, :], actv[:, g_, 0:nq], g_ == 0, g_ == 3, [b_const, b_act[g_]], [bp])
        act(lnout[:, 0:nq], pt_[:, 0:nq], AF.Ln, [bp, b_const], [b_lnout], scale=1.0 / 512.0, bias=epsb[:, 0:1])
        act(rstt[1][:, 0:nq], lnout[:, 0:nq], AF.Exp, [b_lnout], [b_rst[1]], scale=-0.5)
        for g_ in range(4):
            col = cfg.GB + 8 * l + g_
            stt("dve", onv[:, g_, out_cols], ov[:, g_, :], gn[:, col:col + 1], rstt[1][:, 0:nq], ALU.mult, ALU.mult,
                [b_otmp, b_rst[1], b_const], [b_on[g_]])

    def pass_b(l, g, last):
        gid = g["gid"]; nreal = g["nreal"]; nmeta = g["nmeta"]; nh = g["nh"]
        ntr = nreal // 128
        nnt = ntr + 1
        nblk_h = HS // 128
        for ci, c in enumerate(g["chunks"]):
            n = c["n"]; g0 = c["g0"]
            cast_some(l + 1, 1)
            hi = load_h(c)
            norm_h(hi, n, cfg.GM + 8 * l)
            wb = 6 * NF
            ycv = yct[:, :].rearrange("p (c t) -> p c t", c=2)
            accs = [PS() for _ in range(4)]
            for nt in range(nnt):
                nr = 128 if nt < ntr else nmeta
                flat, br_ = ring_load(dft_d[gid][ci, nt][0:nr, :, 0:n],
                                      lambda s: s.rearrange("p (a k) -> p a k", a=2)[0:nr, :, 0:n], None)
                tv_ = flat.rearrange("p (a k) -> p a k", a=2)
                for cc in range(2):
                    for ab in range(2):
                        pa_, bpa_ = accs[ab * 2 + cc]
                        mm(pa_[:, :n], uftv[0:nr, nt, cc * 128:(cc + 1) * 128], tv_[0:nr, ab, 0:n], nt == 0, nt == nnt - 1,
                           [b_uft, br_], [bpa_])
            if c["kind"] == "real":
                segs = [(cbase(c["hloc"]) + 16 + c["coff"], 0, n)]
            else:
                segs = [(cbase(hl), hl * 16, 16) for hl in range(nh)]
            for cc in range(2):
                py, bpy = PS()
                for (p0, o0, ns) in segs:
                    for j in range(CW):
                        mm(py[:, o0:o0 + ns], dgv[:, cc, j, :], cbv[:, cc, p0 + j:p0 + j + ns], j == 0, j == CW - 1,
                           [b_dg[cc], b_cbuf], [bpy])
                col = cfg.CBD + 2 * l + cc
                ts("dve", ycv[:, cc, :n], py[:, :n], gn[:, col:col + 1], None, ALU.add, None, [bpy, b_const], [b_yc])
                cp("act", actv[:, cc, :n], ycv[:, cc, :n], [b_yc], [b_act[cc]])
            pfs = []
            for cc in range(2):
                cp("act", actv[:, 16 + cc, :n], accs[cc][0][:, :n], [accs[cc][1]], [b_act[16 + cc]])
                cp("dve", actv[:, 18 + cc, :n], accs[2 + cc][0][:, :n], [accs[2 + cc][1]], [b_act[18 + cc]])
            for cc in range(2):
                pf_, bpf_ = PS()
                mm(pf_[:, :n], c64v[:, cc, 0, :], actv[:, 16 + cc, :n], True, False, [b_const, b_act[16 + cc]], [bpf_])
                mm(pf_[:, :n], c64v[:, cc, 1, :], actv[:, 18 + cc, :n], False, True, [b_const, b_act[18 + cc]], [bpf_])
                pfs.append((pf_, bpf_))
            pm, bpm = PS()
            for cc in range(2):
                mm(pm[:, :n], onesb[:, :], actv[:, cc, :n], cc == 0, cc == 1, [b_const, b_act[cc]], [bpm])
            for j in range(4):
                tq, bq = wtile(l, wb + j)
                pq, bpq = PS()
                for kc in range(8):
                    mm(pq[:, :n], tq[:, kc, :], hnv[:, kc, :n], kc == 0, kc == 7, [bq, b_hn[kc]], [bpq])
                P.add("pool", (lambda o: (lambda e: e.memset(o, 0.0)))(actv[64:128, 8 + j, :n]), [], [b_act[8 + j]])
                P.add("pool", (lambda o: (lambda e: e.memset(o, 0.0)))(actv[0:64, 12 + j, :n]), [], [b_act[12 + j]])
                cp("act", actv[0:64, 8 + j, :n], pq[0:64, :n], [bpq], [b_act[8 + j]])
                cp("dve", actv[64:128, 12 + j, :n], pq[64:128, :n], [bpq], [b_act[12 + j]])
            stats([(pfs[cc][0][:, :n], pfs[cc][1], True) for cc in range(2)], n, 256.0, 1, s0=2)
            for cc in range(2):
                col = cfg.GB + 8 * l + 4 + cc
                stt("dve", onv[:, 4 + cc, :n], pfs[cc][0][:, :n], gn[:, col:col + 1], rstt[1][:, :n], ALU.mult, ALU.mult,
                    [pfs[cc][1], b_rst[1], b_const], [b_on[4 + cc]])
            for cc in range(2):
                stt("dve", ycv[:, cc, :n], pm[:, :n], -1.0 / 256.0, ycv[:, cc, :n], ALU.mult, ALU.add, [bpm, b_yc], [b_yc])
            stats_pre([(ycv[:, cc, :n], b_yc, False) for cc in range(2)], n, s0=4)

            def conv_stage2(l=l, n=n, ycv=ycv):
                stats_post(2, n, 256.0, 2, s0=4)
                for cc in range(2):
                    tt("dve", ycv[:, cc, :n], ycv[:, cc, :n], rstt[2][:, :n], ALU.mult, [b_yc, b_rst[2]], [b_yc])
                    cg, cb_ = cfg.CLG + 2 * l + cc, cfg.CLB + 2 * l + cc
                    ts("dve", ycv[:, cc, :n], ycv[:, cc, :n], gn[:, cg:cg + 1], gn[:, cb_:cb_ + 1], ALU.mult, ALU.add,
                       [b_yc, b_const], [b_yc])
                    act(ycv[:, cc, :n], ycv[:, cc, :n], AF.Silu, [b_yc], [b_yc])
                stats_pre([(ycv[:, cc, :n], b_yc, False) for cc in range(2)], n, s0=6)

            def conv_stage3(l=l, n=n, ycv=ycv):
                stats_post(2, n, 256.0, 2, s0=6)
                for cc in range(2):
                    col = cfg.GB + 8 * l + 6 + cc
                    stt("dve", onv[:, 6 + cc, :n], ycv[:, cc, :n], gn[:, col:col + 1], rstt[2][:, :n], ALU.mult, ALU.mult,
                        [b_yc, b_rst[2], b_const], [b_on[6 + cc]])
            mk0 = nreal
            units = []
            if c["kind"] == "real":
                hloc = c["hloc"]
                for blk in range(4):
                    bi = c["coff"] // 128 + blk
                    gb = g0 + blk * 128
                    qcols = slice(blk * 128, blk * 128 + 128)
                    for gi in range(2):
                        kts = []
                        if gid == 0:
                            mb = mtab[0:32, hloc * 512:(hloc + 1) * 512]
                        else:
                            mb = None
                        kts.append((mk0, nmeta, vtv[0:nmeta, ntr, :], mb, False))
                        if bi > 0:
                            kts.append((gb - 128, 128, vtv[:, gb // 128 - 1, :], btv[:, gi * 3 + 0, :], False))
                        elif hloc == 1:
                            kts.append((gb - 128, 128, vtv[:, gb // 128 - 1, :], btv[:, gi * 3 + 0, :], True))
                        kts.append((gb, 128, vtv[:, gb // 128, :], btv[:, gi * 3 + 1, :], False))
                        if bi < nblk_h - 1:
                            kts.append((gb + 128, 128, vtv[:, gb // 128 + 1, :], btv[:, gi * 3 + 2, :], False))
                        elif hloc == 0 and nh == 2:
                            kts.append((gb + 128, 128, vtv[:, gb // 128 + 1, :], btv[:, gi * 3 + 2, :], True))
                        units.append(dict(gi=gi, qcols=qcols, nq=128, kts=kts, oi=blk % 2, pts=[],
                                          sap=esrow[0:1, gi * 512:(gi + 1) * 512],
                                          fin=(128, qcols, blk % 2) if gi == 1 else None))
            else:
                for hl in range(nh):
                    qcols = slice(hl * 16, hl * 16 + 16)
                    for gi in range(2):
                        kts = []
                        mb = mmb[0:32, hl * 64:(hl + 1) * 64] if gid == 0 else None
                        kts.append((mk0, nmeta, vtv[0:nmeta, ntr, :], mb, False))
                        kts.append((hl * HS, 128, vtv[:, hl * HS // 128, :], mqb[:, gi * 64:(gi + 1) * 64], False))
                        sap = esrow[0:1, gi * 512:(gi + 1) * 512].rearrange("p (g q) -> p g q", g=4)[:, :, 0:16]
                        units.append(dict(gi=gi, qcols=qcols, nq=16, kts=kts, oi=hl % 2, pts=[], sap=sap,
                                          fin=(16, qcols, hl % 2) if gi == 1 else None))
            attention_run(l, units, {0: conv_stage2, 2: conv_stage3})
            for dc in range(8):
                two, bwo = wtile(l, wb + 12 + dc)
                pw, bpw = PS()
                for fc in range(8):
                    mm(pw[:, :n], two[:, fc, :], onv[:, fc, :n], fc == 0, fc == 7, [bwo, b_on[fc]], [bpw])
                tt("dve", hbv[hi][:, dc, :n], pw[:, :n], hbv[hi][:, dc, :n], ALU.add, [bpw, b_hb[hi][dc]], [b_hb[hi][dc]])
            ffn(l, 1, hi, n)
            if not last:
                store_h(c, hi)
            elif c["kind"] == "real":
                stats([(hbv[hi][:, kc, :n], b_hb[hi][kc], False) for kc in range(8)], n, float(D), 0)
                for kc in range(8):
                    stt("dve", hbv[hi][:, kc, :n], hbv[hi][:, kc, :n], gn[:, cfg.GF + kc:cfg.GF + kc + 1], rstt[0][:, :n],
                        ALU.mult, ALU.mult, [b_hb[hi][kc], b_rst[0], b_const], [b_hb[hi][kc]])
                for t_ in range(n // 128):
                    oi = t_ % 2
                    for hf in range(2):
                        pt_, bp = PS()
                        for j in range(4):
                            kc = hf * 4 + j
                            P.add("pe", (lambda o, i_, idn: (lambda e: e.transpose(o, i_, idn)))(
                                pt_[:, j * 128:(j + 1) * 128], hbv[hi][:, kc, t_ * 128:(t_ + 1) * 128], identf[:, :]),
                                [b_hb[hi][kc], b_const], [bp])
                        cp("act" if hf == 0 else "dve", ost[:, oi * 1024 + hf * 512:oi * 1024 + hf * 512 + 512], pt_[:, :],
                           [bp], [b_ost[oi]])
                    r0 = c["s0"] + t_ * 128
                    dma("pool", "out%d" % oi, yout[r0:r0 + 128, :], ost[:, oi * 1024:(oi + 1) * 1024], [b_ost[oi]], [])

    for l in range(L):
        layer_consts(l)
        for g in groups:
            pass_a(l, g)
            pass_b(l, g, l == L - 1)
        cast_some(l + 1, 1000)

    P.emit(nc, es)
    es.close()
    return nc


_CACHE = {}


def run(cfg, inputs):
    in_maps = host_prep(cfg, inputs)
    key = (cfg.L, cfg.HS, cfg.DFF)
    if key not in _CACHE:
        _CACHE[key] = build_program(cfg)
    nc = _CACHE[key]
    res = run_bass_kernel_spmd(nc, in_maps, core_ids=list(range(NCORES)))
    HS = cfg.HS
    nb_p = 16
    yp = np.empty((nb_p, HS, D), np.float32)
    ys = np.empty((4, 2 * HS, D), np.float32)
    for c in range(NCORES):
        y = np.asarray(res.results[c]["yout"], np.float32)
        if c < 4:
            for i in range(3):
                yp[3 * c + i] = y[i * HS:(i + 1) * HS]
        else:
            s_ = c - 4
            ys[s_] = y[0:2 * HS]
            yp[12 + s_] = y[2 * HS:3 * HS]
    return yp, ys


def kernel(**inputs):
    cfg = Cfg(L=4, HS=2048, DFF=2816)
    return run(cfg, inputs)
```

```python
import math
from contextlib import ExitStack

import numpy as np
import ml_dtypes

import concourse.bass as bass
import concourse.mybir as mybir
from concourse.bass_utils import run_bass_kernel_spmd

F32 = mybir.dt.float32
BF16 = mybir.dt.bfloat16
AF = mybir.ActivationFunctionType
ALU = mybir.AluOpType
NPBF = ml_dtypes.bfloat16

D = 1024
NM = 16
EPS = 1e-6
NEG = -1e30
CW = 31
NCORES = 8


class Cfg:
    def __init__(s, L=4, HS=2048, DFF=2816):
        s.L, s.HS, s.DFF = L, HS, DFF
        s.NF = DFF // 128
        assert s.NF % 2 == 0 and HS % 512 == 0
        s.TR = 3 * HS
        s.T = s.TR + 3 * NM
        s.NTL = 6 * s.NF + 20
        s.NKC0, s.NNT0 = 2 * HS // 512 + 1, 2 * HS // 128 + 1
        s.NKC1, s.NNT1 = HS // 512 + 1, HS // 128 + 1
        L8 = L * 8
        s.G1, s.GM, s.G2, s.GF, s.GB = 0, L8, 2 * L8, 3 * L8, 3 * L8 + 8
        s.CBD = s.GB + L8
        s.CLG = s.CBD + 2 * L
        s.CLB = s.CLG + 2 * L
        s.CWD = s.CLB + 2 * L
        s.NGC = s.CWD + 2 * L * CW
        s.CB = 2 * (HS + 46)
        s.NSLOT = max(s.NF, 22)


def _cols(v):
    return np.ascontiguousarray(np.asarray(v, np.float32).reshape(-1, 128).T)


def _dft_table(cfg, pos, seq, Ls, nkc, nnt):
    n = len(pos)
    pp = np.zeros(nnt * 128, np.int64); ss = np.full(nnt * 128, -1, np.int64)
    pp[:n] = pos; ss[:n] = seq
    pk = np.zeros(nkc * 512, np.int64); sk = np.full(nkc * 512, -2, np.int64)
    pk[:n] = pos; sk[:n] = seq
    out = np.empty((nkc, nnt, 128, 2, 512), NPBF)
    for kc in range(nkc):
        k_p = pk[kc * 512:(kc + 1) * 512]; k_s = sk[kc * 512:(kc + 1) * 512]
        r = (pp[:, None] * k_p[None, :]) % Ls
        ang = (2.0 * np.pi / Ls) * r.astype(np.float64)
        same = ((ss[:, None] == k_s[None, :]) & (ss[:, None] >= 0)).astype(np.float64) / math.sqrt(Ls)
        c = (np.cos(ang) * same).reshape(nnt, 128, 512)
        s_ = (np.sin(ang) * same).reshape(nnt, 128, 512)
        out[kc, :, :, 0, :] = c.astype(NPBF)
        out[kc, :, :, 1, :] = s_.astype(NPBF)
    return out


def host_prep(cfg, inp):
    L, HS, NF = cfg.L, cfg.HS, cfg.NF
    f32 = np.float32
    xp = np.asarray(inp["x_prompt"], f32); xs = np.asarray(inp["x_sample"], f32)
    meta = np.asarray(inp["meta_tokens"], f32)
    wt = np.empty((L * cfg.NTL, 128, 1024), f32)
    qperm = np.concatenate([np.r_[64 * j:64 * j + 64, 256 + 64 * j:256 + 64 * j + 64] for j in range(4)])
    for l in range(L):
        base = l * cfg.NTL
        for wi, (g, u, d) in enumerate((("ffn1_w_gate", "ffn1_w_up", "ffn1_w_down"),
                                        ("ffn2_w_gate", "ffn2_w_up", "ffn2_w_down"))):
            o = base + wi * 3 * NF
            wg = np.asarray(inp[g][l], f32).reshape(8, 128, NF, 128).transpose(2, 1, 0, 3)
            wu = np.asarray(inp[u][l], f32).reshape(8, 128, NF, 128).transpose(2, 1, 0, 3)
            wt[o:o + 2 * NF:2] = wg.reshape(NF, 128, 1024)
            wt[o + 1:o + 2 * NF:2] = wu.reshape(NF, 128, 1024)
            wd = np.asarray(inp[d][l], f32).reshape(NF // 2, 2, 128, 2, 4, 128).transpose(3, 0, 2, 1, 4, 5)
            wt[o + 2 * NF:o + 3 * NF] = wd.reshape(NF, 128, 1024)
        win = np.asarray(inp["w_in"][l], f32)
        cols = np.concatenate([qperm, np.arange(512, 1024),
                               np.r_[1024:1152], np.r_[1280:1408],
                               np.r_[1152:1280], np.r_[1408:1536]])
        wp = win[:, cols].reshape(8, 128, 12, 128).transpose(2, 1, 0, 3)
        wt[base + 6 * NF:base + 6 * NF + 12] = wp.reshape(12, 128, 1024)
        wo = np.asarray(inp["w_out"][l], f32)
        rows = np.concatenate([qperm, np.arange(512, 1024)])
        wop = wo[rows].reshape(8, 128, 8, 128).transpose(2, 1, 0, 3)
        wt[base + 6 * NF + 12:base + 6 * NF + 20] = wop.reshape(8, 128, 1024)
    gn = np.zeros((128, cfg.NGC), f32)
    for l in range(L):
        gn[:, cfg.G1 + 8 * l:cfg.G1 + 8 * l + 8] = _cols(inp["ffn1_norm"][l])
        gn[:, cfg.GM + 8 * l:cfg.GM + 8 * l + 8] = _cols(inp["mix_norm"][l])
        gn[:, cfg.G2 + 8 * l:cfg.G2 + 8 * l + 8] = _cols(inp["ffn2_norm"][l])
        br = np.asarray(inp["branch_norm"][l], f32)
        gn[:, cfg.GB + 8 * l:cfg.GB + 8 * l + 8] = _cols(br[rows])
        gn[:, cfg.CBD + 2 * l:cfg.CBD + 2 * l + 2] = _cols(inp["conv_b_dw"][l])
        gn[:, cfg.CLG + 2 * l:cfg.CLG + 2 * l + 2] = _cols(inp["conv_ln_g"][l])
        gn[:, cfg.CLB + 2 * l:cfg.CLB + 2 * l + 2] = _cols(inp["conv_ln_b"][l])
        w = np.asarray(inp["conv_w_dw"][l], f32)
        for cc in range(2):
            o = cfg.CWD + (2 * l + cc) * CW
            gn[:, o:o + CW] = w[:, cc * 128:(cc + 1) * 128].T
    gn[:, cfg.GF:cfg.GF + 8] = _cols(inp["final_norm"])
    sink = np.asarray(inp["attn_sink"], f32)
    sinkx = np.ascontiguousarray(np.repeat(sink.reshape(L, 2, 4, 1), 128, axis=3)).reshape(1, L * 1024)
    slopes = 2.0 ** (-(np.arange(8) + 1.0))
    ii = np.arange(128)[None, :]; jj = np.arange(128)[:, None]
    btab = np.zeros((128, 6, 4, 128), f32)
    for gi in range(2):
        for g in range(4):
            s = slopes[4 * gi + g]
            dp = 128 + ii - jj
            btab[:, gi * 3 + 0, g] = np.where(jj >= ii, -s * dp, NEG)
            btab[:, gi * 3 + 1, g] = -s * np.abs(ii - jj)
            dn = 128 + jj - ii
            btab[:, gi * 3 + 2, g] = np.where(jj <= ii, -s * dn, NEG)
    btab = btab.reshape(128, 6 * 512).astype(NPBF)
    mqb = np.zeros((128, 2, 4, 16), f32)
    tt = np.arange(128)[:, None]; mm = np.arange(16)[None, :]
    for gi in range(2):
        for g in range(4):
            dist = 16 + tt - mm
            mqb[:, gi, g] = np.where(dist <= 128, -slopes[4 * gi + g] * dist, NEG)
    mqb = mqb.reshape(128, 128)
    mmb = np.full((32, 2, 64), NEG, f32)
    mmb[0:16, 0] = 0.0; mmb[16:32, 1] = 0.0
    mmb = mmb.reshape(32, 128)
    c64 = np.zeros((128, 2, 2, 128), np.float64)
    a = np.arange(128)
    ang = 2 * np.pi * ((a[:, None] % 64) * (a[None, :] % 64) % 64) / 64.0
    blk = ((a[:, None] // 64) == (a[None, :] // 64)) / 8.0
    for cc in range(2):
        c64[:, cc, 0] = np.cos(ang) * blk
        c64[:, cc, 1] = -np.sin(ang) * blk
    c64 = c64.reshape(128, 512).astype(NPBF)
    identf = np.eye(128, dtype=f32)

    def dft_for(kind):
        if kind == "P":
            pos = np.concatenate([16 + np.arange(HS), 16 + np.arange(HS), np.arange(16), np.arange(16)])
            seq = np.concatenate([np.zeros(HS), np.ones(HS), np.zeros(16), np.ones(16)]).astype(np.int64)
            return _dft_table(cfg, pos, seq, HS + 16, cfg.NKC0, cfg.NNT0)
        pos = np.concatenate([16 + np.arange(2 * HS), np.arange(16), np.zeros(16, np.int64)])
        seq = np.concatenate([np.zeros(2 * HS + 16), -np.ones(16)]).astype(np.int64)
        return _dft_table(cfg, pos, seq, 2 * HS + 16, cfg.NKC0, cfg.NNT0)

    dftP, dftS = dft_for("P"), dft_for("S")
    pos1 = np.concatenate([16 + np.arange(HS), np.arange(16)])
    dft1 = _dft_table(cfg, pos1, np.zeros(HS + 16, np.int64), HS + 16, cfg.NKC1, cfg.NNT1)

    def mtab_for(kind):
        m = np.full((32, 2, 512), NEG, f32)
        m[0:16, 0] = 0.0
        if kind == "P":
            m[16:32, 1] = 0.0
        else:
            m[0:16, 1] = 0.0
        return m.reshape(32, 1024)

    in_maps = []
    for c in range(NCORES):
        kind = "P" if c < 4 else "S"
        xin = np.empty((cfg.T, D), f32)
        if kind == "P":
            for i in range(3):
                xin[i * HS:(i + 1) * HS] = xp[3 * c + i]
            mrows = [meta, meta, meta]
        else:
            s_ = c - 4
            xin[0:2 * HS] = xs[s_]
            xin[2 * HS:3 * HS] = xp[12 + s_]
            mrows = [meta, np.zeros_like(meta), meta]
        for i in range(3):
            xin[cfg.TR + 16 * i:cfg.TR + 16 * i + 16] = mrows[i]
        flags = np.zeros((128, 4), f32)
        flags[:, 0] = NEG if kind == "P" else 0.0
        flags[:, 1] = 0.0 if kind == "P" else 1.0
        flags[:, 2] = 1.0 if kind == "P" else 0.0
        in_maps.append(dict(xin=xin, wt=wt, gn=gn, sinkx=sinkx, btab=btab, mtab=mtab_for(kind), mqb=mqb,
                            mmb=mmb, c64=c64, identf=identf, flags=flags,
                            dft0=dftP if kind == "P" else dftS, dft1=dft1))
    return in_maps


class Buf:
    __slots__ = ("name", "lw", "rd")

    def __init__(s, name):
        s.name, s.lw, s.rd = name, None, {}


class Op:
    __slots__ = ("eng", "fn", "deps", "dmaq", "dmaval", "need_inc", "incval")


class Prog:
    ENGS = ("pe", "act", "dve", "pool", "sp")

    def __init__(s):
        s.ops = {e: [] for e in s.ENGS}
        s.ndma = {}
        s.dmaeng = {}

    def add(s, eng, fn, R=(), W=(), dma=None):
        op = Op()
        op.eng, op.fn, op.need_inc, op.incval = eng, fn, False, 0
        op.dmaq = dma
        if dma:
            s.ndma[dma] = s.ndma.get(dma, 0) + 1
            assert s.dmaeng.setdefault(dma, eng) == eng
            op.dmaval = 16 * s.ndma[dma]
        deps = {}

        def need(p):
            if p is None:
                return
            if p.dmaq is None and p.eng == "pe" and eng == "pe" and not dma:
                return
            key = ("d", p.dmaq) if p.dmaq else ("e", p.eng)
            cur = deps.get(key)
            if cur is None or s._later(p, cur):
                deps[key] = p

        for b in R:
            need(b.lw)
        for b in W:
            need(b.lw)
            for r in b.rd.values():
                need(r)
        op.deps = list(deps.values())
        for p in op.deps:
            if p.dmaq is None:
                p.need_inc = True
        op_key = ("d", dma) if dma else ("e", eng)
        for b in R:
            b.rd[op_key] = op
        for b in W:
            b.lw = op; b.rd = {}
        op.incval = len(s.ops[eng])
        s.ops[eng].append(op)
        return op

    @staticmethod
    def _later(a, b):
        if a.dmaq:
            return a.dmaval > b.dmaval
        return a.incval > b.incval

    def emit(s, nc, es):
        esem = {e: es.enter_context(nc.semaphore("sem_" + e)) for e in s.ENGS}
        dsem = {q: es.enter_context(nc.semaphore("dsem_" + q)) for q in s.ndma}
        for e in s.ENGS:
            c = 0
            for op in s.ops[e]:
                if op.dmaq is None and op.need_inc:
                    c += 1; op.incval = c
                else:
                    op.incval = -1
            assert c < 60000, (e, c)
        for q in s.ndma:
            assert 16 * s.ndma[q] < 60000, (q, s.ndma[q])
        block = es.enter_context(nc.Block())

        def body(ename):
            def run(eng):
                waited = {}
                for op in s.ops[ename]:
                    for p in op.deps:
                        if p.dmaq:
                            key = ("d", p.dmaq); sem = dsem[p.dmaq]; val = p.dmaval
                        else:
                            key = ("e", p.eng); sem = esem[p.eng]; val = p.incval
                        if waited.get(key, 0) < val:
                            eng.wait_ge(sem, val); waited[key] = val
                    ins = op.fn(eng)
                    if op.dmaq:
                        ins.then_inc(dsem[op.dmaq], 16)
                    elif op.need_inc:
                        ins.then_inc(esem[ename], 1)
                for q in s.ndma:
                    if s.dmaeng[q] == ename:
                        eng.wait_ge(dsem[q], 16 * s.ndma[q])
            return run

        block.sync(body("sp")); block.gpsimd(body("pool")); block.vector(body("dve"))
        block.tensor(body("pe")); block.scalar(body("act"))


def build_program(cfg):
    L, HS, NF = cfg.L, cfg.HS, cfg.NF
    nc = bass.Bass("TRN2", target_bir_lowering=False)
    P = Prog()

    def DT(name, shape, dt, kind):
        return nc.dram_tensor(name, list(shape), dt, kind=kind).ap()

    xin = DT("xin", [cfg.T, D], F32, "ExternalInput")
    wt = DT("wt", [L * cfg.NTL, 128, 1024], F32, "ExternalInput")
    gn_d = DT("gn", [128, cfg.NGC], F32, "ExternalInput")
    sinkx_d = DT("sinkx", [1, L * 1024], F32, "ExternalInput")
    btab_d = DT("btab", [128, 6 * 512], BF16, "ExternalInput")
    mtab_d = DT("mtab", [32, 1024], F32, "ExternalInput")
    mqb_d = DT("mqb", [128, 128], F32, "ExternalInput")
    mmb_d = DT("mmb", [32, 128], F32, "ExternalInput")
    c64_d = DT("c64", [128, 512], BF16, "ExternalInput")
    identf_d = DT("identf", [128, 128], F32, "ExternalInput")
    flags_d = DT("flags", [128, 4], F32, "ExternalInput")
    dft_d = [DT("dft0", [cfg.NKC0, cfg.NNT0, 128, 2, 512], BF16, "ExternalInput"),
             DT("dft1", [cfg.NKC1, cfg.NNT1, 128, 2, 512], BF16, "ExternalInput")]
    yout = DT("yout", [cfg.TR, D], F32, "ExternalOutput")
    wtb = DT("wtb", [L * cfg.NTL, 128, 1024], BF16, "Internal")
    hT = DT("hT", [128, 8, cfg.T], F32, "Internal")

    es = ExitStack()

    def SB(name, cols, dt, parts=128):
        return es.enter_context(nc.sbuf_tensor("sb_" + name, [parts, cols], dt))

    identf = SB("identf", 128, F32); identb = SB("identb", 128, BF16); onesb = SB("onesb", 128, BF16)
    gn = SB("gn", cfg.NGC, F32); flags = SB("flags", 4, F32); epsb = SB("epsb", 1, F32)
    btab = SB("btab", 6 * 512, BF16); mtab = SB("mtab", 1024, F32); mqb = SB("mqb", 128, F32); mmb = SB("mmb", 128, F32)
    c64 = SB("c64", 512, BF16)
    esrow = SB("esrow", 1024, BF16)
    dg = SB("dg", 2 * CW * 128, BF16)
    NT0 = cfg.NNT0
    TG0 = 2 * HS + 32
    kT = SB("kT", TG0, BF16); vt = SB("vt", NT0 * 128, BF16); uft = SB("uft", NT0 * 256, BF16)
    cbuf = SB("cbuf", 2 * cfg.CB, BF16)
    hbt = [SB("hb%d" % i, 8 * 512, F32) for i in range(2)]
    hnt = SB("hn", 8 * 512, BF16); hnt2 = SB("hn2", 8 * 512, BF16); sqt = SB("sqr", 3 * 512, BF16)
    actt = SB("act", cfg.NSLOT * 512, BF16)
    ont = SB("on", 8 * 512, BF16)
    sgt = SB("sg", 2 * 512, BF16)
    lnout = SB("lnout", 512, F32)
    rstt = [SB("rst%d" % i, 512, F32) for i in range(3)]
    misc = SB("misc", 4 * 512, F32)
    sbt = [misc[:, 0:512], misc[:, 512:1024]]
    otmps = [misc[:, 1024:1536], misc[:, 1536:2048]]; rden = SB("rden", 512, F32)
    yct = SB("yct", 2 * 512, F32); ost = misc; sinkst = yct
    ringt = SB("ring", 8 * 1024, BF16)
    pst = [es.enter_context(nc.psum_tensor("ps%d" % i, [128, 512], F32)) for i in range(8)]

    B = Buf
    b_const = B("const"); b_dg = [B("dg0"), B("dg1")]; b_esrow = B("esrow")
    b_kT, b_vt, b_uft, b_cbuf = B("kT"), B("vt"), B("uft"), B("cbuf")
    b_hb = [[B("hb%d_%d" % (i, k)) for k in range(8)] for i in range(2)]
    b_hn_l = [[B("hn%d_%d" % (j, k)) for k in range(8)] for j in range(2)]
    b_sq = [B("sq%d" % k) for k in range(3)]
    b_act = [B("act%d" % k) for k in range(cfg.NSLOT)]
    b_on = [B("on%d" % k) for k in range(8)]
    b_sg = [B("sg0"), B("sg1")]; b_lnout = B("lnout"); b_rst = [B("rst%d" % i) for i in range(3)]
    b_sbt = [B("sbt0"), B("sbt1")]; b_otmps = [B("otmp0"), B("otmp1")]; b_rden = B("rden"); b_yc = B("yc")
    b_ost = [B("ost0"), B("ost1")]
    b_ring = [B("ring%d" % i) for i in range(8)]
    b_ps = [B("ps%d" % i) for i in range(8)]
    b_hTt = [B("hT%d" % i) for i in range((cfg.T + 127) // 128)]; b_wtb = [B("wtb%d" % l) for l in range(L)]

    hbv = [t[:, :].rearrange("p (c t) -> p c t", c=8) for t in hbt]
    hnv_l = [hnt[:, :].rearrange("p (c t) -> p c t", c=8), hnt2[:, :].rearrange("p (c t) -> p c t", c=8)]
    sqv = sqt[:, :].rearrange("p (c t) -> p c t", c=3)
    sel = {"hn": 0}

    def HV():
        return hnv_l[sel["hn"]]

    def HB():
        return b_hn_l[sel["hn"]]
    actv = actt[:, :].rearrange("p (c t) -> p c t", t=512)
    onv = ont[:, :].rearrange("p (c t) -> p c t", c=8)
    vtv = vt[:, :].rearrange("p (n c) -> p n c", c=128)
    uftv = uft[:, :].rearrange("p (n c) -> p n c", c=256)
    cbv = cbuf[:, :].rearrange("p (c t) -> p c t", c=2)
    dgv = dg[:, :].rearrange("p (c j m) -> p c j m", c=2, j=CW)
    btv = btab[:, :].rearrange("p (k t) -> p k t", k=6)
    c64v = c64[:, :].rearrange("p (c s m) -> p c s m", c=2, s=2)
    ringv = [ringt[:, i * 1024:(i + 1) * 1024] for i in range(8)]

    def mm(out, lhsT, rhs, start, stop, R, W):
        P.add("pe", lambda e: e.matmul(out, lhsT=lhsT, rhs=rhs, start=start, stop=stop), R, W)

    def act(out, in_, func, R, W, scale=1.0, bias=None):
        if bias is None:
            P.add("act", lambda e: e.activation(out=out, in_=in_, func=func, scale=scale), R, W)
        else:
            P.add("act", lambda e: e.activation(out=out, in_=in_, func=func, scale=scale, bias=bias), R, W)

    def tt(eng, out, in0, in1, op, R, W):
        P.add(eng, lambda e: e.tensor_tensor(out=out, in0=in0, in1=in1, op=op), R, W)

    def ts(eng, out, in0, s1, s2, op0, op1, R, W):
        if s2 is None:
            P.add(eng, lambda e: e.tensor_scalar(out=out, in0=in0, scalar1=s1, scalar2=None, op0=op0), R, W)
        else:
            P.add(eng, lambda e: e.tensor_scalar(out=out, in0=in0, scalar1=s1, scalar2=s2, op0=op0, op1=op1), R, W)

    def stt(eng, out, in0, scalar, in1, op0, op1, R, W):
        P.add(eng, lambda e: e.scalar_tensor_tensor(out=out, in0=in0, scalar=scalar, in1=in1, op0=op0, op1=op1), R, W)

    def cp(eng, out, in_, R, W):
        if eng == "act":
            P.add("act", lambda e: e.activation(out=out, in_=in_, func=AF.Copy), R, W)
        else:
            P.add(eng, lambda e: e.tensor_copy(out=out, in_=in_), R, W)

    def dma(q, key, out, in_, R, W):
        P.add(q, lambda e: e.dma_start(out=out, in_=in_), R, W, dma=key)

    st = {"ps": 0, "ring": 0, "sg": 0, "sb": 0, "pt": 0, "hb": 0, "sq": 0}

    def PS():
        i = st["ps"]; st["ps"] = (i + 1) % 8
        return pst[i], b_ps[i]

    def ring_load(src, view, l):
        i = st["ring"]; st["ring"] = (i + 1) % 8
        dma("sp", "ring%d" % i, view(ringv[i]), src, [b_wtb[l]] if l is not None else [], [b_ring[i]])
        return ringv[i], b_ring[i]

    def wtile(l, idx):
        flat, b = ring_load(wtb[l * cfg.NTL + idx], lambda s: s, l)
        return flat.rearrange("p (c m) -> p c m", c=8), b

    for (dst, src) in ((identf, identf_d), (gn, gn_d), (flags, flags_d), (btab, btab_d), (mqb, mqb_d), (c64, c64_d)):
        dma("sp", "const", dst[:, :], src[:, :], [], [b_const])
    dma("sp", "const", mtab[0:32, :], mtab_d[:, :], [], [b_const])
    dma("sp", "const", mmb[0:32, :], mmb_d[:, :], [], [b_const])
    cp("dve", identb[:, :], identf[:, :], [b_const], [b_const])
    P.add("dve", lambda e: e.memset(onesb[:, :], 1.0), [], [b_const])
    P.add("dve", lambda e: e.memset(epsb[:, :], EPS), [], [b_const])
    P.add("dve", lambda e: e.memset(cbuf[:, :], 0.0), [], [b_cbuf])
    P.add("dve", lambda e: e.memset(actt[:, :], 0.0), [], b_act)
    P.add("pool", lambda e: e.memset(kT[:, :], 0.0), [], [b_kT])
    P.add("pool", lambda e: e.memset(vt[:, :], 0.0), [], [b_vt])
    P.add("pool", lambda e: e.memset(uft[:, :], 0.0), [], [b_uft])

    CH = 8
    cast_todo = {l: [(l * cfg.NTL + t0, l * cfg.NTL + min(cfg.NTL, t0 + CH)) for t0 in range(0, cfg.NTL, CH)] for l in range(L)}

    def cast_some(l, k):
        for _ in range(k):
            if l < L and cast_todo[l]:
                a, b = cast_todo[l].pop(0)
                dma("pool", "cast%d" % l, wtb[a:b].rearrange("a p c -> (a p) c"), wt[a:b].rearrange("a p c -> (a p) c"),
                    [], [b_wtb[l]])

    ntile_in = (cfg.T + 127) // 128
    for ti in range(ntile_in):
        r0 = ti * 128; nr = min(128, cfg.T - r0)
        xi = ti % 2
        xst = hbt[0][:, xi * 1024:(xi + 1) * 1024]; bx = b_hb[0][xi]
        hst = hbt[1][:, xi * 1024:(xi + 1) * 1024].rearrange("p (c t) -> p c t", c=8); bh = b_hb[1][xi]
        dma("sp", "xl%d" % xi, xst[0:nr, :], xin[r0:r0 + nr, :], [], [bx])
        for hf in range(2):
            pt_, bp = PS()
            for j in range(4):
                kc = hf * 4 + j
                P.add("pe", (lambda o, i_, idn: (lambda e: e.transpose(o, i_, idn)))(
                    pt_[:, j * 128:j * 128 + nr], xst[0:nr, kc * 128:(kc + 1) * 128], identf[0:nr, 0:nr]),
                    [bx, b_const], [bp])
            cp("act" if hf == 0 else "dve", hst[:, hf * 4:hf * 4 + 4, 0:nr],
               pt_[:, :].rearrange("p (c t) -> p c t", c=4)[:, :, 0:nr], [bp], [bh])
        dma("pool", "xs%d" % xi, hT[:, :, r0:r0 + nr], hst[:, :, 0:nr], [bh], [b_hTt[ti]])
        if ti % 2 == 1:
            cast_some(0, 1)
    cast_some(0, 1000)

    def stats(srcs, n, count, rst_i, s0=0):
        stats_pre(srcs, n, s0)
        stats_post(len(srcs), n, count, rst_i, s0)

    def stats_post(nsrc, n, count, rst_i, s0=0):
        pt_, bp = PS()
        for k in range(nsrc):
            mm(pt_[:, :n], onesb[:, :], actv[:, s0 + k, :n], k == 0, k == nsrc - 1, [b_const, b_act[s0 + k]], [bp])
        act(lnout[:, :n], pt_[:, :n], AF.Ln, [bp, b_const], [b_lnout], scale=1.0 / count, bias=epsb[:, 0:1])
        act(rstt[rst_i][:, :n], lnout[:, :n], AF.Exp, [b_lnout], [b_rst[rst_i]], scale=-0.5)

    def stats_pre(srcs, n, s0=0):
        for k, (ap, b, inps) in enumerate(srcs):
            if inps or k % 3 == 1:
                act(actv[:, s0 + k, :n], ap, AF.Square, [b], [b_act[s0 + k]])
            else:
                tt("pool" if k % 3 == 0 else "dve", actv[:, s0 + k, :n], ap, ap, ALU.mult, [b], [b_act[s0 + k]])

    def norm_h(hi, n, gcol, rst_i=0):
        pt_, bp = PS()
        for kc in range(8):
            qi = st["sq"]; st["sq"] = (qi + 1) % 3
            ap = hbv[hi][:, kc, :n]
            if kc % 3 == 1:
                act(sqv[:, qi, :n], ap, AF.Square, [b_hb[hi][kc]], [b_sq[qi]])
            else:
                tt("pool" if kc % 3 == 0 else "dve", sqv[:, qi, :n], ap, ap, ALU.mult, [b_hb[hi][kc]], [b_sq[qi]])
            mm(pt_[:, :n], onesb[:, :], sqv[:, qi, :n], kc == 0, kc == 7, [b_const, b_sq[qi]], [bp])
        act(lnout[:, :n], pt_[:, :n], AF.Ln, [bp, b_const], [b_lnout], scale=1.0 / float(D), bias=epsb[:, 0:1])
        act(rstt[rst_i][:, :n], lnout[:, :n], AF.Exp, [b_lnout], [b_rst[rst_i]], scale=-0.5)
        for kc in range(8):
            stt("dve", HV()[:, kc, :n], hbv[hi][:, kc, :n], gn[:, gcol + kc:gcol + kc + 1], rstt[rst_i][:, :n],
                ALU.mult, ALU.mult, [b_hb[hi][kc], b_rst[rst_i], b_const], [HB()[kc]])

    def ffn(l, which, hi, n):
        ffn_norm(l, which, hi, n)
        ffn_gateup(l, which, n)
        ffn_down(l, which, hi, n)

    def ffn_norm(l, which, hi, n):
        norm_h(hi, n, (cfg.G1 if which == 0 else cfg.G2) + 8 * l)

    def ffn_gateup(l, which, n):
        base = which * 3 * NF
        for f in range(NF):
            tg, bg = wtile(l, base + 2 * f)
            tu, bu = wtile(l, base + 2 * f + 1)
            pg, bpg = PS()
            for kc in range(8):
                mm(pg[:, :n], tg[:, kc, :], HV()[:, kc, :n], kc == 0, kc == 7, [bg, HB()[kc]], [bpg])
            pu, bpu = PS()
            for kc in range(8):
                mm(pu[:, :n], tu[:, kc, :], HV()[:, kc, :n], kc == 0, kc == 7, [bu, HB()[kc]], [bpu])
            si = st["sg"]; st["sg"] = 1 - si
            sgv = sgt[:, si * 512:si * 512 + n]
            act(sgv, pg[:, :n], AF.Silu, [bpg], [b_sg[si]])
            tt("dve", actv[:, f, :n], pu[:, :n], sgv, ALU.mult, [bpu, b_sg[si]], [b_act[f]])

    def ffn_down(l, which, hi, n):
        base = which * 3 * NF
        for half in range(2):
            acc = [PS() for _ in range(4)]
            for pair in range(NF // 2):
                td, bd = wtile(l, base + 2 * NF + half * (NF // 2) + pair)
                for m in range(2):
                    f = 2 * pair + m
                    for dcl in range(4):
                        mm(acc[dcl][0][:, :n], td[:, m * 4 + dcl, :], actv[:, f, :n], f == 0, f == NF - 1,
                           [bd, b_act[f]], [acc[dcl][1]])
            for dcl in range(4):
                dc = half * 4 + dcl
                stt("dve", hbv[hi][:, dc, :n], acc[dcl][0][:, :n], 0.5, hbv[hi][:, dc, :n], ALU.mult, ALU.add,
                    [acc[dcl][1], b_hb[hi][dc]], [b_hb[hi][dc]])

    groups = [dict(gid=0, halves=[0, 1], nreal=2 * HS, mslot=cfg.TR, nmeta=32),
              dict(gid=1, halves=[2], nreal=HS, mslot=cfg.TR + 32, nmeta=16)]
    for g in groups:
        ch = []
        for hi_, hh in enumerate(g["halves"]):
            for c in range(HS // 512):
                ch.append(dict(kind="real", hloc=hi_, g0=hi_ * HS + c * 512, n=512, s0=hh * HS + c * 512, coff=c * 512))
        ch.append(dict(kind="meta", g0=g["nreal"], n=g["nmeta"], s0=g["mslot"]))
        g["chunks"] = ch
        g["nh"] = len(g["halves"])

    def cbase(hloc):
        return hloc * (HS + 46)

    def hT_bufs(c):
        return b_hTt[c["s0"] // 128:(c["s0"] + c["n"] - 1) // 128 + 1]

    def load_h(c):
        hi = st["hb"]; st["hb"] = 1 - hi
        n = c["n"]
        dma("sp", "hb%d" % hi, hbv[hi][:, :, :n], hT[:, :, c["s0"]:c["s0"] + n], hT_bufs(c), b_hb[hi])
        return hi

    def store_h(c, hi):
        n = c["n"]
        dma("pool", "st%d" % hi, hT[:, :, c["s0"]:c["s0"] + n], hbv[hi][:, :, :n], b_hb[hi], hT_bufs(c))

    def pass_a(l, g):
        if g["nh"] == 1:
            P.add("pool", lambda e: e.memset(cbv[:, :, 31 + HS:46 + HS], 0.0), [], [b_cbuf])
        chunks = g["chunks"]
        his = {}
        cast_some(l + 1, 1)
        his[0] = load_h(chunks[0])
        sel["hn"] = 0
        ffn_norm(l, 0, his[0], chunks[0]["n"])
        ffn_gateup(l, 0, chunks[0]["n"])
        for i, c in enumerate(chunks):
            n = c["n"]
            nxt = chunks[i + 1] if i + 1 < len(chunks) else None
            if nxt is not None:
                cast_some(l + 1, 1)
                his[i + 1] = load_h(nxt)
                sel["hn"] = (i + 1) % 2
                ffn_norm(l, 0, his[i + 1], nxt["n"])
            ffn_down(l, 0, his[i], n)
            if g["gid"] == 0 and i == 0:
                layer_consts_dg(l)
            store_h(c, his[i])
            sel["hn"] = i % 2
            norm_h(his[i], n, cfg.GM + 8 * l)
            if nxt is not None:
                sel["hn"] = (i + 1) % 2
                ffn_gateup(l, 0, nxt["n"])
            sel["hn"] = i % 2
            pass_a_m1(l, g, c)
        sel["hn"] = 0
        if g["nh"] == 2:
            for cc in range(2):
                b0, b1 = cbase(0), cbase(1)
                ts("pool", cbv[:, cc, b0 + 31 + HS:b0 + 31 + HS + 15], cbv[:, cc, b1 + 31:b1 + 46], flags[:, 1:2], None,
                   ALU.mult, None, [b_cbuf, b_const], [b_cbuf])
                ts("pool", cbv[:, cc, b1:b1 + 31], cbv[:, cc, b1:b1 + 31], flags[:, 2:3], None,
                   ALU.mult, None, [b_cbuf, b_const], [b_cbuf])
                stt("dve", cbv[:, cc, b1:b1 + 31], cbv[:, cc, b0 + HS:b0 + HS + 31], flags[:, 1:2], cbv[:, cc, b1:b1 + 31],
                    ALU.mult, ALU.add, [b_cbuf, b_const], [b_cbuf])

    def pass_a_m1(l, g, c):
        if True:
            n = c["n"]; g0 = c["g0"]
            wb = 6 * NF
            tk, bk = wtile(l, wb + 4)
            pk, bpk = PS()
            for kc in range(8):
                mm(pk[:, :n], tk[:, kc, :], HV()[:, kc, :n], kc == 0, kc == 7, [bk, HB()[kc]], [bpk])
            cp("act", kT[:, g0:g0 + n], pk[:, :n], [bpk], [b_kT])
            tv, bv = wtile(l, wb + 5)
            pv, bpv = PS()
            ntt = (n + 127) // 128
            for t_ in range(ntt):
                nr = min(128, n - t_ * 128)
                for kc in range(8):
                    mm(pv[0:nr, t_ * 128:(t_ + 1) * 128], HV()[:, kc, t_ * 128:t_ * 128 + nr], tv[:, kc, :],
                       kc == 0, kc == 7, [bv, HB()[kc]], [bpv])
            nr = min(128, n)
            cp("dve", vtv[0:nr, g0 // 128:g0 // 128 + ntt, :],
               pv[0:nr, 0:ntt * 128].rearrange("p (a c) -> p a c", c=128), [bpv], [b_vt])
            tf0, bf0 = wtile(l, wb + 6)
            tf1, bf1 = wtile(l, wb + 7)
            pf = [PS() for _ in range((ntt + 1) // 2)]
            for c2, (tf_, bf_) in enumerate(((tf0, bf0), (tf1, bf1))):
                for t_ in range(ntt):
                    nr = min(128, n - t_ * 128)
                    pp, bpp = pf[t_ // 2]
                    o = (t_ % 2) * 256 + c2 * 128
                    for kc in range(8):
                        mm(pp[0:nr, o:o + 128], HV()[:, kc, t_ * 128:t_ * 128 + nr], tf_[:, kc, :],
                           kc == 0, kc == 7, [bf_, HB()[kc]], [bpp])
            for q_, (pp, bpp) in enumerate(pf):
                na = min(2, ntt - 2 * q_)
                nr = min(128, n)
                cp("act" if q_ == 0 else "dve", uftv[0:nr, g0 // 128 + 2 * q_:g0 // 128 + 2 * q_ + na, :],
                   pp[0:nr, 0:na * 256].rearrange("p (a c) -> p a c", c=256), [bpp], [b_uft])
            for cc in range(2):
                ta, ba = wtile(l, wb + 8 + 2 * cc)
                tg_, bg_ = wtile(l, wb + 9 + 2 * cc)
                pa, bpa = PS()
                for kc in range(8):
                    mm(pa[:, :n], ta[:, kc, :], HV()[:, kc, :n], kc == 0, kc == 7, [ba, HB()[kc]], [bpa])
                pg, bpg = PS()
                for kc in range(8):
                    mm(pg[:, :n], tg_[:, kc, :], HV()[:, kc, :n], kc == 0, kc == 7, [bg_, HB()[kc]], [bpg])
                si = st["sb"]; st["sb"] = 1 - si
                sv = sbt[si][:, :n]
                act(sv, pg[:, :n], AF.Tanh, [bpg], [b_sbt[si]], scale=0.5)
                ts("dve", sv, sv, 0.5, 0.5, ALU.mult, ALU.add, [b_sbt[si]], [b_sbt[si]])
                if c["kind"] == "real":
                    p0 = cbase(c["hloc"]) + 31 + c["coff"]
                    tt("dve", cbv[:, cc, p0:p0 + n], pa[:, :n], sv, ALU.mult, [bpa, b_sbt[si]], [b_cbuf])
                else:
                    for hl in range(g["nh"]):
                        p0 = cbase(hl) + 15
                        tt("dve", cbv[:, cc, p0:p0 + 16], pa[:, hl * 16:hl * 16 + 16], sbt[si][:, hl * 16:hl * 16 + 16],
                           ALU.mult, [bpa, b_sbt[si]], [b_cbuf])

    def layer_consts_dg(l):
        for cc in range(2):
            for j in range(CW):
                col = cfg.CWD + (2 * l + cc) * CW + j
                ts("pool", dgv[:, cc, j, :], identb[:, :], gn[:, col:col + 1], None, ALU.mult, None,
                   [b_const], [b_dg[cc]])

    def layer_consts(l):
        dma("sp", "sink", sinkst[0:1, :], sinkx_d[0:1, l * 1024:(l + 1) * 1024], [], [b_yc])
        act(esrow[0:1, :], sinkst[0:1, :], AF.Exp, [b_yc], [b_esrow])

    def attn_scores(u, lo, hi_):
        gi, qcols, nq = u["gi"], u["qcols"], u["nq"]
        N = 4 * nq
        for (k0, nk, vap, bias, negf) in u["kts"][lo:hi_]:
            ps_, bps = PS()
            mm(ps_[0:nk, 0:N], kT[:, k0:k0 + nk], actv[:, 8 + 4 * gi:12 + 4 * gi, qcols], True, True,
               [b_kT] + b_act[8 + 4 * gi:12 + 4 * gi], [bps])
            pi = 16 + st["pt"]; st["pt"] = (st["pt"] + 1) % 6
            ptv = actv[0:nk, pi, 0:N]
            if bias is None:
                act(ptv, ps_[0:nk, 0:N], AF.Exp, [bps], [b_act[pi]], scale=0.125)
            else:
                si = st["sb"]; st["sb"] = 1 - si
                sv = sbt[si][0:nk, 0:N]
                stt("dve", sv, ps_[0:nk, 0:N], 0.125, bias, ALU.mult, ALU.add, [bps, b_const], [b_sbt[si]])
                if negf:
                    ts("dve", sv, sv, flags[0:nk, 0:1], None, ALU.add, None, [b_sbt[si], b_const], [b_sbt[si]])
                act(ptv, sv, AF.Exp, [b_sbt[si]], [b_act[pi]])
            u["pts"].append((ptv, b_act[pi], nk, vap))

    def attn_pv(u):
        gi, nq, oi = u["gi"], u["nq"], u["oi"]
        R0 = 64 * gi
        N = 4 * nq
        pts = u["pts"]
        po, bpo = PS()
        for i, (ptv, bpt, nk, vap) in enumerate(pts):
            mm(po[:, 0:N], vap, ptv, i == 0, i == len(pts) - 1, [b_vt, bpt], [bpo])
        pd, bpd = PS()
        for i, (ptv, bpt, nk, vap) in enumerate(pts):
            mm(pd[:, 0:N], onesb[0:nk, :], ptv, i == 0, False, [b_const, bpt], [bpd])
        mm(pd[:, 0:N], onesb[0:1, :], u["sap"], False, True, [b_const, b_esrow], [bpd])
        act(rden[R0:R0 + 64, 0:N], pd[R0:R0 + 64, 0:N], AF.Ln, [bpd], [b_rden])
        act(rden[R0:R0 + 64, 0:N], rden[R0:R0 + 64, 0:N], AF.Exp, [b_rden], [b_rden], scale=-1.0)
        tt("dve", otmps[oi][R0:R0 + 64, 0:N], po[R0:R0 + 64, 0:N], rden[R0:R0 + 64, 0:N], ALU.mult, [bpo, b_rden], [b_otmps[oi]])

    def attention_run(l, units, hooks):
        def split(u):
            return (len(u["kts"]) + 1) // 2
        pend = []
        if units:
            attn_scores(units[0], 0, len(units[0]["kts"]))
        for i, u in enumerate(units):
            nxt = units[i + 1] if i + 1 < len(units) else None
            if nxt is not None:
                attn_scores(nxt, 0, split(nxt))
            attn_pv(u)
            if nxt is not None:
                attn_scores(nxt, split(nxt), len(nxt["kts"]))
            if u["fin"] is not None:
                pend.append(u["fin"])
                if len(pend) > 1:
                    attn_finish(l, *pend.pop(0))
            if i in hooks:
                hooks.pop(i)()
        while pend:
            attn_finish(l, *pend.pop(0))
        for k in sorted(hooks):
            hooks[k]()

    def attn_finish(l, nq, out_cols, oi):
        otmp = otmps[oi]; b_otmp = b_otmps[oi]
        ov = otmp[:, 0:4 * nq].rearrange("p (g q) -> p g q", g=4)
        for g_ in range(4):
            tt("pool", actv[:, g_, 0:nq], ov[:, g_, :], ov[:, g_, :], ALU.mult, [b_otmp], [b_act[g_]])
        pt_, bp = PS()
        for g_ in range(4):
            mm(pt_[:, 0:nq], onesb[:, :], actv[:, g_, 0:nq], g_ == 0, g_ == 3, [b_const, b_act[g_]], [bp])
        act(lnout[:, 0:nq], pt_[:, 0:nq], AF.Ln, [bp, b_const], [b_lnout], scale=1.0 / 512.0, bias=epsb[:, 0:1])
        act(rstt[1][:, 0:nq], lnout[:, 0:nq], AF.Exp, [b_lnout], [b_rst[1]], scale=-0.5)
        for g_ in range(4):
            col = cfg.GB + 8 * l + g_
            stt("dve", onv[:, g_, out_cols], ov[:, g_, :], gn[:, col:col + 1], rstt[1][:, 0:nq], ALU.mult, ALU.mult,
                [b_otmp, b_rst[1], b_const], [b_on[g_]])

    def pass_b(l, g, last):
        gid = g["gid"]; nreal = g["nreal"]; nmeta = g["nmeta"]; nh = g["nh"]
        ntr = nreal // 128
        nnt = ntr + 1
        nblk_h = HS // 128
        for ci, c in enumerate(g["chunks"]):
            n = c["n"]; g0 = c["g0"]
            cast_some(l + 1, 1)
            hi = load_h(c)
            norm_h(hi, n, cfg.GM + 8 * l)
            wb = 6 * NF
            ycv = yct[:, :].rearrange("p (c t) -> p c t", c=2)
            accs = [PS() for _ in range(4)]
            for nt in range(nnt):
                nr = 128 if nt < ntr else nmeta
                flat, br_ = ring_load(dft_d[gid][ci, nt][0:nr, :, 0:n],
                                      lambda s: s.rearrange("p (a k) -> p a k", a=2)[0:nr, :, 0:n], None)
                tv_ = flat.rearrange("p (a k) -> p a k", a=2)
                for cc in range(2):
                    for ab in range(2):
                        pa_, bpa_ = accs[ab * 2 + cc]
                        mm(pa_[:, :n], uftv[0:nr, nt, cc * 128:(cc + 1) * 128], tv_[0:nr, ab, 0:n], nt == 0, nt == nnt - 1,
                           [b_uft, br_], [bpa_])
            if c["kind"] == "real":
                segs = [(cbase(c["hloc"]) + 16 + c["coff"], 0, n)]
            else:
                segs = [(cbase(hl), hl * 16, 16) for hl in range(nh)]
            for cc in range(2):
                py, bpy = PS()
                for (p0, o0, ns) in segs:
                    for j in range(CW):
                        mm(py[:, o0:o0 + ns], dgv[:, cc, j, :], cbv[:, cc, p0 + j:p0 + j + ns], j == 0, j == CW - 1,
                           [b_dg[cc], b_cbuf], [bpy])
                col = cfg.CBD + 2 * l + cc
                ts("dve", ycv[:, cc, :n], py[:, :n], gn[:, col:col + 1], None, ALU.add, None, [bpy, b_const], [b_yc])
                cp("act", actv[:, cc, :n], ycv[:, cc, :n], [b_yc], [b_act[cc]])
            pfs = []
            for cc in range(2):
                cp("act", actv[:, 16 + cc, :n], accs[cc][0][:, :n], [accs[cc][1]], [b_act[16 + cc]])
                cp("dve", actv[:, 18 + cc, :n], accs[2 + cc][0][:, :n], [accs[2 + cc][1]], [b_act[18 + cc]])
            for cc in range(2):
                pf_, bpf_ = PS()
                mm(pf_[:, :n], c64v[:, cc, 0, :], actv[:, 16 + cc, :n], True, False, [b_const, b_act[16 + cc]], [bpf_])
                mm(pf_[:, :n], c64v[:, cc, 1, :], actv[:, 18 + cc, :n], False, True, [b_const, b_act[18 + cc]], [bpf_])
                pfs.append((pf_, bpf_))
            pm, bpm = PS()
            for cc in range(2):
                mm(pm[:, :n], onesb[:, :], actv[:, cc, :n], cc == 0, cc == 1, [b_const, b_act[cc]], [bpm])
            for j in range(4):
                tq, bq = wtile(l, wb + j)
                pq, bpq = PS()
                for kc in range(8):
                    mm(pq[:, :n], tq[:, kc, :], HV()[:, kc, :n], kc == 0, kc == 7, [bq, HB()[kc]], [bpq])
                P.add("pool", (lambda o: (lambda e: e.memset(o, 0.0)))(actv[64:128, 8 + j, :n]), [], [b_act[8 + j]])
                P.add("pool", (lambda o: (lambda e: e.memset(o, 0.0)))(actv[0:64, 12 + j, :n]), [], [b_act[12 + j]])
                cp("act", actv[0:64, 8 + j, :n], pq[0:64, :n], [bpq], [b_act[8 + j]])
                cp("dve", actv[64:128, 12 + j, :n], pq[64:128, :n], [bpq], [b_act[12 + j]])
            stats([(pfs[cc][0][:, :n], pfs[cc][1], True) for cc in range(2)], n, 256.0, 1, s0=2)
            for cc in range(2):
                col = cfg.GB + 8 * l + 4 + cc
                stt("dve", onv[:, 4 + cc, :n], pfs[cc][0][:, :n], gn[:, col:col + 1], rstt[1][:, :n], ALU.mult, ALU.mult,
                    [pfs[cc][1], b_rst[1], b_const], [b_on[4 + cc]])
            for cc in range(2):
                stt("dve", ycv[:, cc, :n], pm[:, :n], -1.0 / 256.0, ycv[:, cc, :n], ALU.mult, ALU.add, [bpm, b_yc], [b_yc])
            stats_pre([(ycv[:, cc, :n], b_yc, False) for cc in range(2)], n, s0=4)

            def conv_stage2(l=l, n=n, ycv=ycv):
                stats_post(2, n, 256.0, 2, s0=4)
                for cc in range(2):
                    tt("dve", ycv[:, cc, :n], ycv[:, cc, :n], rstt[2][:, :n], ALU.mult, [b_yc, b_rst[2]], [b_yc])
                    cg, cb_ = cfg.CLG + 2 * l + cc, cfg.CLB + 2 * l + cc
                    ts("dve", ycv[:, cc, :n], ycv[:, cc, :n], gn[:, cg:cg + 1], gn[:, cb_:cb_ + 1], ALU.mult, ALU.add,
                       [b_yc, b_const], [b_yc])
                    act(ycv[:, cc, :n], ycv[:, cc, :n], AF.Silu, [b_yc], [b_yc])
                stats_pre([(ycv[:, cc, :n], b_yc, False) for cc in range(2)], n, s0=6)

            def conv_stage3(l=l, n=n, ycv=ycv):
                stats_post(2, n, 256.0, 2, s0=6)
                for cc in range(2):
                    col = cfg.GB + 8 * l + 6 + cc
                    stt("dve", onv[:, 6 + cc, :n], ycv[:, cc, :n], gn[:, col:col + 1], rstt[2][:, :n], ALU.mult, ALU.mult,
                        [b_yc, b_rst[2], b_const], [b_on[6 + cc]])
            mk0 = nreal
            units = []
            if c["kind"] == "real":
                hloc = c["hloc"]
                for blk in range(4):
                    bi = c["coff"] // 128 + blk
                    gb = g0 + blk * 128
                    qcols = slice(blk * 128, blk * 128 + 128)
                    for gi in range(2):
                        kts = []
                        if gid == 0:
                            mb = mtab[0:32, hloc * 512:(hloc + 1) * 512]
                        else:
                            mb = None
                        kts.append((mk0, nmeta, vtv[0:nmeta, ntr, :], mb, False))
                        if bi > 0:
                            kts.append((gb - 128, 128, vtv[:, gb // 128 - 1, :], btv[:, gi * 3 + 0, :], False))
                        elif hloc == 1:
                            kts.append((gb - 128, 128, vtv[:, gb // 128 - 1, :], btv[:, gi * 3 + 0, :], True))
                        kts.append((gb, 128, vtv[:, gb // 128, :], btv[:, gi * 3 + 1, :], False))
                        if bi < nblk_h - 1:
                            kts.append((gb + 128, 128, vtv[:, gb // 128 + 1, :], btv[:, gi * 3 + 2, :], False))
                        elif hloc == 0 and nh == 2:
                            kts.append((gb + 128, 128, vtv[:, gb // 128 + 1, :], btv[:, gi * 3 + 2, :], True))
                        units.append(dict(gi=gi, qcols=qcols, nq=128, kts=kts, oi=blk % 2, pts=[],
                                          sap=esrow[0:1, gi * 512:(gi + 1) * 512],
                                          fin=(128, qcols, blk % 2) if gi == 1 else None))
            else:
                for hl in range(nh):
                    qcols = slice(hl * 16, hl * 16 + 16)
                    for gi in range(2):
                        kts = []
                        mb = mmb[0:32, hl * 64:(hl + 1) * 64] if gid == 0 else None
                        kts.append((mk0, nmeta, vtv[0:nmeta, ntr, :], mb, False))
                        kts.append((hl * HS, 128, vtv[:, hl * HS // 128, :], mqb[:, gi * 64:(gi + 1) * 64], False))
                        sap = esrow[0:1, gi * 512:(gi + 1) * 512].rearrange("p (g q) -> p g q", g=4)[:, :, 0:16]
                        units.append(dict(gi=gi, qcols=qcols, nq=16, kts=kts, oi=hl % 2, pts=[], sap=sap,
                                          fin=(16, qcols, hl % 2) if gi == 1 else None))
            attention_run(l, units, {0: conv_stage2, 2: conv_stage3})
            for dc in range(8):
                two, bwo = wtile(l, wb + 12 + dc)
                pw, bpw = PS()
                for fc in range(8):
                    mm(pw[:, :n], two[:, fc, :], onv[:, fc, :n], fc == 0, fc == 7, [bwo, b_on[fc]], [bpw])
                tt("dve", hbv[hi][:, dc, :n], pw[:, :n], hbv[hi][:, dc, :n], ALU.add, [bpw, b_hb[hi][dc]], [b_hb[hi][dc]])
            ffn(l, 1, hi, n)
            if not last:
                store_h(c, hi)
            elif c["kind"] == "real":
                stats([(hbv[hi][:, kc, :n], b_hb[hi][kc], False) for kc in range(8)], n, float(D), 0)
                for kc in range(8):
                    stt("dve", hbv[hi][:, kc, :n], hbv[hi][:, kc, :n], gn[:, cfg.GF + kc:cfg.GF + kc + 1], rstt[0][:, :n],
                        ALU.mult, ALU.mult, [b_hb[hi][kc], b_rst[0], b_const], [b_hb[hi][kc]])
                for t_ in range(n // 128):
                    oi = t_ % 2
                    for hf in range(2):
                        pt_, bp = PS()
                        for j in range(4):
                            kc = hf * 4 + j
                            P.add("pe", (lambda o, i_, idn: (lambda e: e.transpose(o, i_, idn)))(
                                pt_[:, j * 128:(j + 1) * 128], hbv[hi][:, kc, t_ * 128:(t_ + 1) * 128], identf[:, :]),
                                [b_hb[hi][kc], b_const], [bp])
                        cp("act" if hf == 0 else "dve", ost[:, oi * 1024 + hf * 512:oi * 1024 + hf * 512 + 512], pt_[:, :],
                           [bp], [b_ost[oi]] + (b_sbt if oi == 0 else b_otmps))
                    r0 = c["s0"] + t_ * 128
                    dma("pool", "out%d" % oi, yout[r0:r0 + 128, :], ost[:, oi * 1024:(oi + 1) * 1024], [b_ost[oi]] + (b_sbt if oi == 0 else b_otmps), [])

    for l in range(L):
        layer_consts(l)
        for g in groups:
            pass_a(l, g)
            pass_b(l, g, l == L - 1)
        cast_some(l + 1, 1000)

    P.emit(nc, es)
    es.close()
    return nc


_CACHE = {}


def run(cfg, inputs):
    in_maps = host_prep(cfg, inputs)
    key = (cfg.L, cfg.HS, cfg.DFF)
    if key not in _CACHE:
        _CACHE[key] = build_program(cfg)
    nc = _CACHE[key]
    res = run_bass_kernel_spmd(nc, in_maps, core_ids=list(range(NCORES)))
    HS = cfg.HS
    nb_p = 16
    yp = np.empty((nb_p, HS, D), np.float32)
    ys = np.empty((4, 2 * HS, D), np.float32)
    for c in range(NCORES):
        y = np.asarray(res.results[c]["yout"], np.float32)
        if c < 4:
            for i in range(3):
                yp[3 * c + i] = y[i * HS:(i + 1) * HS]
        else:
            s_ = c - 4
            ys[s_] = y[0:2 * HS]
            yp[12 + s_] = y[2 * HS:3 * HS]
    return yp, ys


def kernel(**inputs):
    cfg = Cfg(L=4, HS=2048, DFF=2816)
    return run(cfg, inputs)
```

```python
import math
from contextlib import ExitStack

import numpy as np
import ml_dtypes

import concourse.bass as bass
import concourse.mybir as mybir
from concourse.bass_utils import run_bass_kernel_spmd

F32 = mybir.dt.float32
BF16 = mybir.dt.bfloat16
AF = mybir.ActivationFunctionType
ALU = mybir.AluOpType
NPBF = ml_dtypes.bfloat16

D = 1024
NM = 16
EPS = 1e-6
NEG = -1e30
CW = 31
NCORES = 8


class Cfg:
    def __init__(s, L=4, HS=2048, DFF=2816):
        s.L, s.HS, s.DFF = L, HS, DFF
        s.NF = DFF // 128
        assert s.NF % 2 == 0 and HS % 512 == 0
        s.TR = 3 * HS
        s.T = s.TR + 3 * NM
        s.NTL = 6 * s.NF + 20
        s.NKC0, s.NNT0 = 2 * HS // 512 + 1, 2 * HS // 128 + 1
        s.NKC1, s.NNT1 = HS // 512 + 1, HS // 128 + 1
        L8 = L * 8
        s.G1, s.GM, s.G2, s.GF, s.GB = 0, L8, 2 * L8, 3 * L8, 3 * L8 + 8
        s.CBD = s.GB + L8
        s.CLG = s.CBD + 2 * L
        s.CLB = s.CLG + 2 * L
        s.CWD = s.CLB + 2 * L
        s.NGC = s.CWD + 2 * L * CW
        s.CB = 2 * (HS + 46)
        s.NSLOT = max(s.NF, 22)


def _cols(v):
    return np.ascontiguousarray(np.asarray(v, np.float32).reshape(-1, 128).T)


def _dft_table(cfg, pos, seq, Ls, nkc, nnt):
    n = len(pos)
    pp = np.zeros(nnt * 128, np.int64); ss = np.full(nnt * 128, -1, np.int64)
    pp[:n] = pos; ss[:n] = seq
    pk = np.zeros(nkc * 512, np.int64); sk = np.full(nkc * 512, -2, np.int64)
    pk[:n] = pos; sk[:n] = seq
    out = np.empty((nkc, nnt, 128, 2, 512), NPBF)
    for kc in range(nkc):
        k_p = pk[kc * 512:(kc + 1) * 512]; k_s = sk[kc * 512:(kc + 1) * 512]
        r = (pp[:, None] * k_p[None, :]) % Ls
        ang = (2.0 * np.pi / Ls) * r.astype(np.float64)
        same = ((ss[:, None] == k_s[None, :]) & (ss[:, None] >= 0)).astype(np.float64) / math.sqrt(Ls)
        c = (np.cos(ang) * same).reshape(nnt, 128, 512)
        s_ = (np.sin(ang) * same).reshape(nnt, 128, 512)
        out[kc, :, :, 0, :] = c.astype(NPBF)
        out[kc, :, :, 1, :] = s_.astype(NPBF)
    return out


def host_prep(cfg, inp):
    L, HS, NF = cfg.L, cfg.HS, cfg.NF
    f32 = np.float32
    xp = np.asarray(inp["x_prompt"], f32); xs = np.asarray(inp["x_sample"], f32)
    meta = np.asarray(inp["meta_tokens"], f32)
    wt = np.empty((L * cfg.NTL, 128, 1024), f32)
    qperm = np.concatenate([np.r_[64 * j:64 * j + 64, 256 + 64 * j:256 + 64 * j + 64] for j in range(4)])
    for l in range(L):
        base = l * cfg.NTL
        for wi, (g, u, d) in enumerate((("ffn1_w_gate", "ffn1_w_up", "ffn1_w_down"),
                                        ("ffn2_w_gate", "ffn2_w_up", "ffn2_w_down"))):
            o = base + wi * 3 * NF
            wg = np.asarray(inp[g][l], f32).reshape(8, 128, NF, 128).transpose(2, 1, 0, 3)
            wu = np.asarray(inp[u][l], f32).reshape(8, 128, NF, 128).transpose(2, 1, 0, 3)
            wt[o:o + 2 * NF:2] = wg.reshape(NF, 128, 1024)
            wt[o + 1:o + 2 * NF:2] = wu.reshape(NF, 128, 1024)
            wd = np.asarray(inp[d][l], f32).reshape(NF // 2, 2, 128, 2, 4, 128).transpose(3, 0, 2, 1, 4, 5)
            wt[o + 2 * NF:o + 3 * NF] = wd.reshape(NF, 128, 1024)
        win = np.asarray(inp["w_in"][l], f32)
        cols = np.concatenate([qperm, np.arange(512, 1024),
                               np.r_[1024:1152], np.r_[1280:1408],
                               np.r_[1152:1280], np.r_[1408:1536]])
        wp = win[:, cols].reshape(8, 128, 12, 128).transpose(2, 1, 0, 3)
        wt[base + 6 * NF:base + 6 * NF + 12] = wp.reshape(12, 128, 1024)
        wo = np.asarray(inp["w_out"][l], f32)
        rows = np.concatenate([qperm, np.arange(512, 1024)])
        wop = wo[rows].reshape(8, 128, 8, 128).transpose(2, 1, 0, 3)
        wt[base + 6 * NF + 12:base + 6 * NF + 20] = wop.reshape(8, 128, 1024)
    gn = np.zeros((128, cfg.NGC), f32)
    for l in range(L):
        gn[:, cfg.G1 + 8 * l:cfg.G1 + 8 * l + 8] = _cols(inp["ffn1_norm"][l])
        gn[:, cfg.GM + 8 * l:cfg.GM + 8 * l + 8] = _cols(inp["mix_norm"][l])
        gn[:, cfg.G2 + 8 * l:cfg.G2 + 8 * l + 8] = _cols(inp["ffn2_norm"][l])
        br = np.asarray(inp["branch_norm"][l], f32)
        gn[:, cfg.GB + 8 * l:cfg.GB + 8 * l + 8] = _cols(br[rows])
        gn[:, cfg.CBD + 2 * l:cfg.CBD + 2 * l + 2] = _cols(inp["conv_b_dw"][l])
        gn[:, cfg.CLG + 2 * l:cfg.CLG + 2 * l + 2] = _cols(inp["conv_ln_g"][l])
        gn[:, cfg.CLB + 2 * l:cfg.CLB + 2 * l + 2] = _cols(inp["conv_ln_b"][l])
        w = np.asarray(inp["conv_w_dw"][l], f32)
        for cc in range(2):
            o = cfg.CWD + (2 * l + cc) * CW
            gn[:, o:o + CW] = w[:, cc * 128:(cc + 1) * 128].T
    gn[:, cfg.GF:cfg.GF + 8] = _cols(inp["final_norm"])
    sink = np.asarray(inp["attn_sink"], f32)
    sinkx = np.ascontiguousarray(np.repeat(sink.reshape(L, 2, 4, 1), 128, axis=3)).reshape(1, L * 1024)
    slopes = 2.0 ** (-(np.arange(8) + 1.0))
    ii = np.arange(128)[None, :]; jj = np.arange(128)[:, None]
    btab = np.zeros((128, 6, 4, 128), f32)
    for gi in range(2):
        for g in range(4):
            s = slopes[4 * gi + g]
            dp = 128 + ii - jj
            btab[:, gi * 3 + 0, g] = np.where(jj >= ii, -s * dp, NEG)
            btab[:, gi * 3 + 1, g] = -s * np.abs(ii - jj)
            dn = 128 + jj - ii
            btab[:, gi * 3 + 2, g] = np.where(jj <= ii, -s * dn, NEG)
    btab = btab.reshape(128, 6 * 512).astype(NPBF)
    mqb = np.zeros((128, 2, 4, 16), f32)
    tt = np.arange(128)[:, None]; mm = np.arange(16)[None, :]
    for gi in range(2):
        for g in range(4):
            dist = 16 + tt - mm
            mqb[:, gi, g] = np.where(dist <= 128, -slopes[4 * gi + g] * dist, NEG)
    mqb = mqb.reshape(128, 128)
    mmb = np.full((32, 2, 64), NEG, f32)
    mmb[0:16, 0] = 0.0; mmb[16:32, 1] = 0.0
    mmb = mmb.reshape(32, 128)
    c64 = np.zeros((128, 2, 2, 128), np.float64)
    a = np.arange(128)
    ang = 2 * np.pi * ((a[:, None] % 64) * (a[None, :] % 64) % 64) / 64.0
    blk = ((a[:, None] // 64) == (a[None, :] // 64)) / 8.0
    for cc in range(2):
        c64[:, cc, 0] = np.cos(ang) * blk
        c64[:, cc, 1] = -np.sin(ang) * blk
    c64 = c64.reshape(128, 512).astype(NPBF)
    identf = np.eye(128, dtype=f32)

    def dft_for(kind):
        if kind == "P":
            pos = np.concatenate([16 + np.arange(HS), 16 + np.arange(HS), np.arange(16), np.arange(16)])
            seq = np.concatenate([np.zeros(HS), np.ones(HS), np.zeros(16), np.ones(16)]).astype(np.int64)
            return _dft_table(cfg, pos, seq, HS + 16, cfg.NKC0, cfg.NNT0)
        pos = np.concatenate([16 + np.arange(2 * HS), np.arange(16), np.zeros(16, np.int64)])
        seq = np.concatenate([np.zeros(2 * HS + 16), -np.ones(16)]).astype(np.int64)
        return _dft_table(cfg, pos, seq, 2 * HS + 16, cfg.NKC0, cfg.NNT0)

    dftP, dftS = dft_for("P"), dft_for("S")
    pos1 = np.concatenate([16 + np.arange(HS), np.arange(16)])
    dft1 = _dft_table(cfg, pos1, np.zeros(HS + 16, np.int64), HS + 16, cfg.NKC1, cfg.NNT1)

    def mtab_for(kind):
        m = np.full((32, 2, 512), NEG, f32)
        m[0:16, 0] = 0.0
        if kind == "P":
            m[16:32, 1] = 0.0
        else:
            m[0:16, 1] = 0.0
        return m.reshape(32, 1024)

    in_maps = []
    for c in range(NCORES):
        kind = "P" if c < 4 else "S"
        xin = np.empty((cfg.T, D), f32)
        if kind == "P":
            for i in range(3):
                xin[i * HS:(i + 1) * HS] = xp[3 * c + i]
            mrows = [meta, meta, meta]
        else:
            s_ = c - 4
            xin[0:2 * HS] = xs[s_]
            xin[2 * HS:3 * HS] = xp[12 + s_]
            mrows = [meta, np.zeros_like(meta), meta]
        for i in range(3):
            xin[cfg.TR + 16 * i:cfg.TR + 16 * i + 16] = mrows[i]
        flags = np.zeros((128, 4), f32)
        flags[:, 0] = NEG if kind == "P" else 0.0
        flags[:, 1] = 0.0 if kind == "P" else 1.0
        flags[:, 2] = 1.0 if kind == "P" else 0.0
        in_maps.append(dict(xin=xin, wt=wt, gn=gn, sinkx=sinkx, btab=btab, mtab=mtab_for(kind), mqb=mqb,
                            mmb=mmb, c64=c64, identf=identf, flags=flags,
                            dft0=dftP if kind == "P" else dftS, dft1=dft1))
    return in_maps


class Buf:
    __slots__ = ("name", "lw", "rd")

    def __init__(s, name):
        s.name, s.lw, s.rd = name, None, {}


class Op:
    __slots__ = ("eng", "fn", "deps", "dmaq", "dmaval", "need_inc", "incval")


class Prog:
    ENGS = ("pe", "act", "dve", "pool", "sp")

    def __init__(s):
        s.ops = {e: [] for e in s.ENGS}
        s.ndma = {}
        s.dmaeng = {}

    def add(s, eng, fn, R=(), W=(), dma=None):
        op = Op()
        op.eng, op.fn, op.need_inc, op.incval = eng, fn, False, 0
        op.dmaq = dma
        if dma:
            s.ndma[dma] = s.ndma.get(dma, 0) + 1
            assert s.dmaeng.setdefault(dma, eng) == eng
            op.dmaval = 16 * s.ndma[dma]
        deps = {}

        def need(p):
            if p is None:
                return
            if p.dmaq is None and p.eng == "pe" and eng == "pe" and not dma:
                return
            key = ("d", p.dmaq) if p.dmaq else ("e", p.eng)
            cur = deps.get(key)
            if cur is None or s._later(p, cur):
                deps[key] = p

        for b in R:
            need(b.lw)
        for b in W:
            need(b.lw)
            for r in b.rd.values():
                need(r)
        op.deps = list(deps.values())
        for p in op.deps:
            if p.dmaq is None:
                p.need_inc = True
        op_key = ("d", dma) if dma else ("e", eng)
        for b in R:
            b.rd[op_key] = op
        for b in W:
            b.lw = op; b.rd = {}
        op.incval = len(s.ops[eng])
        s.ops[eng].append(op)
        return op

    @staticmethod
    def _later(a, b):
        if a.dmaq:
            return a.dmaval > b.dmaval
        return a.incval > b.incval

    def emit(s, nc, es):
        esem = {e: es.enter_context(nc.semaphore("sem_" + e)) for e in s.ENGS}
        dsem = {q: es.enter_context(nc.semaphore("dsem_" + q)) for q in s.ndma}
        for e in s.ENGS:
            c = 0
            for op in s.ops[e]:
                if op.dmaq is None and op.need_inc:
                    c += 1; op.incval = c
                else:
                    op.incval = -1
            assert c < 60000, (e, c)
        for q in s.ndma:
            assert 16 * s.ndma[q] < 60000, (q, s.ndma[q])
        block = es.enter_context(nc.Block())

        def body(ename):
            def run(eng):
                waited = {}
                for op in s.ops[ename]:
                    for p in op.deps:
                        if p.dmaq:
                            key = ("d", p.dmaq); sem = dsem[p.dmaq]; val = p.dmaval
                        else:
                            key = ("e", p.eng); sem = esem[p.eng]; val = p.incval
                        if waited.get(key, 0) < val:
                            eng.wait_ge(sem, val); waited[key] = val
                    ins = op.fn(eng)
                    if op.dmaq:
                        ins.then_inc(dsem[op.dmaq], 16)
                    elif op.need_inc:
                        ins.then_inc(esem[ename], 1)
                for q in s.ndma:
                    if s.dmaeng[q] == ename:
                        eng.wait_ge(dsem[q], 16 * s.ndma[q])
            return run

        block.sync(body("sp")); block.gpsimd(body("pool")); block.vector(body("dve"))
        block.tensor(body("pe")); block.scalar(body("act"))


def build_program(cfg):
    L, HS, NF = cfg.L, cfg.HS, cfg.NF
    nc = bass.Bass("TRN2", target_bir_lowering=False)
    P = Prog()

    def DT(name, shape, dt, kind):
        return nc.dram_tensor(name, list(shape), dt, kind=kind).ap()

    xin = DT("xin", [cfg.T, D], F32, "ExternalInput")
    wt = DT("wt", [L * cfg.NTL, 128, 1024], F32, "ExternalInput")
    gn_d = DT("gn", [128, cfg.NGC], F32, "ExternalInput")
    sinkx_d = DT("sinkx", [1, L * 1024], F32, "ExternalInput")
    btab_d = DT("btab", [128, 6 * 512], BF16, "ExternalInput")
    mtab_d = DT("mtab", [32, 1024], F32, "ExternalInput")
    mqb_d = DT("mqb", [128, 128], F32, "ExternalInput")
    mmb_d = DT("mmb", [32, 128], F32, "ExternalInput")
    c64_d = DT("c64", [128, 512], BF16, "ExternalInput")
    identf_d = DT("identf", [128, 128], F32, "ExternalInput")
    flags_d = DT("flags", [128, 4], F32, "ExternalInput")
    dft_d = [DT("dft0", [cfg.NKC0, cfg.NNT0, 128, 2, 512], BF16, "ExternalInput"),
             DT("dft1", [cfg.NKC1, cfg.NNT1, 128, 2, 512], BF16, "ExternalInput")]
    yout = DT("yout", [cfg.TR, D], F32, "ExternalOutput")
    wtb = DT("wtb", [L * cfg.NTL, 128, 1024], BF16, "Internal")
    hT = DT("hT", [128, 8, cfg.T], F32, "Internal")

    es = ExitStack()

    def SB(name, cols, dt, parts=128):
        return es.enter_context(nc.sbuf_tensor("sb_" + name, [parts, cols], dt))

    identf = SB("identf", 128, F32); identb = SB("identb", 128, BF16); onesb = SB("onesb", 128, BF16)
    gn = SB("gn", cfg.NGC, F32); flags = SB("flags", 4, F32); epsb = SB("epsb", 1, F32)
    btab = SB("btab", 6 * 512, BF16); mtab = SB("mtab", 1024, F32); mqb = SB("mqb", 128, F32); mmb = SB("mmb", 128, F32)
    c64 = SB("c64", 512, BF16)
    esrow = SB("esrow", 1024, BF16)
    dg = SB("dg", 2 * CW * 128, BF16)
    NT0 = cfg.NNT0
    TG0 = 2 * HS + 32
    kT = SB("kT", TG0, BF16); vt = SB("vt", NT0 * 128, BF16); uft = SB("uft", NT0 * 256, BF16)
    cbuf = SB("cbuf", 2 * cfg.CB, BF16)
    hbt = [SB("hb%d" % i, 8 * 512, F32) for i in range(2)]
    hnt = SB("hn", 8 * 512, BF16); hnt2 = SB("hn2", 8 * 512, BF16); sqt = SB("sqr", 3 * 512, BF16)
    actt = SB("act", cfg.NSLOT * 512, BF16)
    ont = SB("on", 8 * 512, BF16)
    sgt = SB("sg", 2 * 512, BF16)
    lnout = SB("lnout", 512, F32)
    rstt = [SB("rst%d" % i, 512, F32) for i in range(3)]
    misc = SB("misc", 4 * 512, F32)
    sbt = [misc[:, 0:512], misc[:, 512:1024]]
    otmps = [misc[:, 1024:1536], misc[:, 1536:2048]]; rden = SB("rden", 512, F32)
    yct = SB("yct", 2 * 512, F32); ost = misc; sinkst = yct
    ringt = SB("ring", 8 * 1024, BF16)
    pst = [es.enter_context(nc.psum_tensor("ps%d" % i, [128, 512], F32)) for i in range(8)]

    B = Buf
    b_const = B("const"); b_dg = [B("dg0"), B("dg1")]; b_esrow = B("esrow")
    b_kT, b_vt, b_uft, b_cbuf = B("kT"), B("vt"), B("uft"), B("cbuf")
    b_hb = [[B("hb%d_%d" % (i, k)) for k in range(8)] for i in range(2)]
    b_hn_l = [[B("hn%d_%d" % (j, k)) for k in range(8)] for j in range(2)]
    b_sq = [B("sq%d" % k) for k in range(3)]
    b_act = [B("act%d" % k) for k in range(cfg.NSLOT)]
    b_on = [B("on%d" % k) for k in range(8)]
    b_sg = [B("sg0"), B("sg1")]; b_lnout = B("lnout"); b_rst = [B("rst%d" % i) for i in range(3)]
    b_sbt = [B("sbt0"), B("sbt1")]; b_otmps = [B("otmp0"), B("otmp1")]; b_rden = B("rden"); b_yc = B("yc")
    b_ost = [B("ost0"), B("ost1")]
    b_ring = [B("ring%d" % i) for i in range(8)]
    b_ps = [B("ps%d" % i) for i in range(8)]
    b_hTt = [B("hT%d" % i) for i in range((cfg.T + 127) // 128)]; b_wtb = [B("wtb%d" % l) for l in range(L)]

    hbv = [t[:, :].rearrange("p (c t) -> p c t", c=8) for t in hbt]
    hnv_l = [hnt[:, :].rearrange("p (c t) -> p c t", c=8), hnt2[:, :].rearrange("p (c t) -> p c t", c=8)]
    sqv = sqt[:, :].rearrange("p (c t) -> p c t", c=3)
    sel = {"hn": 0}

    def HV():
        return hnv_l[sel["hn"]]

    def HB():
        return b_hn_l[sel["hn"]]
    actv = actt[:, :].rearrange("p (c t) -> p c t", t=512)
    onv = ont[:, :].rearrange("p (c t) -> p c t", c=8)
    vtv = vt[:, :].rearrange("p (n c) -> p n c", c=128)
    uftv = uft[:, :].rearrange("p (n c) -> p n c", c=256)
    cbv = cbuf[:, :].rearrange("p (c t) -> p c t", c=2)
    dgv = dg[:, :].rearrange("p (c j m) -> p c j m", c=2, j=CW)
    btv = btab[:, :].rearrange("p (k t) -> p k t", k=6)
    c64v = c64[:, :].rearrange("p (c s m) -> p c s m", c=2, s=2)
    ringv = [ringt[:, i * 1024:(i + 1) * 1024] for i in range(8)]

    def mm(out, lhsT, rhs, start, stop, R, W):
        P.add("pe", lambda e: e.matmul(out, lhsT=lhsT, rhs=rhs, start=start, stop=stop), R, W)

    def act(out, in_, func, R, W, scale=1.0, bias=None):
        if bias is None:
            P.add("act", lambda e: e.activation(out=out, in_=in_, func=func, scale=scale), R, W)
        else:
            P.add("act", lambda e: e.activation(out=out, in_=in_, func=func, scale=scale, bias=bias), R, W)

    def tt(eng, out, in0, in1, op, R, W):
        P.add(eng, lambda e: e.tensor_tensor(out=out, in0=in0, in1=in1, op=op), R, W)

    def ts(eng, out, in0, s1, s2, op0, op1, R, W):
        if s2 is None:
            P.add(eng, lambda e: e.tensor_scalar(out=out, in0=in0, scalar1=s1, scalar2=None, op0=op0), R, W)
        else:
            P.add(eng, lambda e: e.tensor_scalar(out=out, in0=in0, scalar1=s1, scalar2=s2, op0=op0, op1=op1), R, W)

    def stt(eng, out, in0, scalar, in1, op0, op1, R, W):
        P.add(eng, lambda e: e.scalar_tensor_tensor(out=out, in0=in0, scalar=scalar, in1=in1, op0=op0, op1=op1), R, W)

    def cp(eng, out, in_, R, W):
        if eng == "act":
            P.add("act", lambda e: e.activation(out=out, in_=in_, func=AF.Copy), R, W)
        else:
            P.add(eng, lambda e: e.tensor_copy(out=out, in_=in_), R, W)

    def dma(q, key, out, in_, R, W):
        P.add(q, lambda e: e.dma_start(out=out, in_=in_), R, W, dma=key)

    st = {"ps": 0, "ring": 0, "sg": 0, "sb": 0, "pt": 0, "hb": 0, "sq": 0}

    def PS():
        i = st["ps"]; st["ps"] = (i + 1) % 8
        return pst[i], b_ps[i]

    def ring_load(src, view, l, idx=None):
        i = st["ring"]; st["ring"] = (i + 1) % 8
        if l is None:
            rb = []
        elif l == 0:
            rb = [b_wtb0[idx // CH]]
        else:
            rb = [b_wtb[l]]
        dma("sp", "ring%d" % i, view(ringv[i]), src, rb, [b_ring[i]])
        return ringv[i], b_ring[i]

    def wtile(l, idx):
        flat, b = ring_load(wtb[l * cfg.NTL + idx], lambda s: s, l, idx)
        return flat.rearrange("p (c m) -> p c m", c=8), b

    for (dst, src) in ((identf, identf_d), (gn, gn_d), (flags, flags_d), (btab, btab_d), (mqb, mqb_d), (c64, c64_d)):
        dma("sp", "const", dst[:, :], src[:, :], [], [b_const])
    dma("sp", "const", mtab[0:32, :], mtab_d[:, :], [], [b_const])
    dma("sp", "const", mmb[0:32, :], mmb_d[:, :], [], [b_const])
    cp("dve", identb[:, :], identf[:, :], [b_const], [b_const])
    P.add("dve", lambda e: e.memset(onesb[:, :], 1.0), [], [b_const])
    P.add("dve", lambda e: e.memset(epsb[:, :], EPS), [], [b_const])
    P.add("dve", lambda e: e.memset(cbuf[:, :], 0.0), [], [b_cbuf])
    P.add("dve", lambda e: e.memset(actt[:, :], 0.0), [], b_act)
    P.add("pool", lambda e: e.memset(kT[:, :], 0.0), [], [b_kT])
    P.add("pool", lambda e: e.memset(vt[:, :], 0.0), [], [b_vt])
    P.add("pool", lambda e: e.memset(uft[:, :], 0.0), [], [b_uft])

    CH = 8
    cast_todo = {l: [(l * cfg.NTL + t0, l * cfg.NTL + min(cfg.NTL, t0 + CH)) for t0 in range(0, cfg.NTL, CH)] for l in range(L)}

    ncast0 = len(cast_todo[0])
    b_wtb0 = [B("wtb0_%d" % j) for j in range(ncast0)]

    def cast_some(l, k):
        for _ in range(k):
            if l < L and cast_todo[l]:
                a, b = cast_todo[l].pop(0)
                if l == 0:
                    j = ncast0 - len(cast_todo[0]) - 1
                    dma("pool", "c0_%d" % j, wtb[a:b].rearrange("a p c -> (a p) c"), wt[a:b].rearrange("a p c -> (a p) c"),
                        [], [b_wtb0[j]])
                else:
                    dma("pool", "cast%d" % l, wtb[a:b].rearrange("a p c -> (a p) c"), wt[a:b].rearrange("a p c -> (a p) c"),
                        [], [b_wtb[l]])

    cast_some(0, 2)
    ntile_in = (cfg.T + 127) // 128
    for ti in range(ntile_in):
        r0 = ti * 128; nr = min(128, cfg.T - r0)
        xi = ti % 2
        xst = hbt[0][:, xi * 1024:(xi + 1) * 1024]; bx = b_hb[0][xi]
        hst = hbt[1][:, xi * 1024:(xi + 1) * 1024].rearrange("p (c t) -> p c t", c=8); bh = b_hb[1][xi]
        dma("sp", "xl%d" % xi, xst[0:nr, :], xin[r0:r0 + nr, :], [], [bx])
        for hf in range(2):
            pt_, bp = PS()
            for j in range(4):
                kc = hf * 4 + j
                P.add("pe", (lambda o, i_, idn: (lambda e: e.transpose(o, i_, idn)))(
                    pt_[:, j * 128:j * 128 + nr], xst[0:nr, kc * 128:(kc + 1) * 128], identf[0:nr, 0:nr]),
                    [bx, b_const], [bp])
            cp("act" if hf == 0 else "dve", hst[:, hf * 4:hf * 4 + 4, 0:nr],
               pt_[:, :].rearrange("p (c t) -> p c t", c=4)[:, :, 0:nr], [bp], [bh])
        dma("pool", "xs%d" % xi, hT[:, :, r0:r0 + nr], hst[:, :, 0:nr], [bh], [b_hTt[ti]])
        cast_some(0, 1)
    cast_some(0, 1000)

    def stats(srcs, n, count, rst_i, s0=0):
        stats_pre(srcs, n, s0)
        stats_post(len(srcs), n, count, rst_i, s0)

    def stats_post(nsrc, n, count, rst_i, s0=0):
        pt_, bp = PS()
        for k in range(nsrc):
            mm(pt_[:, :n], onesb[:, :], actv[:, s0 + k, :n], k == 0, k == nsrc - 1, [b_const, b_act[s0 + k]], [bp])
        act(lnout[:, :n], pt_[:, :n], AF.Ln, [bp, b_const], [b_lnout], scale=1.0 / count, bias=epsb[:, 0:1])
        act(rstt[rst_i][:, :n], lnout[:, :n], AF.Exp, [b_lnout], [b_rst[rst_i]], scale=-0.5)

    def stats_pre(srcs, n, s0=0):
        for k, (ap, b, inps) in enumerate(srcs):
            if inps or k % 3 == 1:
                act(actv[:, s0 + k, :n], ap, AF.Square, [b], [b_act[s0 + k]])
            else:
                tt("pool" if k % 3 == 0 else "dve", actv[:, s0 + k, :n], ap, ap, ALU.mult, [b], [b_act[s0 + k]])

    def norm_h(hi, n, gcol, rst_i=0):
        pt_, bp = PS()
        for kc in range(8):
            qi = st["sq"]; st["sq"] = (qi + 1) % 3
            ap = hbv[hi][:, kc, :n]
            if kc % 3 == 1:
                act(sqv[:, qi, :n], ap, AF.Square, [b_hb[hi][kc]], [b_sq[qi]])
            else:
                tt("pool" if kc % 3 == 0 else "dve", sqv[:, qi, :n], ap, ap, ALU.mult, [b_hb[hi][kc]], [b_sq[qi]])
            mm(pt_[:, :n], onesb[:, :], sqv[:, qi, :n], kc == 0, kc == 7, [b_const, b_sq[qi]], [bp])
        act(lnout[:, :n], pt_[:, :n], AF.Ln, [bp, b_const], [b_lnout], scale=1.0 / float(D), bias=epsb[:, 0:1])
        act(rstt[rst_i][:, :n], lnout[:, :n], AF.Exp, [b_lnout], [b_rst[rst_i]], scale=-0.5)
        for kc in range(8):
            stt("dve", HV()[:, kc, :n], hbv[hi][:, kc, :n], gn[:, gcol + kc:gcol + kc + 1], rstt[rst_i][:, :n],
                ALU.mult, ALU.mult, [b_hb[hi][kc], b_rst[rst_i], b_const], [HB()[kc]])

    def ffn(l, which, hi, n):
        ffn_norm(l, which, hi, n)
        ffn_gateup(l, which, n)
        ffn_down(l, which, hi, n)

    def ffn_norm(l, which, hi, n):
        norm_h(hi, n, (cfg.G1 if which == 0 else cfg.G2) + 8 * l)

    def ffn_gateup(l, which, n):
        base = which * 3 * NF
        for f in range(NF):
            tg, bg = wtile(l, base + 2 * f)
            tu, bu = wtile(l, base + 2 * f + 1)
            pg, bpg = PS()
            for kc in range(8):
                mm(pg[:, :n], tg[:, kc, :], HV()[:, kc, :n], kc == 0, kc == 7, [bg, HB()[kc]], [bpg])
            pu, bpu = PS()
            for kc in range(8):
                mm(pu[:, :n], tu[:, kc, :], HV()[:, kc, :n], kc == 0, kc == 7, [bu, HB()[kc]], [bpu])
            si = st["sg"]; st["sg"] = 1 - si
            sgv = sgt[:, si * 512:si * 512 + n]
            act(sgv, pg[:, :n], AF.Silu, [bpg], [b_sg[si]])
            tt("dve", actv[:, f, :n], pu[:, :n], sgv, ALU.mult, [bpu, b_sg[si]], [b_act[f]])

    def ffn_down(l, which, hi, n):
        base = which * 3 * NF
        for half in range(2):
            acc = [PS() for _ in range(4)]
            for pair in range(NF // 2):
                td, bd = wtile(l, base + 2 * NF + half * (NF // 2) + pair)
                for m in range(2):
                    f = 2 * pair + m
                    for dcl in range(4):
                        mm(acc[dcl][0][:, :n], td[:, m * 4 + dcl, :], actv[:, f, :n], f == 0, f == NF - 1,
                           [bd, b_act[f]], [acc[dcl][1]])
            for dcl in range(4):
                dc = half * 4 + dcl
                stt("dve", hbv[hi][:, dc, :n], acc[dcl][0][:, :n], 0.5, hbv[hi][:, dc, :n], ALU.mult, ALU.add,
                    [acc[dcl][1], b_hb[hi][dc]], [b_hb[hi][dc]])

    groups = [dict(gid=0, halves=[0, 1], nreal=2 * HS, mslot=cfg.TR, nmeta=32),
              dict(gid=1, halves=[2], nreal=HS, mslot=cfg.TR + 32, nmeta=16)]
    for g in groups:
        ch = []
        for hi_, hh in enumerate(g["halves"]):
            for c in range(HS // 512):
                ch.append(dict(kind="real", hloc=hi_, g0=hi_ * HS + c * 512, n=512, s0=hh * HS + c * 512, coff=c * 512))
        ch.append(dict(kind="meta", g0=g["nreal"], n=g["nmeta"], s0=g["mslot"]))
        g["chunks"] = ch
        g["nh"] = len(g["halves"])

    def cbase(hloc):
        return hloc * (HS + 46)

    def hT_bufs(c):
        return b_hTt[c["s0"] // 128:(c["s0"] + c["n"] - 1) // 128 + 1]

    def load_h(c):
        hi = st["hb"]; st["hb"] = 1 - hi
        n = c["n"]
        dma("sp", "hb%d" % hi, hbv[hi][:, :, :n], hT[:, :, c["s0"]:c["s0"] + n], hT_bufs(c), b_hb[hi])
        return hi

    def store_h(c, hi):
        n = c["n"]
        dma("pool", "st%d" % hi, hT[:, :, c["s0"]:c["s0"] + n], hbv[hi][:, :, :n], b_hb[hi], hT_bufs(c))

    def pass_a(l, g):
        if g["nh"] == 1:
            P.add("pool", lambda e: e.memset(cbv[:, :, 31 + HS:46 + HS], 0.0), [], [b_cbuf])
        chunks = g["chunks"]
        his = {}
        cast_some(l + 1, 1)
        his[0] = load_h(chunks[0])
        sel["hn"] = 0
        ffn_norm(l, 0, his[0], chunks[0]["n"])
        ffn_gateup(l, 0, chunks[0]["n"])
        for i, c in enumerate(chunks):
            n = c["n"]
            nxt = chunks[i + 1] if i + 1 < len(chunks) else None
            if nxt is not None:
                cast_some(l + 1, 1)
                his[i + 1] = load_h(nxt)
                sel["hn"] = (i + 1) % 2
                ffn_norm(l, 0, his[i + 1], nxt["n"])
            ffn_down(l, 0, his[i], n)
            if g["gid"] == 0 and i == 0:
                layer_consts_dg(l)
            store_h(c, his[i])
            sel["hn"] = i % 2
            norm_h(his[i], n, cfg.GM + 8 * l)
            if nxt is not None:
                sel["hn"] = (i + 1) % 2
                ffn_gateup(l, 0, nxt["n"])
            sel["hn"] = i % 2
            pass_a_m1(l, g, c)
        sel["hn"] = 0
        if g["nh"] == 2:
            for cc in range(2):
                b0, b1 = cbase(0), cbase(1)
                ts("pool", cbv[:, cc, b0 + 31 + HS:b0 + 31 + HS + 15], cbv[:, cc, b1 + 31:b1 + 46], flags[:, 1:2], None,
                   ALU.mult, None, [b_cbuf, b_const], [b_cbuf])
                ts("pool", cbv[:, cc, b1:b1 + 31], cbv[:, cc, b1:b1 + 31], flags[:, 2:3], None,
                   ALU.mult, None, [b_cbuf, b_const], [b_cbuf])
                stt("dve", cbv[:, cc, b1:b1 + 31], cbv[:, cc, b0 + HS:b0 + HS + 31], flags[:, 1:2], cbv[:, cc, b1:b1 + 31],
                    ALU.mult, ALU.add, [b_cbuf, b_const], [b_cbuf])

    def pass_a_m1(l, g, c):
        if True:
            n = c["n"]; g0 = c["g0"]
            wb = 6 * NF
            tk, bk = wtile(l, wb + 4)
            pk, bpk = PS()
            for kc in range(8):
                mm(pk[:, :n], tk[:, kc, :], HV()[:, kc, :n], kc == 0, kc == 7, [bk, HB()[kc]], [bpk])
            cp("act", kT[:, g0:g0 + n], pk[:, :n], [bpk], [b_kT])
            tv, bv = wtile(l, wb + 5)
            pv, bpv = PS()
            ntt = (n + 127) // 128
            for t_ in range(ntt):
                nr = min(128, n - t_ * 128)
                for kc in range(8):
                    mm(pv[0:nr, t_ * 128:(t_ + 1) * 128], HV()[:, kc, t_ * 128:t_ * 128 + nr], tv[:, kc, :],
                       kc == 0, kc == 7, [bv, HB()[kc]], [bpv])
            nr = min(128, n)
            cp("dve", vtv[0:nr, g0 // 128:g0 // 128 + ntt, :],
               pv[0:nr, 0:ntt * 128].rearrange("p (a c) -> p a c", c=128), [bpv], [b_vt])
            tf0, bf0 = wtile(l, wb + 6)
            tf1, bf1 = wtile(l, wb + 7)
            pf = [PS() for _ in range((ntt + 1) // 2)]
            for c2, (tf_, bf_) in enumerate(((tf0, bf0), (tf1, bf1))):
                for t_ in range(ntt):
                    nr = min(128, n - t_ * 128)
                    pp, bpp = pf[t_ // 2]
                    o = (t_ % 2) * 256 + c2 * 128
                    for kc in range(8):
                        mm(pp[0:nr, o:o + 128], HV()[:, kc, t_ * 128:t_ * 128 + nr], tf_[:, kc, :],
                           kc == 0, kc == 7, [bf_, HB()[kc]], [bpp])
            for q_, (pp, bpp) in enumerate(pf):
                na = min(2, ntt - 2 * q_)
                nr = min(128, n)
                cp("act" if q_ == 0 else "dve", uftv[0:nr, g0 // 128 + 2 * q_:g0 // 128 + 2 * q_ + na, :],
                   pp[0:nr, 0:na * 256].rearrange("p (a c) -> p a c", c=256), [bpp], [b_uft])
            for cc in range(2):
                ta, ba = wtile(l, wb + 8 + 2 * cc)
                tg_, bg_ = wtile(l, wb + 9 + 2 * cc)
                pa, bpa = PS()
                for kc in range(8):
                    mm(pa[:, :n], ta[:, kc, :], HV()[:, kc, :n], kc == 0, kc == 7, [ba, HB()[kc]], [bpa])
                pg, bpg = PS()
                for kc in range(8):
                    mm(pg[:, :n], tg_[:, kc, :], HV()[:, kc, :n], kc == 0, kc == 7, [bg_, HB()[kc]], [bpg])
                si = st["sb"]; st["sb"] = 1 - si
                sv = sbt[si][:, :n]
                act(sv, pg[:, :n], AF.Tanh, [bpg], [b_sbt[si]], scale=0.5)
                ts("dve", sv, sv, 0.5, 0.5, ALU.mult, ALU.add, [b_sbt[si]], [b_sbt[si]])
                if c["kind"] == "real":
                    p0 = cbase(c["hloc"]) + 31 + c["coff"]
                    tt("dve", cbv[:, cc, p0:p0 + n], pa[:, :n], sv, ALU.mult, [bpa, b_sbt[si]], [b_cbuf])
                else:
                    for hl in range(g["nh"]):
                        p0 = cbase(hl) + 15
                        tt("dve", cbv[:, cc, p0:p0 + 16], pa[:, hl * 16:hl * 16 + 16], sbt[si][:, hl * 16:hl * 16 + 16],
                           ALU.mult, [bpa, b_sbt[si]], [b_cbuf])

    def layer_consts_dg(l):
        for cc in range(2):
            for j in range(CW):
                col = cfg.CWD + (2 * l + cc) * CW + j
                ts("pool", dgv[:, cc, j, :], identb[:, :], gn[:, col:col + 1], None, ALU.mult, None,
                   [b_const], [b_dg[cc]])

    def layer_consts(l):
        dma("sp", "sink", sinkst[0:1, :], sinkx_d[0:1, l * 1024:(l + 1) * 1024], [], [b_yc])
        act(esrow[0:1, :], sinkst[0:1, :], AF.Exp, [b_yc], [b_esrow])

    def attn_scores(u, lo, hi_):
        gi, qcols, nq = u["gi"], u["qcols"], u["nq"]
        N = 4 * nq
        for (k0, nk, vap, bias, negf) in u["kts"][lo:hi_]:
            ps_, bps = PS()
            mm(ps_[0:nk, 0:N], kT[:, k0:k0 + nk], actv[:, 8 + 4 * gi:12 + 4 * gi, qcols], True, True,
               [b_kT] + b_act[8 + 4 * gi:12 + 4 * gi], [bps])
            pi = 16 + st["pt"]; st["pt"] = (st["pt"] + 1) % 6
            ptv = actv[0:nk, pi, 0:N]
            if bias is None:
                act(ptv, ps_[0:nk, 0:N], AF.Exp, [bps], [b_act[pi]], scale=0.125)
            else:
                si = st["sb"]; st["sb"] = 1 - si
                sv = sbt[si][0:nk, 0:N]
                stt("dve", sv, ps_[0:nk, 0:N], 0.125, bias, ALU.mult, ALU.add, [bps, b_const], [b_sbt[si]])
                if negf:
                    ts("dve", sv, sv, flags[0:nk, 0:1], None, ALU.add, None, [b_sbt[si], b_const], [b_sbt[si]])
                act(ptv, sv, AF.Exp, [b_sbt[si]], [b_act[pi]])
            u["pts"].append((ptv, b_act[pi], nk, vap))

    def attn_pv(u):
        gi, nq, oi = u["gi"], u["nq"], u["oi"]
        R0 = 64 * gi
        N = 4 * nq
        pts = u["pts"]
        po, bpo = PS()
        for i, (ptv, bpt, nk, vap) in enumerate(pts):
            mm(po[:, 0:N], vap, ptv, i == 0, i == len(pts) - 1, [b_vt, bpt], [bpo])
        pd, bpd = PS()
        for i, (ptv, bpt, nk, vap) in enumerate(pts):
            mm(pd[:, 0:N], onesb[0:nk, :], ptv, i == 0, False, [b_const, bpt], [bpd])
        mm(pd[:, 0:N], onesb[0:1, :], u["sap"], False, True, [b_const, b_esrow], [bpd])
        act(rden[R0:R0 + 64, 0:N], pd[R0:R0 + 64, 0:N], AF.Ln, [bpd], [b_rden])
        act(rden[R0:R0 + 64, 0:N], rden[R0:R0 + 64, 0:N], AF.Exp, [b_rden], [b_rden], scale=-1.0)
        tt("dve", otmps[oi][R0:R0 + 64, 0:N], po[R0:R0 + 64, 0:N], rden[R0:R0 + 64, 0:N], ALU.mult, [bpo, b_rden], [b_otmps[oi]])

    def attention_run(l, units, hooks):
        def split(u):
            return (len(u["kts"]) + 1) // 2
        pend = []
        if units:
            attn_scores(units[0], 0, len(units[0]["kts"]))
        for i, u in enumerate(units):
            nxt = units[i + 1] if i + 1 < len(units) else None
            if nxt is not None:
                attn_scores(nxt, 0, split(nxt))
            attn_pv(u)
            if nxt is not None:
                attn_scores(nxt, split(nxt), len(nxt["kts"]))
            if u["fin"] is not None:
                pend.append(u["fin"])
                if len(pend) > 1:
                    attn_finish(l, *pend.pop(0))
            if i in hooks:
                hooks.pop(i)()
        while pend:
            attn_finish(l, *pend.pop(0))
        for k in sorted(hooks):
            hooks[k]()

    def attn_finish(l, nq, out_cols, oi):
        otmp = otmps[oi]; b_otmp = b_otmps[oi]
        ov = otmp[:, 0:4 * nq].rearrange("p (g q) -> p g q", g=4)
        for g_ in range(4):
            tt("pool", actv[:, g_, 0:nq], ov[:, g_, :], ov[:, g_, :], ALU.mult, [b_otmp], [b_act[g_]])
        pt_, bp = PS()
        for g_ in range(4):
            mm(pt_[:, 0:nq], onesb[:, :], actv[:, g_, 0:nq], g_ == 0, g_ == 3, [b_const, b_act[g_]], [bp])
        act(lnout[:, 0:nq], pt_[:, 0:nq], AF.Ln, [bp, b_const], [b_lnout], scale=1.0 / 512.0, bias=epsb[:, 0:1])
        act(rstt[1][:, 0:nq], lnout[:, 0:nq], AF.Exp, [b_lnout], [b_rst[1]], scale=-0.5)
        for g_ in range(4):
            col = cfg.GB + 8 * l + g_
            stt("dve", onv[:, g_, out_cols], ov[:, g_, :], gn[:, col:col + 1], rstt[1][:, 0:nq], ALU.mult, ALU.mult,
                [b_otmp, b_rst[1], b_const], [b_on[g_]])

    def pass_b(l, g, last):
        gid = g["gid"]; nreal = g["nreal"]; nmeta = g["nmeta"]; nh = g["nh"]
        ntr = nreal // 128
        nnt = ntr + 1
        nblk_h = HS // 128
        for ci, c in enumerate(g["chunks"]):
            n = c["n"]; g0 = c["g0"]
            cast_some(l + 1, 1)
            hi = load_h(c)
            norm_h(hi, n, cfg.GM + 8 * l)
            wb = 6 * NF
            ycv = yct[:, :].rearrange("p (c t) -> p c t", c=2)
            accs = [PS() for _ in range(4)]
            for nt in range(nnt):
                nr = 128 if nt < ntr else nmeta
                flat, br_ = ring_load(dft_d[gid][ci, nt][0:nr, :, 0:n],
                                      lambda s: s.rearrange("p (a k) -> p a k", a=2)[0:nr, :, 0:n], None)
                tv_ = flat.rearrange("p (a k) -> p a k", a=2)
                for cc in range(2):
                    for ab in range(2):
                        pa_, bpa_ = accs[ab * 2 + cc]
                        mm(pa_[:, :n], uftv[0:nr, nt, cc * 128:(cc + 1) * 128], tv_[0:nr, ab, 0:n], nt == 0, nt == nnt - 1,
                           [b_uft, br_], [bpa_])
            if c["kind"] == "real":
                segs = [(cbase(c["hloc"]) + 16 + c["coff"], 0, n)]
            else:
                segs = [(cbase(hl), hl * 16, 16) for hl in range(nh)]
            for cc in range(2):
                py, bpy = PS()
                for (p0, o0, ns) in segs:
                    for j in range(CW):
                        mm(py[:, o0:o0 + ns], dgv[:, cc, j, :], cbv[:, cc, p0 + j:p0 + j + ns], j == 0, j == CW - 1,
                           [b_dg[cc], b_cbuf], [bpy])
                col = cfg.CBD + 2 * l + cc
                ts("dve", ycv[:, cc, :n], py[:, :n], gn[:, col:col + 1], None, ALU.add, None, [bpy, b_const], [b_yc])
                cp("act", actv[:, cc, :n], ycv[:, cc, :n], [b_yc], [b_act[cc]])
            pfs = []
            for cc in range(2):
                cp("act", actv[:, 16 + cc, :n], accs[cc][0][:, :n], [accs[cc][1]], [b_act[16 + cc]])
                cp("dve", actv[:, 18 + cc, :n], accs[2 + cc][0][:, :n], [accs[2 + cc][1]], [b_act[18 + cc]])
            for cc in range(2):
                pf_, bpf_ = PS()
                mm(pf_[:, :n], c64v[:, cc, 0, :], actv[:, 16 + cc, :n], True, False, [b_const, b_act[16 + cc]], [bpf_])
                mm(pf_[:, :n], c64v[:, cc, 1, :], actv[:, 18 + cc, :n], False, True, [b_const, b_act[18 + cc]], [bpf_])
                pfs.append((pf_, bpf_))
            pm, bpm = PS()
            for cc in range(2):
                mm(pm[:, :n], onesb[:, :], actv[:, cc, :n], cc == 0, cc == 1, [b_const, b_act[cc]], [bpm])
            for j in range(4):
                tq, bq = wtile(l, wb + j)
                pq, bpq = PS()
                for kc in range(8):
                    mm(pq[:, :n], tq[:, kc, :], HV()[:, kc, :n], kc == 0, kc == 7, [bq, HB()[kc]], [bpq])
                P.add("pool", (lambda o: (lambda e: e.memset(o, 0.0)))(actv[64:128, 8 + j, :n]), [], [b_act[8 + j]])
                P.add("pool", (lambda o: (lambda e: e.memset(o, 0.0)))(actv[0:64, 12 + j, :n]), [], [b_act[12 + j]])
                cp("act", actv[0:64, 8 + j, :n], pq[0:64, :n], [bpq], [b_act[8 + j]])
                cp("dve", actv[64:128, 12 + j, :n], pq[64:128, :n], [bpq], [b_act[12 + j]])
            stats([(pfs[cc][0][:, :n], pfs[cc][1], True) for cc in range(2)], n, 256.0, 1, s0=2)
            for cc in range(2):
                col = cfg.GB + 8 * l + 4 + cc
                stt("dve", onv[:, 4 + cc, :n], pfs[cc][0][:, :n], gn[:, col:col + 1], rstt[1][:, :n], ALU.mult, ALU.mult,
                    [pfs[cc][1], b_rst[1], b_const], [b_on[4 + cc]])
            for cc in range(2):
                stt("dve", ycv[:, cc, :n], pm[:, :n], -1.0 / 256.0, ycv[:, cc, :n], ALU.mult, ALU.add, [bpm, b_yc], [b_yc])
            stats_pre([(ycv[:, cc, :n], b_yc, False) for cc in range(2)], n, s0=4)

            def conv_stage2(l=l, n=n, ycv=ycv):
                stats_post(2, n, 256.0, 2, s0=4)
                for cc in range(2):
                    tt("dve", ycv[:, cc, :n], ycv[:, cc, :n], rstt[2][:, :n], ALU.mult, [b_yc, b_rst[2]], [b_yc])
                    cg, cb_ = cfg.CLG + 2 * l + cc, cfg.CLB + 2 * l + cc
                    ts("dve", ycv[:, cc, :n], ycv[:, cc, :n], gn[:, cg:cg + 1], gn[:, cb_:cb_ + 1], ALU.mult, ALU.add,
                       [b_yc, b_const], [b_yc])
                    act(ycv[:, cc, :n], ycv[:, cc, :n], AF.Silu, [b_yc], [b_yc])
                stats_pre([(ycv[:, cc, :n], b_yc, False) for cc in range(2)], n, s0=6)

            def conv_stage3(l=l, n=n, ycv=ycv):
                stats_post(2, n, 256.0, 2, s0=6)
                for cc in range(2):
                    col = cfg.GB + 8 * l + 6 + cc
                    stt("dve", onv[:, 6 + cc, :n], ycv[:, cc, :n], gn[:, col:col + 1], rstt[2][:, :n], ALU.mult, ALU.mult,
                        [b_yc, b_rst[2], b_const], [b_on[6 + cc]])
            mk0 = nreal
            units = []
            if c["kind"] == "real":
                hloc = c["hloc"]
                for blk in range(4):
                    bi = c["coff"] // 128 + blk
                    gb = g0 + blk * 128
                    qcols = slice(blk * 128, blk * 128 + 128)
                    for gi in range(2):
                        kts = []
                        if gid == 0:
                            mb = mtab[0:32, hloc * 512:(hloc + 1) * 512]
                        else:
                            mb = None
                        kts.append((mk0, nmeta, vtv[0:nmeta, ntr, :], mb, False))
                        if bi > 0:
                            kts.append((gb - 128, 128, vtv[:, gb // 128 - 1, :], btv[:, gi * 3 + 0, :], False))
                        elif hloc == 1:
                            kts.append((gb - 128, 128, vtv[:, gb // 128 - 1, :], btv[:, gi * 3 + 0, :], True))
                        kts.append((gb, 128, vtv[:, gb // 128, :], btv[:, gi * 3 + 1, :], False))
                        if bi < nblk_h - 1:
                            kts.append((gb + 128, 128, vtv[:, gb // 128 + 1, :], btv[:, gi * 3 + 2, :], False))
                        elif hloc == 0 and nh == 2:
                            kts.append((gb + 128, 128, vtv[:, gb // 128 + 1, :], btv[:, gi * 3 + 2, :], True))
                        units.append(dict(gi=gi, qcols=qcols, nq=128, kts=kts, oi=blk % 2, pts=[],
                                          sap=esrow[0:1, gi * 512:(gi + 1) * 512],
                                          fin=(128, qcols, blk % 2) if gi == 1 else None))
            else:
                for hl in range(nh):
                    qcols = slice(hl * 16, hl * 16 + 16)
                    for gi in range(2):
                        kts = []
                        mb = mmb[0:32, hl * 64:(hl + 1) * 64] if gid == 0 else None
                        kts.append((mk0, nmeta, vtv[0:nmeta, ntr, :], mb, False))
                        kts.append((hl * HS, 128, vtv[:, hl * HS // 128, :], mqb[:, gi * 64:(gi + 1) * 64], False))
                        sap = esrow[0:1, gi * 512:(gi + 1) * 512].rearrange("p (g q) -> p g q", g=4)[:, :, 0:16]
                        units.append(dict(gi=gi, qcols=qcols, nq=16, kts=kts, oi=hl % 2, pts=[], sap=sap,
                                          fin=(16, qcols, hl % 2) if gi == 1 else None))
            attention_run(l, units, {0: conv_stage2, 2: conv_stage3})
            for dc in range(8):
                two, bwo = wtile(l, wb + 12 + dc)
                pw, bpw = PS()
                for fc in range(8):
                    mm(pw[:, :n], two[:, fc, :], onv[:, fc, :n], fc == 0, fc == 7, [bwo, b_on[fc]], [bpw])
                tt("dve", hbv[hi][:, dc, :n], pw[:, :n], hbv[hi][:, dc, :n], ALU.add, [bpw, b_hb[hi][dc]], [b_hb[hi][dc]])
            ffn(l, 1, hi, n)
            if not last:
                store_h(c, hi)
            elif c["kind"] == "real":
                stats([(hbv[hi][:, kc, :n], b_hb[hi][kc], False) for kc in range(8)], n, float(D), 0)
                for kc in range(8):
                    stt("dve", hbv[hi][:, kc, :n], hbv[hi][:, kc, :n], gn[:, cfg.GF + kc:cfg.GF + kc + 1], rstt[0][:, :n],
                        ALU.mult, ALU.mult, [b_hb[hi][kc], b_rst[0], b_const], [b_hb[hi][kc]])
                for t_ in range(n // 128):
                    oi = t_ % 2
                    for hf in range(2):
                        pt_, bp = PS()
                        for j in range(4):
                            kc = hf * 4 + j
                            P.add("pe", (lambda o, i_, idn: (lambda e: e.transpose(o, i_, idn)))(
                                pt_[:, j * 128:(j + 1) * 128], hbv[hi][:, kc, t_ * 128:(t_ + 1) * 128], identf[:, :]),
                                [b_hb[hi][kc], b_const], [bp])
                        cp("act" if hf == 0 else "dve", ost[:, oi * 1024 + hf * 512:oi * 1024 + hf * 512 + 512], pt_[:, :],
                           [bp], [b_ost[oi]] + (b_sbt if oi == 0 else b_otmps))
                    r0 = c["s0"] + t_ * 128
                    dma("pool", "out%d" % oi, yout[r0:r0 + 128, :], ost[:, oi * 1024:(oi + 1) * 1024], [b_ost[oi]] + (b_sbt if oi == 0 else b_otmps), [])

    for l in range(L):
        layer_consts(l)
        for g in groups:
            pass_a(l, g)
            pass_b(l, g, l == L - 1)
        cast_some(l + 1, 1000)

    P.emit(nc, es)
    es.close()
    return nc


_CACHE = {}


def run(cfg, inputs):
    in_maps = host_prep(cfg, inputs)
    key = (cfg.L, cfg.HS, cfg.DFF)
    if key not in _CACHE:
        _CACHE[key] = build_program(cfg)
    nc = _CACHE[key]
    res = run_bass_kernel_spmd(nc, in_maps, core_ids=list(range(NCORES)))
    HS = cfg.HS
    nb_p = 16
    yp = np.empty((nb_p, HS, D), np.float32)
    ys = np.empty((4, 2 * HS, D), np.float32)
    for c in range(NCORES):
        y = np.asarray(res.results[c]["yout"], np.float32)
        if c < 4:
            for i in range(3):
                yp[3 * c + i] = y[i * HS:(i + 1) * HS]
        else:
            s_ = c - 4
            ys[s_] = y[0:2 * HS]
            yp[12 + s_] = y[2 * HS:3 * HS]
    return yp, ys


def kernel(**inputs):
    cfg = Cfg(L=4, HS=2048, DFF=2816)
    return run(cfg, inputs)
```

```python
import math
from contextlib import ExitStack

import numpy as np
import ml_dtypes

import concourse.bass as bass
import concourse.mybir as mybir
from concourse.bass_utils import run_bass_kernel_spmd

F32 = mybir.dt.float32
BF16 = mybir.dt.bfloat16
AF = mybir.ActivationFunctionType
ALU = mybir.AluOpType
NPBF = ml_dtypes.bfloat16

D = 1024
NM = 16
EPS = 1e-6
NEG = -1e30
CW = 31
NCORES = 8


class Cfg:
    def __init__(s, L=4, HS=2048, DFF=2816):
        s.L, s.HS, s.DFF = L, HS, DFF
        s.NF = DFF // 128
        assert s.NF % 2 == 0 and HS % 512 == 0
        s.TR = 3 * HS
        s.T = s.TR + 3 * NM
        s.NTL = 6 * s.NF + 20
        s.NKC0, s.NNT0 = 2 * HS // 512 + 1, 2 * HS // 128 + 1
        s.NKC1, s.NNT1 = HS // 512 + 1, HS // 128 + 1
        L8 = L * 8
        s.G1, s.GM, s.G2, s.GF, s.GB = 0, L8, 2 * L8, 3 * L8, 3 * L8 + 8
        s.CBD = s.GB + L8
        s.CLG = s.CBD + 2 * L
        s.CLB = s.CLG + 2 * L
        s.CWD = s.CLB + 2 * L
        s.NGC = s.CWD + 2 * L * CW
        s.CB = 2 * (HS + 46)
        s.NSLOT = max(s.NF, 22)


def _cols(v):
    return np.ascontiguousarray(np.asarray(v, np.float32).reshape(-1, 128).T)


def _dft_table(cfg, pos, seq, Ls, nkc, nnt):
    n = len(pos)
    pp = np.zeros(nnt * 128, np.int64); ss = np.full(nnt * 128, -1, np.int64)
    pp[:n] = pos; ss[:n] = seq
    pk = np.zeros(nkc * 512, np.int64); sk = np.full(nkc * 512, -2, np.int64)
    pk[:n] = pos; sk[:n] = seq
    out = np.empty((nkc, nnt, 128, 2, 512), NPBF)
    for kc in range(nkc):
        k_p = pk[kc * 512:(kc + 1) * 512]; k_s = sk[kc * 512:(kc + 1) * 512]
        r = (pp[:, None] * k_p[None, :]) % Ls
        ang = (2.0 * np.pi / Ls) * r.astype(np.float64)
        same = ((ss[:, None] == k_s[None, :]) & (ss[:, None] >= 0)).astype(np.float64) / math.sqrt(Ls)
        c = (np.cos(ang) * same).reshape(nnt, 128, 512)
        s_ = (np.sin(ang) * same).reshape(nnt, 128, 512)
        out[kc, :, :, 0, :] = c.astype(NPBF)
        out[kc, :, :, 1, :] = s_.astype(NPBF)
    return out


def host_prep(cfg, inp):
    L, HS, NF = cfg.L, cfg.HS, cfg.NF
    f32 = np.float32
    xp = np.asarray(inp["x_prompt"], f32); xs = np.asarray(inp["x_sample"], f32)
    meta = np.asarray(inp["meta_tokens"], f32)
    wt = np.empty((L * cfg.NTL, 128, 1024), f32)
    qperm = np.concatenate([np.r_[64 * j:64 * j + 64, 256 + 64 * j:256 + 64 * j + 64] for j in range(4)])
    for l in range(L):
        base = l * cfg.NTL
        for wi, (g, u, d) in enumerate((("ffn1_w_gate", "ffn1_w_up", "ffn1_w_down"),
                                        ("ffn2_w_gate", "ffn2_w_up", "ffn2_w_down"))):
            o = base + wi * 3 * NF
            wg = np.asarray(inp[g][l], f32).reshape(8, 128, NF, 128).transpose(2, 1, 0, 3)
            wu = np.asarray(inp[u][l], f32).reshape(8, 128, NF, 128).transpose(2, 1, 0, 3)
            wt[o:o + 2 * NF:2] = wg.reshape(NF, 128, 1024)
            wt[o + 1:o + 2 * NF:2] = wu.reshape(NF, 128, 1024)
            wd = np.asarray(inp[d][l], f32).reshape(NF // 2, 2, 128, 2, 4, 128).transpose(3, 0, 2, 1, 4, 5)
            wt[o + 2 * NF:o + 3 * NF] = wd.reshape(NF, 128, 1024)
        win = np.asarray(inp["w_in"][l], f32)
        cols = np.concatenate([qperm, np.arange(512, 1024),
                               np.r_[1024:1152], np.r_[1280:1408],
                               np.r_[1152:1280], np.r_[1408:1536]])
        wp = win[:, cols].reshape(8, 128, 12, 128).transpose(2, 1, 0, 3)
        wt[base + 6 * NF:base + 6 * NF + 12] = wp.reshape(12, 128, 1024)
        wo = np.asarray(inp["w_out"][l], f32)
        rows = np.concatenate([qperm, np.arange(512, 1024)])
        wop = wo[rows].reshape(8, 128, 8, 128).transpose(2, 1, 0, 3)
        wt[base + 6 * NF + 12:base + 6 * NF + 20] = wop.reshape(8, 128, 1024)
    gn = np.zeros((128, cfg.NGC), f32)
    for l in range(L):
        gn[:, cfg.G1 + 8 * l:cfg.G1 + 8 * l + 8] = _cols(inp["ffn1_norm"][l])
        gn[:, cfg.GM + 8 * l:cfg.GM + 8 * l + 8] = _cols(inp["mix_norm"][l])
        gn[:, cfg.G2 + 8 * l:cfg.G2 + 8 * l + 8] = _cols(inp["ffn2_norm"][l])
        br = np.asarray(inp["branch_norm"][l], f32)
        gn[:, cfg.GB + 8 * l:cfg.GB + 8 * l + 8] = _cols(br[rows])
        gn[:, cfg.CBD + 2 * l:cfg.CBD + 2 * l + 2] = _cols(inp["conv_b_dw"][l])
        gn[:, cfg.CLG + 2 * l:cfg.CLG + 2 * l + 2] = _cols(inp["conv_ln_g"][l])
        gn[:, cfg.CLB + 2 * l:cfg.CLB + 2 * l + 2] = _cols(inp["conv_ln_b"][l])
        w = np.asarray(inp["conv_w_dw"][l], f32)
        for cc in range(2):
            o = cfg.CWD + (2 * l + cc) * CW
            gn[:, o:o + CW] = w[:, cc * 128:(cc + 1) * 128].T
    gn[:, cfg.GF:cfg.GF + 8] = _cols(inp["final_norm"])
    sink = np.asarray(inp["attn_sink"], f32)
    sinkx = np.ascontiguousarray(np.repeat(sink.reshape(L, 2, 4, 1), 128, axis=3)).reshape(1, L * 1024)
    slopes = 2.0 ** (-(np.arange(8) + 1.0))
    ii = np.arange(128)[None, :]; jj = np.arange(128)[:, None]
    btab = np.zeros((128, 6, 4, 128), f32)
    for gi in range(2):
        for g in range(4):
            s = slopes[4 * gi + g]
            dp = 128 + ii - jj
            btab[:, gi * 3 + 0, g] = np.where(jj >= ii, -s * dp, NEG)
            btab[:, gi * 3 + 1, g] = -s * np.abs(ii - jj)
            dn = 128 + jj - ii
            btab[:, gi * 3 + 2, g] = np.where(jj <= ii, -s * dn, NEG)
    btab = btab.reshape(128, 6 * 512).astype(NPBF)
    mqb = np.zeros((128, 2, 4, 16), f32)
    tt = np.arange(128)[:, None]; mm = np.arange(16)[None, :]
    for gi in range(2):
        for g in range(4):
            dist = 16 + tt - mm
            mqb[:, gi, g] = np.where(dist <= 128, -slopes[4 * gi + g] * dist, NEG)
    mqb = mqb.reshape(128, 128)
    mmb = np.full((32, 2, 64), NEG, f32)
    mmb[0:16, 0] = 0.0; mmb[16:32, 1] = 0.0
    mmb = mmb.reshape(32, 128)
    c64 = np.zeros((128, 2, 2, 128), np.float64)
    a = np.arange(128)
    ang = 2 * np.pi * ((a[:, None] % 64) * (a[None, :] % 64) % 64) / 64.0
    blk = ((a[:, None] // 64) == (a[None, :] // 64)) / 8.0
    for cc in range(2):
        c64[:, cc, 0] = np.cos(ang) * blk
        c64[:, cc, 1] = -np.sin(ang) * blk
    c64 = c64.reshape(128, 512).astype(NPBF)
    identf = np.eye(128, dtype=f32)

    def dft_for(kind):
        if kind == "P":
            pos = np.concatenate([16 + np.arange(HS), 16 + np.arange(HS), np.arange(16), np.arange(16)])
            seq = np.concatenate([np.zeros(HS), np.ones(HS), np.zeros(16), np.ones(16)]).astype(np.int64)
            return _dft_table(cfg, pos, seq, HS + 16, cfg.NKC0, cfg.NNT0)
        pos = np.concatenate([16 + np.arange(2 * HS), np.arange(16), np.zeros(16, np.int64)])
        seq = np.concatenate([np.zeros(2 * HS + 16), -np.ones(16)]).astype(np.int64)
        return _dft_table(cfg, pos, seq, 2 * HS + 16, cfg.NKC0, cfg.NNT0)

    dftP, dftS = dft_for("P"), dft_for("S")
    pos1 = np.concatenate([16 + np.arange(HS), np.arange(16)])
    dft1 = _dft_table(cfg, pos1, np.zeros(HS + 16, np.int64), HS + 16, cfg.NKC1, cfg.NNT1)

    def mtab_for(kind):
        m = np.full((32, 2, 512), NEG, f32)
        m[0:16, 0] = 0.0
        if kind == "P":
            m[16:32, 1] = 0.0
        else:
            m[0:16, 1] = 0.0
        return m.reshape(32, 1024)

    in_maps = []
    for c in range(NCORES):
        kind = "P" if c < 4 else "S"
        xin = np.empty((cfg.T, D), f32)
        if kind == "P":
            for i in range(3):
                xin[i * HS:(i + 1) * HS] = xp[3 * c + i]
            mrows = [meta, meta, meta]
        else:
            s_ = c - 4
            xin[0:2 * HS] = xs[s_]
            xin[2 * HS:3 * HS] = xp[12 + s_]
            mrows = [meta, np.zeros_like(meta), meta]
        for i in range(3):
            xin[cfg.TR + 16 * i:cfg.TR + 16 * i + 16] = mrows[i]
        flags = np.zeros((128, 4), f32)
        flags[:, 0] = NEG if kind == "P" else 0.0
        flags[:, 1] = 0.0 if kind == "P" else 1.0
        flags[:, 2] = 1.0 if kind == "P" else 0.0
        in_maps.append(dict(xin=xin, wt=wt, gn=gn, sinkx=sinkx, btab=btab, mtab=mtab_for(kind), mqb=mqb,
                            mmb=mmb, c64=c64, identf=identf, flags=flags,
                            dft0=dftP if kind == "P" else dftS, dft1=dft1))
    return in_maps


class Buf:
    __slots__ = ("name", "lw", "rd")

    def __init__(s, name):
        s.name, s.lw, s.rd = name, None, {}


class Op:
    __slots__ = ("eng", "fn", "deps", "dmaq", "dmaval", "need_inc", "incval")


class Prog:
    ENGS = ("pe", "act", "dve", "pool", "sp")

    def __init__(s):
        s.ops = {e: [] for e in s.ENGS}
        s.ndma = {}
        s.dmaeng = {}

    def add(s, eng, fn, R=(), W=(), dma=None):
        op = Op()
        op.eng, op.fn, op.need_inc, op.incval = eng, fn, False, 0
        op.dmaq = dma
        if dma:
            s.ndma[dma] = s.ndma.get(dma, 0) + 1
            assert s.dmaeng.setdefault(dma, eng) == eng
            op.dmaval = 16 * s.ndma[dma]
        deps = {}

        def need(p):
            if p is None:
                return
            if p.dmaq is None and p.eng == "pe" and eng == "pe" and not dma:
                return
            key = ("d", p.dmaq) if p.dmaq else ("e", p.eng)
            cur = deps.get(key)
            if cur is None or s._later(p, cur):
                deps[key] = p

        for b in R:
            need(b.lw)
        for b in W:
            need(b.lw)
            for r in b.rd.values():
                need(r)
        op.deps = list(deps.values())
        for p in op.deps:
            if p.dmaq is None:
                p.need_inc = True
        op_key = ("d", dma) if dma else ("e", eng)
        for b in R:
            b.rd[op_key] = op
        for b in W:
            b.lw = op; b.rd = {}
        op.incval = len(s.ops[eng])
        s.ops[eng].append(op)
        return op

    @staticmethod
    def _later(a, b):
        if a.dmaq:
            return a.dmaval > b.dmaval
        return a.incval > b.incval

    def emit(s, nc, es):
        esem = {e: es.enter_context(nc.semaphore("sem_" + e)) for e in s.ENGS}
        dsem = {q: es.enter_context(nc.semaphore("dsem_" + q)) for q in s.ndma}
        for e in s.ENGS:
            c = 0
            for op in s.ops[e]:
                if op.dmaq is None and op.need_inc:
                    c += 1; op.incval = c
                else:
                    op.incval = -1
            assert c < 60000, (e, c)
        for q in s.ndma:
            assert 16 * s.ndma[q] < 60000, (q, s.ndma[q])
        block = es.enter_context(nc.Block())

        def body(ename):
            def run(eng):
                waited = {}
                for op in s.ops[ename]:
                    for p in op.deps:
                        if p.dmaq:
                            key = ("d", p.dmaq); sem = dsem[p.dmaq]; val = p.dmaval
                        else:
                            key = ("e", p.eng); sem = esem[p.eng]; val = p.incval
                        if waited.get(key, 0) < val:
                            eng.wait_ge(sem, val); waited[key] = val
                    ins = op.fn(eng)
                    if op.dmaq:
                        ins.then_inc(dsem[op.dmaq], 16)
                    elif op.need_inc:
                        ins.then_inc(esem[ename], 1)
                for q in s.ndma:
                    if s.dmaeng[q] == ename:
                        eng.wait_ge(dsem[q], 16 * s.ndma[q])
            return run

        block.sync(body("sp")); block.gpsimd(body("pool")); block.vector(body("dve"))
        block.tensor(body("pe")); block.scalar(body("act"))


def build_program(cfg):
    L, HS, NF = cfg.L, cfg.HS, cfg.NF
    nc = bass.Bass("TRN2", target_bir_lowering=False)
    P = Prog()

    def DT(name, shape, dt, kind):
        return nc.dram_tensor(name, list(shape), dt, kind=kind).ap()

    xin = DT("xin", [cfg.T, D], F32, "ExternalInput")
    wt = DT("wt", [L * cfg.NTL, 128, 1024], F32, "ExternalInput")
    gn_d = DT("gn", [128, cfg.NGC], F32, "ExternalInput")
    sinkx_d = DT("sinkx", [1, L * 1024], F32, "ExternalInput")
    btab_d = DT("btab", [128, 6 * 512], BF16, "ExternalInput")
    mtab_d = DT("mtab", [32, 1024], F32, "ExternalInput")
    mqb_d = DT("mqb", [128, 128], F32, "ExternalInput")
    mmb_d = DT("mmb", [32, 128], F32, "ExternalInput")
    c64_d = DT("c64", [128, 512], BF16, "ExternalInput")
    identf_d = DT("identf", [128, 128], F32, "ExternalInput")
    flags_d = DT("flags", [128, 4], F32, "ExternalInput")
    dft_d = [DT("dft0", [cfg.NKC0, cfg.NNT0, 128, 2, 512], BF16, "ExternalInput"),
             DT("dft1", [cfg.NKC1, cfg.NNT1, 128, 2, 512], BF16, "ExternalInput")]
    yout = DT("yout", [cfg.TR, D], F32, "ExternalOutput")
    wtb = DT("wtb", [L * cfg.NTL, 128, 1024], BF16, "Internal")
    hT = DT("hT", [128, 8, cfg.T], F32, "Internal")

    es = ExitStack()

    def SB(name, cols, dt, parts=128):
        return es.enter_context(nc.sbuf_tensor("sb_" + name, [parts, cols], dt))

    identf = SB("identf", 128, F32); identb = SB("identb", 128, BF16); onesb = SB("onesb", 128, BF16)
    gn = SB("gn", cfg.NGC, F32); flags = SB("flags", 4, F32); epsb = SB("epsb", 1, F32)
    btab = SB("btab", 6 * 512, BF16); mtab = SB("mtab", 1024, F32); mqb = SB("mqb", 128, F32); mmb = SB("mmb", 128, F32)
    c64 = SB("c64", 512, BF16)
    esrow = SB("esrow", 1024, BF16)
    dg = SB("dg", 2 * CW * 128, BF16)
    NT0 = cfg.NNT0
    TG0 = 2 * HS + 32
    kT = SB("kT", TG0, BF16); vt = SB("vt", NT0 * 128, BF16); uft = SB("uft", NT0 * 256, BF16)
    cbuf = SB("cbuf", 2 * cfg.CB, BF16)
    hbt = [SB("hb%d" % i, 8 * 512, F32) for i in range(2)]
    hnt = SB("hn", 8 * 512, BF16); hnt2 = SB("hn2", 8 * 512, BF16); sqt = SB("sqr", 3 * 512, BF16)
    actt = SB("act", cfg.NSLOT * 512, BF16)
    ont = SB("on", 8 * 512, BF16)
    sgt = SB("sg", 2 * 512, BF16)
    lnout = SB("lnout", 512, F32)
    rstt = [SB("rst%d" % i, 512, F32) for i in range(3)]
    misc = SB("misc", 4 * 512, F32)
    sbt = [misc[:, 0:512], misc[:, 512:1024]]
    otmps = [misc[:, 1024:1536], misc[:, 1536:2048]]; rden = SB("rden", 512, F32)
    yct = SB("yct", 2 * 512, F32); ost = misc; sinkst = yct
    ringt = SB("ring", 8 * 1024, BF16)
    pst = [es.enter_context(nc.psum_tensor("ps%d" % i, [128, 512], F32)) for i in range(8)]

    B = Buf
    b_const = B("const"); b_dg = [B("dg0"), B("dg1")]; b_esrow = B("esrow")
    b_kT, b_vt, b_uft, b_cbuf = B("kT"), B("vt"), B("uft"), B("cbuf")
    b_hb = [[B("hb%d_%d" % (i, k)) for k in range(8)] for i in range(2)]
    b_hn_l = [[B("hn%d_%d" % (j, k)) for k in range(8)] for j in range(2)]
    b_sq = [B("sq%d" % k) for k in range(3)]
    b_act = [B("act%d" % k) for k in range(cfg.NSLOT)]
    b_on = [B("on%d" % k) for k in range(8)]
    b_sg = [B("sg0"), B("sg1")]; b_lnout = B("lnout"); b_rst = [B("rst%d" % i) for i in range(3)]
    b_sbt = [B("sbt0"), B("sbt1")]; b_otmps = [B("otmp0"), B("otmp1")]; b_rden = B("rden"); b_yc = B("yc")
    b_ost = [B("ost0"), B("ost1")]
    b_ring = [B("ring%d" % i) for i in range(8)]
    b_ps = [B("ps%d" % i) for i in range(8)]
    b_hTt = [B("hT%d" % i) for i in range((cfg.T + 127) // 128)]; b_wtb = [B("wtb%d" % l) for l in range(L)]

    hbv = [t[:, :].rearrange("p (c t) -> p c t", c=8) for t in hbt]
    hnv_l = [hnt[:, :].rearrange("p (c t) -> p c t", c=8), hnt2[:, :].rearrange("p (c t) -> p c t", c=8)]
    sqv = sqt[:, :].rearrange("p (c t) -> p c t", c=3)
    sel = {"hn": 0}

    def HV():
        return hnv_l[sel["hn"]]

    def HB():
        return b_hn_l[sel["hn"]]
    actv = actt[:, :].rearrange("p (c t) -> p c t", t=512)
    onv = ont[:, :].rearrange("p (c t) -> p c t", c=8)
    vtv = vt[:, :].rearrange("p (n c) -> p n c", c=128)
    uftv = uft[:, :].rearrange("p (n c) -> p n c", c=256)
    cbv = cbuf[:, :].rearrange("p (c t) -> p c t", c=2)
    dgv = dg[:, :].rearrange("p (c j m) -> p c j m", c=2, j=CW)
    btv = btab[:, :].rearrange("p (k t) -> p k t", k=6)
    c64v = c64[:, :].rearrange("p (c s m) -> p c s m", c=2, s=2)
    ringv = [ringt[:, i * 1024:(i + 1) * 1024] for i in range(8)]

    def mm(out, lhsT, rhs, start, stop, R, W):
        P.add("pe", lambda e: e.matmul(out, lhsT=lhsT, rhs=rhs, start=start, stop=stop), R, W)

    def act(out, in_, func, R, W, scale=1.0, bias=None):
        if bias is None:
            P.add("act", lambda e: e.activation(out=out, in_=in_, func=func, scale=scale), R, W)
        else:
            P.add("act", lambda e: e.activation(out=out, in_=in_, func=func, scale=scale, bias=bias), R, W)

    def tt(eng, out, in0, in1, op, R, W):
        P.add(eng, lambda e: e.tensor_tensor(out=out, in0=in0, in1=in1, op=op), R, W)

    def ts(eng, out, in0, s1, s2, op0, op1, R, W):
        if s2 is None:
            P.add(eng, lambda e: e.tensor_scalar(out=out, in0=in0, scalar1=s1, scalar2=None, op0=op0), R, W)
        else:
            P.add(eng, lambda e: e.tensor_scalar(out=out, in0=in0, scalar1=s1, scalar2=s2, op0=op0, op1=op1), R, W)

    def stt(eng, out, in0, scalar, in1, op0, op1, R, W):
        P.add(eng, lambda e: e.scalar_tensor_tensor(out=out, in0=in0, scalar=scalar, in1=in1, op0=op0, op1=op1), R, W)

    def cp(eng, out, in_, R, W):
        if eng == "act":
            P.add("act", lambda e: e.activation(out=out, in_=in_, func=AF.Copy), R, W)
        else:
            P.add(eng, lambda e: e.tensor_copy(out=out, in_=in_), R, W)

    def dma(q, key, out, in_, R, W):
        P.add(q, lambda e: e.dma_start(out=out, in_=in_), R, W, dma=key)

    st = {"ps": 0, "ring": 0, "sg": 0, "sb": 0, "pt": 0, "hb": 0, "sq": 0}

    def PS():
        i = st["ps"]; st["ps"] = (i + 1) % 8
        return pst[i], b_ps[i]

    def ring_load(src, view, l, idx=None):
        i = st["ring"]; st["ring"] = (i + 1) % 8
        if l is None:
            rb = []
        elif l == 0:
            rb = [b_wtb0[idx // CH]]
        else:
            rb = [b_wtb[l]]
        dma("sp", "ring%d" % i, view(ringv[i]), src, rb, [b_ring[i]])
        return ringv[i], b_ring[i]

    def wtile(l, idx):
        flat, b = ring_load(wtb[l * cfg.NTL + idx], lambda s: s, l, idx)
        return flat.rearrange("p (c m) -> p c m", c=8), b

    for (dst, src) in ((identf, identf_d), (gn, gn_d), (flags, flags_d), (btab, btab_d), (mqb, mqb_d), (c64, c64_d)):
        dma("sp", "const", dst[:, :], src[:, :], [], [b_const])
    dma("sp", "const", mtab[0:32, :], mtab_d[:, :], [], [b_const])
    dma("sp", "const", mmb[0:32, :], mmb_d[:, :], [], [b_const])
    cp("dve", identb[:, :], identf[:, :], [b_const], [b_const])
    P.add("dve", lambda e: e.memset(onesb[:, :], 1.0), [], [b_const])
    P.add("dve", lambda e: e.memset(epsb[:, :], EPS), [], [b_const])
    P.add("dve", lambda e: e.memset(cbuf[:, :], 0.0), [], [b_cbuf])
    P.add("dve", lambda e: e.memset(actt[:, :], 0.0), [], b_act)
    P.add("pool", lambda e: e.memset(kT[:, :], 0.0), [], [b_kT])
    P.add("pool", lambda e: e.memset(vt[:, :], 0.0), [], [b_vt])
    P.add("pool", lambda e: e.memset(uft[:, :], 0.0), [], [b_uft])

    CH = 8
    cast_todo = {l: [(l * cfg.NTL + t0, l * cfg.NTL + min(cfg.NTL, t0 + CH)) for t0 in range(0, cfg.NTL, CH)] for l in range(L)}

    ncast0 = len(cast_todo[0])
    b_wtb0 = [B("wtb0_%d" % j) for j in range(ncast0)]

    def cast_some(l, k):
        for _ in range(k):
            if l < L and cast_todo[l]:
                a, b = cast_todo[l].pop(0)
                if l == 0:
                    j = ncast0 - len(cast_todo[0]) - 1
                    dma("pool", "c0_%d" % j, wtb[a:b].rearrange("a p c -> (a p) c"), wt[a:b].rearrange("a p c -> (a p) c"),
                        [], [b_wtb0[j]])
                else:
                    dma("pool", "cast%d" % l, wtb[a:b].rearrange("a p c -> (a p) c"), wt[a:b].rearrange("a p c -> (a p) c"),
                        [], [b_wtb[l]])

    cast_some(0, 2)
    ntile_in = (cfg.T + 127) // 128
    for ti in range(ntile_in):
        r0 = ti * 128; nr = min(128, cfg.T - r0)
        xi = ti % 2
        xst = hbt[0][:, xi * 1024:(xi + 1) * 1024]; bx = b_hb[0][xi]
        hst = hbt[1][:, xi * 1024:(xi + 1) * 1024].rearrange("p (c t) -> p c t", c=8); bh = b_hb[1][xi]
        dma("sp", "xl%d" % xi, xst[0:nr, :], xin[r0:r0 + nr, :], [], [bx])
        for hf in range(2):
            pt_, bp = PS()
            for j in range(4):
                kc = hf * 4 + j
                P.add("pe", (lambda o, i_, idn: (lambda e: e.transpose(o, i_, idn)))(
                    pt_[:, j * 128:j * 128 + nr], xst[0:nr, kc * 128:(kc + 1) * 128], identf[0:nr, 0:nr]),
                    [bx, b_const], [bp])
            cp("act" if hf == 0 else "dve", hst[:, hf * 4:hf * 4 + 4, 0:nr],
               pt_[:, :].rearrange("p (c t) -> p c t", c=4)[:, :, 0:nr], [bp], [bh])
        dma("pool", "xs%d" % xi, hT[:, :, r0:r0 + nr], hst[:, :, 0:nr], [bh], [b_hTt[ti]])
        cast_some(0, 1)
    cast_some(0, 1000)

    def stats(srcs, n, count, rst_i, s0=0):
        stats_pre(srcs, n, s0)
        stats_post(len(srcs), n, count, rst_i, s0)

    def stats_post(nsrc, n, count, rst_i, s0=0):
        pt_, bp = PS()
        for k in range(nsrc):
            mm(pt_[:, :n], onesb[:, :], actv[:, s0 + k, :n], k == 0, k == nsrc - 1, [b_const, b_act[s0 + k]], [bp])
        act(lnout[:, :n], pt_[:, :n], AF.Ln, [bp, b_const], [b_lnout], scale=1.0 / count, bias=epsb[:, 0:1])
        act(rstt[rst_i][:, :n], lnout[:, :n], AF.Exp, [b_lnout], [b_rst[rst_i]], scale=-0.5)

    def stats_pre(srcs, n, s0=0):
        for k, (ap, b, inps) in enumerate(srcs):
            if inps or k % 3 == 1:
                act(actv[:, s0 + k, :n], ap, AF.Square, [b], [b_act[s0 + k]])
            else:
                tt("pool" if k % 3 == 0 else "dve", actv[:, s0 + k, :n], ap, ap, ALU.mult, [b], [b_act[s0 + k]])

    def norm_h(hi, n, gcol, rst_i=0):
        pt_, bp = PS()
        for kc in range(8):
            qi = st["sq"]; st["sq"] = (qi + 1) % 3
            ap = hbv[hi][:, kc, :n]
            if kc % 3 == 1:
                act(sqv[:, qi, :n], ap, AF.Square, [b_hb[hi][kc]], [b_sq[qi]])
            else:
                tt("pool" if kc % 3 == 0 else "dve", sqv[:, qi, :n], ap, ap, ALU.mult, [b_hb[hi][kc]], [b_sq[qi]])
            mm(pt_[:, :n], onesb[:, :], sqv[:, qi, :n], kc == 0, kc == 7, [b_const, b_sq[qi]], [bp])
        act(lnout[:, :n], pt_[:, :n], AF.Ln, [bp, b_const], [b_lnout], scale=1.0 / float(D), bias=epsb[:, 0:1])
        act(rstt[rst_i][:, :n], lnout[:, :n], AF.Exp, [b_lnout], [b_rst[rst_i]], scale=-0.5)
        for kc in range(8):
            stt("dve", HV()[:, kc, :n], hbv[hi][:, kc, :n], gn[:, gcol + kc:gcol + kc + 1], rstt[rst_i][:, :n],
                ALU.mult, ALU.mult, [b_hb[hi][kc], b_rst[rst_i], b_const], [HB()[kc]])

    def ffn(l, which, hi, n):
        ffn_norm(l, which, hi, n)
        ffn_gateup(l, which, n)
        ffn_down(l, which, hi, n)

    def ffn_norm(l, which, hi, n):
        norm_h(hi, n, (cfg.G1 if which == 0 else cfg.G2) + 8 * l)

    def ffn_gateup(l, which, n, f0=0, f1=None):
        base = which * 3 * NF
        for f in range(f0, NF if f1 is None else f1):
            tg, bg = wtile(l, base + 2 * f)
            tu, bu = wtile(l, base + 2 * f + 1)
            pg, bpg = PS()
            for kc in range(8):
                mm(pg[:, :n], tg[:, kc, :], HV()[:, kc, :n], kc == 0, kc == 7, [bg, HB()[kc]], [bpg])
            pu, bpu = PS()
            for kc in range(8):
                mm(pu[:, :n], tu[:, kc, :], HV()[:, kc, :n], kc == 0, kc == 7, [bu, HB()[kc]], [bpu])
            si = st["sg"]; st["sg"] = 1 - si
            sgv = sgt[:, si * 512:si * 512 + n]
            act(sgv, pg[:, :n], AF.Silu, [bpg], [b_sg[si]])
            tt("dve", actv[:, f, :n], pu[:, :n], sgv, ALU.mult, [bpu, b_sg[si]], [b_act[f]])

    def ffn_down(l, which, hi, n):
        base = which * 3 * NF
        for half in range(2):
            acc = [PS() for _ in range(4)]
            for pair in range(NF // 2):
                td, bd = wtile(l, base + 2 * NF + half * (NF // 2) + pair)
                for m in range(2):
                    f = 2 * pair + m
                    for dcl in range(4):
                        mm(acc[dcl][0][:, :n], td[:, m * 4 + dcl, :], actv[:, f, :n], f == 0, f == NF - 1,
                           [bd, b_act[f]], [acc[dcl][1]])
            for dcl in range(4):
                dc = half * 4 + dcl
                stt("dve", hbv[hi][:, dc, :n], acc[dcl][0][:, :n], 0.5, hbv[hi][:, dc, :n], ALU.mult, ALU.add,
                    [acc[dcl][1], b_hb[hi][dc]], [b_hb[hi][dc]])

    groups = [dict(gid=0, halves=[0, 1], nreal=2 * HS, mslot=cfg.TR, nmeta=32),
              dict(gid=1, halves=[2], nreal=HS, mslot=cfg.TR + 32, nmeta=16)]
    for g in groups:
        ch = []
        for hi_, hh in enumerate(g["halves"]):
            for c in range(HS // 512):
                ch.append(dict(kind="real", hloc=hi_, g0=hi_ * HS + c * 512, n=512, s0=hh * HS + c * 512, coff=c * 512))
        ch.append(dict(kind="meta", g0=g["nreal"], n=g["nmeta"], s0=g["mslot"]))
        g["chunks"] = ch
        g["nh"] = len(g["halves"])

    def cbase(hloc):
        return hloc * (HS + 46)

    def hT_bufs(c):
        return b_hTt[c["s0"] // 128:(c["s0"] + c["n"] - 1) // 128 + 1]

    def load_h(c):
        hi = st["hb"]; st["hb"] = 1 - hi
        n = c["n"]
        dma("sp", "hb%d" % hi, hbv[hi][:, :, :n], hT[:, :, c["s0"]:c["s0"] + n], hT_bufs(c), b_hb[hi])
        return hi

    def store_h(c, hi):
        n = c["n"]
        dma("pool", "st%d" % hi, hT[:, :, c["s0"]:c["s0"] + n], hbv[hi][:, :, :n], b_hb[hi], hT_bufs(c))

    def pass_a(l, g):
        if g["nh"] == 1:
            P.add("pool", lambda e: e.memset(cbv[:, :, 31 + HS:46 + HS], 0.0), [], [b_cbuf])
        chunks = g["chunks"]
        his = {}
        cast_some(l + 1, 1)
        his[0] = load_h(chunks[0])
        sel["hn"] = 0
        ffn_norm(l, 0, his[0], chunks[0]["n"])
        ffn_gateup(l, 0, chunks[0]["n"])
        for i, c in enumerate(chunks):
            n = c["n"]
            nxt = chunks[i + 1] if i + 1 < len(chunks) else None
            if nxt is not None:
                cast_some(l + 1, 1)
                his[i + 1] = load_h(nxt)
                sel["hn"] = (i + 1) % 2
                ffn_norm(l, 0, his[i + 1], nxt["n"])
            ffn_down(l, 0, his[i], n)
            if g["gid"] == 0 and i == 0:
                layer_consts_dg(l)
            store_h(c, his[i])
            fs = min(4, NF)
            if nxt is not None:
                sel["hn"] = (i + 1) % 2
                ffn_gateup(l, 0, nxt["n"], 0, fs)
            sel["hn"] = i % 2
            norm_h(his[i], n, cfg.GM + 8 * l)
            if nxt is not None:
                sel["hn"] = (i + 1) % 2
                ffn_gateup(l, 0, nxt["n"], fs, None)
            sel["hn"] = i % 2
            pass_a_m1(l, g, c)
        sel["hn"] = 0
        if g["nh"] == 2:
            for cc in range(2):
                b0, b1 = cbase(0), cbase(1)
                ts("pool", cbv[:, cc, b0 + 31 + HS:b0 + 31 + HS + 15], cbv[:, cc, b1 + 31:b1 + 46], flags[:, 1:2], None,
                   ALU.mult, None, [b_cbuf, b_const], [b_cbuf])
                ts("pool", cbv[:, cc, b1:b1 + 31], cbv[:, cc, b1:b1 + 31], flags[:, 2:3], None,
                   ALU.mult, None, [b_cbuf, b_const], [b_cbuf])
                stt("dve", cbv[:, cc, b1:b1 + 31], cbv[:, cc, b0 + HS:b0 + HS + 31], flags[:, 1:2], cbv[:, cc, b1:b1 + 31],
                    ALU.mult, ALU.add, [b_cbuf, b_const], [b_cbuf])

    def pass_a_m1(l, g, c):
        if True:
            n = c["n"]; g0 = c["g0"]
            wb = 6 * NF
            tk, bk = wtile(l, wb + 4)
            pk, bpk = PS()
            for kc in range(8):
                mm(pk[:, :n], tk[:, kc, :], HV()[:, kc, :n], kc == 0, kc == 7, [bk, HB()[kc]], [bpk])
            cp("act", kT[:, g0:g0 + n], pk[:, :n], [bpk], [b_kT])
            tv, bv = wtile(l, wb + 5)
            pv, bpv = PS()
            ntt = (n + 127) // 128
            for t_ in range(ntt):
                nr = min(128, n - t_ * 128)
                for kc in range(8):
                    mm(pv[0:nr, t_ * 128:(t_ + 1) * 128], HV()[:, kc, t_ * 128:t_ * 128 + nr], tv[:, kc, :],
                       kc == 0, kc == 7, [bv, HB()[kc]], [bpv])
            nr = min(128, n)
            cp("dve", vtv[0:nr, g0 // 128:g0 // 128 + ntt, :],
               pv[0:nr, 0:ntt * 128].rearrange("p (a c) -> p a c", c=128), [bpv], [b_vt])
            tf0, bf0 = wtile(l, wb + 6)
            tf1, bf1 = wtile(l, wb + 7)
            pf = [PS() for _ in range((ntt + 1) // 2)]
            for c2, (tf_, bf_) in enumerate(((tf0, bf0), (tf1, bf1))):
                for t_ in range(ntt):
                    nr = min(128, n - t_ * 128)
                    pp, bpp = pf[t_ // 2]
                    o = (t_ % 2) * 256 + c2 * 128
                    for kc in range(8):
                        mm(pp[0:nr, o:o + 128], HV()[:, kc, t_ * 128:t_ * 128 + nr], tf_[:, kc, :],
                           kc == 0, kc == 7, [bf_, HB()[kc]], [bpp])
            for q_, (pp, bpp) in enumerate(pf):
                na = min(2, ntt - 2 * q_)
                nr = min(128, n)
                cp("act" if q_ == 0 else "dve", uftv[0:nr, g0 // 128 + 2 * q_:g0 // 128 + 2 * q_ + na, :],
                   pp[0:nr, 0:na * 256].rearrange("p (a c) -> p a c", c=256), [bpp], [b_uft])
            for cc in range(2):
                ta, ba = wtile(l, wb + 8 + 2 * cc)
                tg_, bg_ = wtile(l, wb + 9 + 2 * cc)
                pa, bpa = PS()
                for kc in range(8):
                    mm(pa[:, :n], ta[:, kc, :], HV()[:, kc, :n], kc == 0, kc == 7, [ba, HB()[kc]], [bpa])
                pg, bpg = PS()
                for kc in range(8):
                    mm(pg[:, :n], tg_[:, kc, :], HV()[:, kc, :n], kc == 0, kc == 7, [bg_, HB()[kc]], [bpg])
                si = st["sb"]; st["sb"] = 1 - si
                sv = sbt[si][:, :n]
                act(sv, pg[:, :n], AF.Tanh, [bpg], [b_sbt[si]], scale=0.5)
                ts("dve", sv, sv, 0.5, 0.5, ALU.mult, ALU.add, [b_sbt[si]], [b_sbt[si]])
                if c["kind"] == "real":
                    p0 = cbase(c["hloc"]) + 31 + c["coff"]
                    tt("dve", cbv[:, cc, p0:p0 + n], pa[:, :n], sv, ALU.mult, [bpa, b_sbt[si]], [b_cbuf])
                else:
                    for hl in range(g["nh"]):
                        p0 = cbase(hl) + 15
                        tt("dve", cbv[:, cc, p0:p0 + 16], pa[:, hl * 16:hl * 16 + 16], sbt[si][:, hl * 16:hl * 16 + 16],
                           ALU.mult, [bpa, b_sbt[si]], [b_cbuf])

    def layer_consts_dg(l):
        for cc in range(2):
            for j in range(CW):
                col = cfg.CWD + (2 * l + cc) * CW + j
                ts("pool", dgv[:, cc, j, :], identb[:, :], gn[:, col:col + 1], None, ALU.mult, None,
                   [b_const], [b_dg[cc]])

    def layer_consts(l):
        dma("sp", "sink", sinkst[0:1, :], sinkx_d[0:1, l * 1024:(l + 1) * 1024], [], [b_yc])
        act(esrow[0:1, :], sinkst[0:1, :], AF.Exp, [b_yc], [b_esrow])

    def attn_scores(u, lo, hi_):
        gi, qcols, nq = u["gi"], u["qcols"], u["nq"]
        N = 4 * nq
        for (k0, nk, vap, bias, negf) in u["kts"][lo:hi_]:
            ps_, bps = PS()
            mm(ps_[0:nk, 0:N], kT[:, k0:k0 + nk], actv[:, 8 + 4 * gi:12 + 4 * gi, qcols], True, True,
               [b_kT] + b_act[8 + 4 * gi:12 + 4 * gi], [bps])
            pi = 16 + st["pt"]; st["pt"] = (st["pt"] + 1) % 6
            ptv = actv[0:nk, pi, 0:N]
            if bias is None:
                act(ptv, ps_[0:nk, 0:N], AF.Exp, [bps], [b_act[pi]], scale=0.125)
            else:
                si = st["sb"]; st["sb"] = 1 - si
                sv = sbt[si][0:nk, 0:N]
                stt("dve", sv, ps_[0:nk, 0:N], 0.125, bias, ALU.mult, ALU.add, [bps, b_const], [b_sbt[si]])
                if negf:
                    ts("dve", sv, sv, flags[0:nk, 0:1], None, ALU.add, None, [b_sbt[si], b_const], [b_sbt[si]])
                act(ptv, sv, AF.Exp, [b_sbt[si]], [b_act[pi]])
            u["pts"].append((ptv, b_act[pi], nk, vap))

    def attn_pv(u):
        gi, nq, oi = u["gi"], u["nq"], u["oi"]
        R0 = 64 * gi
        N = 4 * nq
        pts = u["pts"]
        po, bpo = PS()
        for i, (ptv, bpt, nk, vap) in enumerate(pts):
            mm(po[:, 0:N], vap, ptv, i == 0, i == len(pts) - 1, [b_vt, bpt], [bpo])
        pd, bpd = PS()
        for i, (ptv, bpt, nk, vap) in enumerate(pts):
            mm(pd[:, 0:N], onesb[0:nk, :], ptv, i == 0, False, [b_const, bpt], [bpd])
        mm(pd[:, 0:N], onesb[0:1, :], u["sap"], False, True, [b_const, b_esrow], [bpd])
        act(rden[R0:R0 + 64, 0:N], pd[R0:R0 + 64, 0:N], AF.Ln, [bpd], [b_rden])
        act(rden[R0:R0 + 64, 0:N], rden[R0:R0 + 64, 0:N], AF.Exp, [b_rden], [b_rden], scale=-1.0)
        tt("dve", otmps[oi][R0:R0 + 64, 0:N], po[R0:R0 + 64, 0:N], rden[R0:R0 + 64, 0:N], ALU.mult, [bpo, b_rden], [b_otmps[oi]])

    def attention_run(l, units, hooks):
        def split(u):
            return (len(u["kts"]) + 1) // 2
        pend = []
        if units:
            attn_scores(units[0], 0, len(units[0]["kts"]))
        for i, u in enumerate(units):
            nxt = units[i + 1] if i + 1 < len(units) else None
            if nxt is not None:
                attn_scores(nxt, 0, split(nxt))
            attn_pv(u)
            if nxt is not None:
                attn_scores(nxt, split(nxt), len(nxt["kts"]))
            if u["fin"] is not None:
                pend.append(u["fin"])
                if len(pend) > 1:
                    attn_finish(l, *pend.pop(0))
            if i in hooks:
                hooks.pop(i)()
        while pend:
            attn_finish(l, *pend.pop(0))
        for k in sorted(hooks):
            hooks[k]()

    def attn_finish(l, nq, out_cols, oi):
        otmp = otmps[oi]; b_otmp = b_otmps[oi]
        ov = otmp[:, 0:4 * nq].rearrange("p (g q) -> p g q", g=4)
        for g_ in range(4):
            tt("pool", actv[:, g_, 0:nq], ov[:, g_, :], ov[:, g_, :], ALU.mult, [b_otmp], [b_act[g_]])
        pt_, bp = PS()
        for g_ in range(4):
            mm(pt_[:, 0:nq], onesb[:, :], actv[:, g_, 0:nq], g_ == 0, g_ == 3, [b_const, b_act[g_]], [bp])
        act(lnout[:, 0:nq], pt_[:, 0:nq], AF.Ln, [bp, b_const], [b_lnout], scale=1.0 / 512.0, bias=epsb[:, 0:1])
        act(rstt[1][:, 0:nq], lnout[:, 0:nq], AF.Exp, [b_lnout], [b_rst[1]], scale=-0.5)
        for g_ in range(4):
            col = cfg.GB + 8 * l + g_
            stt("dve", onv[:, g_, out_cols], ov[:, g_, :], gn[:, col:col + 1], rstt[1][:, 0:nq], ALU.mult, ALU.mult,
                [b_otmp, b_rst[1], b_const], [b_on[g_]])

    def pass_b(l, g, last):
        gid = g["gid"]; nreal = g["nreal"]; nmeta = g["nmeta"]; nh = g["nh"]
        ntr = nreal // 128
        nnt = ntr + 1
        nblk_h = HS // 128
        for ci, c in enumerate(g["chunks"]):
            n = c["n"]; g0 = c["g0"]
            cast_some(l + 1, 1)
            hi = load_h(c)
            norm_h(hi, n, cfg.GM + 8 * l)
            wb = 6 * NF
            ycv = yct[:, :].rearrange("p (c t) -> p c t", c=2)
            accs = [PS() for _ in range(4)]
            for nt in range(nnt):
                nr = 128 if nt < ntr else nmeta
                flat, br_ = ring_load(dft_d[gid][ci, nt][0:nr, :, 0:n],
                                      lambda s: s.rearrange("p (a k) -> p a k", a=2)[0:nr, :, 0:n], None)
                tv_ = flat.rearrange("p (a k) -> p a k", a=2)
                for cc in range(2):
                    for ab in range(2):
                        pa_, bpa_ = accs[ab * 2 + cc]
                        mm(pa_[:, :n], uftv[0:nr, nt, cc * 128:(cc + 1) * 128], tv_[0:nr, ab, 0:n], nt == 0, nt == nnt - 1,
                           [b_uft, br_], [bpa_])
            if c["kind"] == "real":
                segs = [(cbase(c["hloc"]) + 16 + c["coff"], 0, n)]
            else:
                segs = [(cbase(hl), hl * 16, 16) for hl in range(nh)]
            for cc in range(2):
                py, bpy = PS()
                for (p0, o0, ns) in segs:
                    for j in range(CW):
                        mm(py[:, o0:o0 + ns], dgv[:, cc, j, :], cbv[:, cc, p0 + j:p0 + j + ns], j == 0, j == CW - 1,
                           [b_dg[cc], b_cbuf], [bpy])
                col = cfg.CBD + 2 * l + cc
                ts("dve", ycv[:, cc, :n], py[:, :n], gn[:, col:col + 1], None, ALU.add, None, [bpy, b_const], [b_yc])
                cp("act", actv[:, cc, :n], ycv[:, cc, :n], [b_yc], [b_act[cc]])
            pfs = []
            for cc in range(2):
                cp("act", actv[:, 16 + cc, :n], accs[cc][0][:, :n], [accs[cc][1]], [b_act[16 + cc]])
                cp("dve", actv[:, 18 + cc, :n], accs[2 + cc][0][:, :n], [accs[2 + cc][1]], [b_act[18 + cc]])
            for cc in range(2):
                pf_, bpf_ = PS()
                mm(pf_[:, :n], c64v[:, cc, 0, :], actv[:, 16 + cc, :n], True, False, [b_const, b_act[16 + cc]], [bpf_])
                mm(pf_[:, :n], c64v[:, cc, 1, :], actv[:, 18 + cc, :n], False, True, [b_const, b_act[18 + cc]], [bpf_])
                pfs.append((pf_, bpf_))
            pm, bpm = PS()
            for cc in range(2):
                mm(pm[:, :n], onesb[:, :], actv[:, cc, :n], cc == 0, cc == 1, [b_const, b_act[cc]], [bpm])
            for j in range(4):
                tq, bq = wtile(l, wb + j)
                pq, bpq = PS()
                for kc in range(8):
                    mm(pq[:, :n], tq[:, kc, :], HV()[:, kc, :n], kc == 0, kc == 7, [bq, HB()[kc]], [bpq])
                P.add("pool", (lambda o: (lambda e: e.memset(o, 0.0)))(actv[64:128, 8 + j, :n]), [], [b_act[8 + j]])
                P.add("pool", (lambda o: (lambda e: e.memset(o, 0.0)))(actv[0:64, 12 + j, :n]), [], [b_act[12 + j]])
                cp("act", actv[0:64, 8 + j, :n], pq[0:64, :n], [bpq], [b_act[8 + j]])
                cp("dve", actv[64:128, 12 + j, :n], pq[64:128, :n], [bpq], [b_act[12 + j]])
            stats([(pfs[cc][0][:, :n], pfs[cc][1], True) for cc in range(2)], n, 256.0, 1, s0=2)
            for cc in range(2):
                col = cfg.GB + 8 * l + 4 + cc
                stt("dve", onv[:, 4 + cc, :n], pfs[cc][0][:, :n], gn[:, col:col + 1], rstt[1][:, :n], ALU.mult, ALU.mult,
                    [pfs[cc][1], b_rst[1], b_const], [b_on[4 + cc]])
            for cc in range(2):
                stt("dve", ycv[:, cc, :n], pm[:, :n], -1.0 / 256.0, ycv[:, cc, :n], ALU.mult, ALU.add, [bpm, b_yc], [b_yc])
            stats_pre([(ycv[:, cc, :n], b_yc, False) for cc in range(2)], n, s0=4)

            def conv_stage2(l=l, n=n, ycv=ycv):
                stats_post(2, n, 256.0, 2, s0=4)
                for cc in range(2):
                    tt("dve", ycv[:, cc, :n], ycv[:, cc, :n], rstt[2][:, :n], ALU.mult, [b_yc, b_rst[2]], [b_yc])
                    cg, cb_ = cfg.CLG + 2 * l + cc, cfg.CLB + 2 * l + cc
                    ts("dve", ycv[:, cc, :n], ycv[:, cc, :n], gn[:, cg:cg + 1], gn[:, cb_:cb_ + 1], ALU.mult, ALU.add,
                       [b_yc, b_const], [b_yc])
                    act(ycv[:, cc, :n], ycv[:, cc, :n], AF.Silu, [b_yc], [b_yc])
                stats_pre([(ycv[:, cc, :n], b_yc, False) for cc in range(2)], n, s0=6)

            def conv_stage3(l=l, n=n, ycv=ycv):
                stats_post(2, n, 256.0, 2, s0=6)
                for cc in range(2):
                    col = cfg.GB + 8 * l + 6 + cc
                    stt("dve", onv[:, 6 + cc, :n], ycv[:, cc, :n], gn[:, col:col + 1], rstt[2][:, :n], ALU.mult, ALU.mult,
                        [b_yc, b_rst[2], b_const], [b_on[6 + cc]])
            conv_stage2()
            mk0 = nreal
            units = []
            if c["kind"] == "real":
                hloc = c["hloc"]
                for blk in range(4):
                    bi = c["coff"] // 128 + blk
                    gb = g0 + blk * 128
                    qcols = slice(blk * 128, blk * 128 + 128)
                    for gi in range(2):
                        kts = []
                        if gid == 0:
                            mb = mtab[0:32, hloc * 512:(hloc + 1) * 512]
                        else:
                            mb = None
                        kts.append((mk0, nmeta, vtv[0:nmeta, ntr, :], mb, False))
                        if bi > 0:
                            kts.append((gb - 128, 128, vtv[:, gb // 128 - 1, :], btv[:, gi * 3 + 0, :], False))
                        elif hloc == 1:
                            kts.append((gb - 128, 128, vtv[:, gb // 128 - 1, :], btv[:, gi * 3 + 0, :], True))
                        kts.append((gb, 128, vtv[:, gb // 128, :], btv[:, gi * 3 + 1, :], False))
                        if bi < nblk_h - 1:
                            kts.append((gb + 128, 128, vtv[:, gb // 128 + 1, :], btv[:, gi * 3 + 2, :], False))
                        elif hloc == 0 and nh == 2:
                            kts.append((gb + 128, 128, vtv[:, gb // 128 + 1, :], btv[:, gi * 3 + 2, :], True))
                        units.append(dict(gi=gi, qcols=qcols, nq=128, kts=kts, oi=blk % 2, pts=[],
                                          sap=esrow[0:1, gi * 512:(gi + 1) * 512],
                                          fin=(128, qcols, blk % 2) if gi == 1 else None))
            else:
                for hl in range(nh):
                    qcols = slice(hl * 16, hl * 16 + 16)
                    for gi in range(2):
                        kts = []
                        mb = mmb[0:32, hl * 64:(hl + 1) * 64] if gid == 0 else None
                        kts.append((mk0, nmeta, vtv[0:nmeta, ntr, :], mb, False))
                        kts.append((hl * HS, 128, vtv[:, hl * HS // 128, :], mqb[:, gi * 64:(gi + 1) * 64], False))
                        sap = esrow[0:1, gi * 512:(gi + 1) * 512].rearrange("p (g q) -> p g q", g=4)[:, :, 0:16]
                        units.append(dict(gi=gi, qcols=qcols, nq=16, kts=kts, oi=hl % 2, pts=[], sap=sap,
                                          fin=(16, qcols, hl % 2) if gi == 1 else None))
            attention_run(l, units, {1: conv_stage3})
            for dc in range(8):
                two, bwo = wtile(l, wb + 12 + dc)
                pw, bpw = PS()
                for fc in range(8):
                    mm(pw[:, :n], two[:, fc, :], onv[:, fc, :n], fc == 0, fc == 7, [bwo, b_on[fc]], [bpw])
                tt("dve", hbv[hi][:, dc, :n], pw[:, :n], hbv[hi][:, dc, :n], ALU.add, [bpw, b_hb[hi][dc]], [b_hb[hi][dc]])
            ffn(l, 1, hi, n)
            if not last:
                store_h(c, hi)
            elif c["kind"] == "real":
                stats([(hbv[hi][:, kc, :n], b_hb[hi][kc], False) for kc in range(8)], n, float(D), 0)
                for kc in range(8):
                    stt("dve", hbv[hi][:, kc, :n], hbv[hi][:, kc, :n], gn[:, cfg.GF + kc:cfg.GF + kc + 1], rstt[0][:, :n],
                        ALU.mult, ALU.mult, [b_hb[hi][kc], b_rst[0], b_const], [b_hb[hi][kc]])
                for t_ in range(n // 128):
                    oi = t_ % 2
                    for hf in range(2):
                        pt_, bp = PS()
                        for j in range(4):
                            kc = hf * 4 + j
                            P.add("pe", (lambda o, i_, idn: (lambda e: e.transpose(o, i_, idn)))(
                                pt_[:, j * 128:(j + 1) * 128], hbv[hi][:, kc, t_ * 128:(t_ + 1) * 128], identf[:, :]),
                                [b_hb[hi][kc], b_const], [bp])
                        cp("act" if hf == 0 else "dve", ost[:, oi * 1024 + hf * 512:oi * 1024 + hf * 512 + 512], pt_[:, :],
                           [bp], [b_ost[oi]] + (b_sbt if oi == 0 else b_otmps))
                    r0 = c["s0"] + t_ * 128
                    dma("pool", "out%d" % oi, yout[r0:r0 + 128, :], ost[:, oi * 1024:(oi + 1) * 1024], [b_ost[oi]] + (b_sbt if oi == 0 else b_otmps), [])

    for l in range(L):
        layer_consts(l)
        for g in groups:
            pass_a(l, g)
            pass_b(l, g, l == L - 1)
        cast_some(l + 1, 1000)

    P.emit(nc, es)
    es.close()
    return nc


_CACHE = {}


def run(cfg, inputs):
    in_maps = host_prep(cfg, inputs)
    key = (cfg.L, cfg.HS, cfg.DFF)
    if key not in _CACHE:
        _CACHE[key] = build_program(cfg)
    nc = _CACHE[key]
    res = run_bass_kernel_spmd(nc, in_maps, core_ids=list(range(NCORES)))
    HS = cfg.HS
    nb_p = 16
    yp = np.empty((nb_p, HS, D), np.float32)
    ys = np.empty((4, 2 * HS, D), np.float32)
    for c in range(NCORES):
        y = np.asarray(res.results[c]["yout"], np.float32)
        if c < 4:
            for i in range(3):
                yp[3 * c + i] = y[i * HS:(i + 1) * HS]
        else:
            s_ = c - 4
            ys[s_] = y[0:2 * HS]
            yp[12 + s_] = y[2 * HS:3 * HS]
    return yp, ys


def kernel(**inputs):
    cfg = Cfg(L=4, HS=2048, DFF=2816)
    return run(cfg, inputs)
```
